# Optimizing a Trainium2 kernel written in Bass

```python
import math
import jax, jax.numpy as jnp
from jax import lax
import numpy as np

D_MODEL = 1024
BATCH = 4
SEQ = 4096
DEPTH = 2
DEC_BATCH = 16
DEC_SEQ = 64
PAST_LEN = 2048

CHUNK = 64
N_EVEN = (DEPTH + 1) // 2
N_ODD = DEPTH // 2
FF_DIM = 2816
EPS = 1e-6
CONV_W = 4
RET_HEADS = 4
RET_DK = 128
RET_DV = 128
ROPE_BASE = 10000.0
HG_HEADS = 4
HG_DK = 128
HG_DV = 128
LRU_WIDTH = 512
LRU_BLOCKS = 8
LRU_BS = LRU_WIDTH // LRU_BLOCKS
LRU_C = 8.0
DN_HEADS = 4
DN_DK = 128
DN_DV = 128
DN_CONV_CH = 2 * DN_HEADS * DN_DK + DN_HEADS * DN_DV

EVEN_SPLITS = [RET_HEADS * RET_DK, RET_HEADS * RET_DK, RET_HEADS * RET_DV, RET_HEADS * RET_DV,
               HG_HEADS * HG_DK, HG_HEADS * HG_DK, HG_HEADS * HG_DV, HG_HEADS * HG_DV]
EVEN_IN = sum(EVEN_SPLITS)
EVEN_MIX = RET_HEADS * RET_DV + HG_HEADS * HG_DV
ODD_SPLITS = [LRU_WIDTH, LRU_WIDTH, DN_HEADS * DN_DK, DN_HEADS * DN_DK, DN_HEADS * DN_DV,
              DN_HEADS * DN_DV, DN_HEADS, DN_HEADS]
ODD_IN = sum(ODD_SPLITS)
ODD_MIX = LRU_WIDTH + DN_HEADS * DN_DV

kernel_name = 'chunk_causal_hybrid_retention_hgrn2_rglru_gdn_step'


def split_cols(z, sizes):
    idx = np.cumsum(sizes)[:-1].tolist()
    return jnp.split(z, idx, axis=-1)


def rmsnorm(x, g):
    xf = x.astype(jnp.float32)
    y = xf * lax.rsqrt(jnp.mean(xf * xf, axis=-1, keepdims=True) + EPS)
    return (y * g).astype(x.dtype)


def head_rmsnorm(o, g):
    B, T, H, d = o.shape
    o = o * lax.rsqrt(jnp.mean(o * o, axis=-1, keepdims=True) + EPS)
    return o.reshape(B, T, H * d) * g.astype(o.dtype)


def l2norm(x):
    return x * lax.rsqrt(jnp.sum(x * x, axis=-1, keepdims=True) + EPS)


def swiglu(x, w_in, w_out):
    gate, up = jnp.split(x @ w_in, 2, axis=-1)
    return (jax.nn.silu(gate) * up) @ w_out


def rope(x, pos):
    half = x.shape[-1] // 2
    freq = ROPE_BASE ** (-jnp.arange(half, dtype=jnp.float32) / half)
    ang = pos.astype(jnp.float32)[:, None] * freq[None, :]
    cos = jnp.cos(ang)[None, :, None, :]
    sin = jnp.sin(ang)[None, :, None, :]
    x1, x2 = x[..., :half], x[..., half:]
    return jnp.concatenate([x1 * cos - x2 * sin, x1 * sin + x2 * cos], axis=-1)


def to_chunks(x, c):
    B, T, H = x.shape[:3]
    rest = x.shape[3:]
    x = x.reshape((B, T // c, c, H) + rest)
    perm = (1, 0, 3, 2) + tuple(range(4, x.ndim))
    return x.transpose(perm)


def from_chunks(o):
    N, B, H, C, d = o.shape
    return o.transpose(1, 0, 3, 2, 4).reshape(B, N * C, H, d)


def causal_conv(x, buf, w):
    T = x.shape[1]
    xp = jnp.concatenate([buf.astype(x.dtype), x], axis=1)
    y = sum(xp[:, j:j + T] * w[j] for j in range(CONV_W))
    return y, xp[:, -(CONV_W - 1):]


def retention(q, k, v, s0, c):
    ar = jnp.arange(RET_HEADS, dtype=jnp.float32)
    log_g = jnp.log1p(-jnp.exp2(-5.0 - ar))
    p = jnp.arange(c, dtype=jnp.float32)
    intra = jnp.exp(log_g[:, None, None] * jnp.abs(p[:, None] - p[None, :]))
    q_dec = jnp.exp(log_g[:, None] * (p + 1.0))[..., None]
    k_dec = jnp.exp(log_g[:, None] * (c - 1.0 - p))[..., None]
    s_dec = jnp.exp(log_g * c)[:, None, None]

    def step(s, inp):
        qc, kc, vc = inp
        att = jnp.einsum('bhtd,bhsd->bhts', qc, kc) * intra
        o = (jnp.einsum('bhts,bhse->bhte', att, vc)
             + jnp.einsum('bhtd,bhde->bhte', qc * q_dec, s))
        s = s * s_dec + jnp.einsum('bhsd,bhse->bhde', kc * k_dec, vc)
        return s, o

    s, o = lax.scan(step, s0, (to_chunks(q, c), to_chunks(k, c), to_chunks(v, c)))
    return from_chunks(o), s


def hgrn2_chunked(q, k, v, logf, s0, c):
    incl = jnp.tril(jnp.ones((c, c), bool))[:, :, None]

    def step(s, inp):
        qc, kc, vc, gc = inp
        b = jnp.cumsum(gc, axis=2)
        rel = b[:, :, :, None, :] - b[:, :, None, :, :]
        dec = jnp.exp(jnp.where(incl, rel, -jnp.inf))
        att = jnp.einsum('bhtd,bhsd,bhtsd->bhts', qc, kc, dec)
        o = (jnp.einsum('bhts,bhse->bhte', att, vc)
             + jnp.einsum('bhtd,bhde->bhte', qc * jnp.exp(b), s))
        b_end = b[:, :, -1:, :]
        s = (s * jnp.exp(b_end)[:, :, 0, :, None]
             + jnp.einsum('bhsd,bhse->bhde', kc * jnp.exp(b_end - b), vc))
        return s, o

    xs = (to_chunks(q, c), to_chunks(k, c), to_chunks(v, c), to_chunks(logf, c))
    s, o = lax.scan(step, s0, xs)
    return from_chunks(o), s


def gated_delta_chunked(q, k, v, beta, loga, s0, c):
    strict = jnp.tril(jnp.ones((c, c), bool), -1)
    incl = jnp.tril(jnp.ones((c, c), bool))
    eye = jnp.eye(c, dtype=jnp.float32)

    def step(s, inp):
        qc, kc, vc, bc, gc = inp
        g = jnp.cumsum(gc, axis=-1)
        rel = g[..., :, None] - g[..., None, :]
        a_mat = (bc[..., None] * jnp.einsum('bhtd,bhsd->bhts', kc, kc)
                 * jnp.exp(jnp.where(strict, rel, -jnp.inf)))
        eg = jnp.exp(g)[..., None]
        rhs = bc[..., None] * (vc - eg * jnp.einsum('bhtd,bhde->bhte', kc, s))
        w = lax.linalg.triangular_solve(eye + a_mat, rhs, left_side=True, lower=True,
                                        unit_diagonal=True)
        qk = jnp.einsum('bhtd,bhsd->bhts', qc, kc) * jnp.exp(jnp.where(incl, rel, -jnp.inf))
        o = eg * jnp.einsum('bhtd,bhde->bhte', qc, s) + jnp.einsum('bhts,bhse->bhte', qk, w)
        g_end = g[..., -1:]
        s = (s * jnp.exp(g_end)[..., None]
             + jnp.einsum('bhsd,bhse->bhde', kc * jnp.exp(g_end - g)[..., None], w))
        return s, o

    xs = (to_chunks(q, c), to_chunks(k, c), to_chunks(v, c), to_chunks(beta, c), to_chunks(loga, c))
    s, o = lax.scan(step, s0, xs)
    return from_chunks(o), s


def rg_lru(x, h0, w_a, b_a, w_x, b_x, lam):
    B, T, W = x.shape
    xb = x.reshape(B, T, LRU_BLOCKS, LRU_BS)
    r = jax.nn.sigmoid(jnp.einsum('btnd,nde->btne', xb, w_a).reshape(B, T, W) + b_a)
    i = jax.nn.sigmoid(jnp.einsum('btnd,nde->btne', xb, w_x).reshape(B, T, W) + b_x)
    log_a = -LRU_C * r * jax.nn.softplus(-lam)
    a = jnp.exp(log_a)
    u = jnp.sqrt(-jnp.expm1(2.0 * log_a)) * (i * x)

    def combine(l, rr):
        return (l[0] * rr[0], rr[0] * l[1] + rr[1])

    a_cum, h_part = lax.associative_scan(combine, (a, u), axis=1)
    h = a_cum * h0[:, None, :] + h_part
    return h, h[:, -1]


def even_mixer(h, pos, c, ret_s, hg_s, lb, p, j):
    B, T, _ = h.shape
    f32 = jnp.float32
    z = (h @ p['even_w_in'][j]).astype(f32)
    rq, rk, rv, rg, hq, hf, hi, hog = split_cols(z, EVEN_SPLITS)

    def heads(t, n):
        return t.reshape(B, T, n, -1)

    rq = rope(heads(rq, RET_HEADS), pos) * (RET_DK ** -0.5)
    rk = rope(heads(rk, RET_HEADS), pos)
    ret_o, ret_new = retention(rq, rk, heads(rv, RET_HEADS), ret_s.astype(f32), c)
    ret_o = head_rmsnorm(ret_o, p['ret_out_norm'][j]) * jax.nn.silu(rg)
    logf = jnp.logaddexp(jnp.log(lb), jnp.log1p(-lb) + jax.nn.log_sigmoid(hf))
    hk = (1.0 - lb) * jax.nn.sigmoid(-hf)
    hg_o, hg_new = hgrn2_chunked(heads(jax.nn.silu(hq), HG_HEADS), heads(hk, HG_HEADS),
                                 heads(hi, HG_HEADS), heads(logf, HG_HEADS), hg_s.astype(f32), c)
    hg_o = head_rmsnorm(hg_o, p['hg_out_norm'][j]) * jax.nn.silu(hog)
    out = jnp.concatenate([ret_o, hg_o], axis=-1).astype(h.dtype) @ p['even_w_out'][j]
    return out, ret_new, hg_new


def odd_mixer(h, c, lru_h, lru_conv, dn_s, dn_conv, p, j):
    B, T, _ = h.shape
    f32 = jnp.float32
    z = (h @ p['odd_w_in'][j]).astype(f32)
    lx, lg, dq, dk, dv, dog, db, da = split_cols(z, ODD_SPLITS)
    lx, lru_conv_new = causal_conv(lx, lru_conv.astype(f32), p['lru_conv_w'][j])
    lx = lx + p['lru_conv_b'][j]
    hseq, lru_h_new = rg_lru(lx, lru_h.astype(f32), p['lru_w_a'][j], p['lru_b_a'][j],
                             p['lru_w_x'][j], p['lru_b_x'][j], p['lru_lambda'][j])
    lru_o = jax.nn.gelu(lg) * hseq
    qkv, dn_conv_new = causal_conv(jnp.concatenate([dq, dk, dv], axis=-1), dn_conv.astype(f32),
                                   p['dn_conv_w'][j])
    qkv = jax.nn.silu(qkv)
    q, k, v = split_cols(qkv, [DN_HEADS * DN_DK, DN_HEADS * DN_DK, DN_HEADS * DN_DV])
    q = l2norm(q.reshape(B, T, DN_HEADS, DN_DK)) * (DN_DK ** -0.5)
    k = l2norm(k.reshape(B, T, DN_HEADS, DN_DK))
    v = v.reshape(B, T, DN_HEADS, DN_DV)
    beta = jax.nn.sigmoid(db)
    loga = -jnp.exp(p['dn_a_log'][j]) * jax.nn.softplus(da + p['dn_dt_bias'][j])
    dn_o, dn_new = gated_delta_chunked(q, k, v, beta, loga, dn_s.astype(f32), c)
    dn_o = head_rmsnorm(dn_o, p['dn_out_norm'][j]) * jax.nn.silu(dog)
    out = jnp.concatenate([lru_o, dn_o], axis=-1).astype(h.dtype) @ p['odd_w_out'][j]
    return out, lru_h_new, lru_conv_new, dn_new, dn_conv_new


def trunk(x, pos, ret_s, hg_s, lru_h, lru_conv, dn_s, dn_conv, p):
    c = min(CHUNK, x.shape[1])
    lb_all = jnp.cumsum(jax.nn.softmax(p['hg_lb_logits'].astype(jnp.float32), axis=0), axis=0)
    n_ret, n_hg, n_lh, n_lc, n_dn, n_dc = [], [], [], [], [], []
    for l in range(DEPTH):
        j = l // 2
        x = x + 0.5 * swiglu(rmsnorm(x, p['ffn1_norm'][l]), p['ffn1_w_in'][l], p['ffn1_w_out'][l])
        hn = rmsnorm(x, p['mix_norm'][l])
        if l % 2 == 0:
            mix, rs, hs = even_mixer(hn, pos, c, ret_s[j], hg_s[j], lb_all[j], p, j)
            n_ret.append(rs)
            n_hg.append(hs)
        else:
            mix, lh, lc, ds, dc = odd_mixer(hn, c, lru_h[j], lru_conv[j], dn_s[j], dn_conv[j], p, j)
            n_lh.append(lh)
            n_lc.append(lc)
            n_dn.append(ds)
            n_dc.append(dc)
        x = x + mix
        x = x + 0.5 * swiglu(rmsnorm(x, p['ffn2_norm'][l]), p['ffn2_w_in'][l], p['ffn2_w_out'][l])
    y = rmsnorm(x, p['final_norm'])
    return (y, jnp.stack(n_ret), jnp.stack(n_hg), jnp.stack(n_lh), jnp.stack(n_lc),
            jnp.stack(n_dn), jnp.stack(n_dc))


def setup_inputs(seed: int = 0) -> dict:
    key = jax.random.key(seed)
    keys = iter(jax.random.split(key, 40))
    f32 = jnp.float32

    def nrm(shape, scale):
        return jax.random.normal(next(keys), shape, f32) * scale

    def unif(shape, lo, hi):
        return jax.random.uniform(next(keys), shape, f32, lo, hi)

    dsc = D_MODEL ** -0.5
    a8 = unif((N_ODD, LRU_WIDTH), 0.9, 0.999)
    a = a8 ** (1.0 / LRU_C)
    dt = jnp.exp(unif((N_ODD, DN_HEADS), math.log(1e-3), math.log(1e-1)))
    a_heads = unif((N_ODD, DN_HEADS), 1.0, 16.0)
    return {
        'x_prompt': nrm((BATCH, SEQ, D_MODEL), 1.0),
        'x_sample': nrm((DEC_BATCH, DEC_SEQ, D_MODEL), 1.0),
        'state_ret': nrm((N_EVEN, DEC_BATCH, RET_HEADS, RET_DK, RET_DV), 2.0),
        'state_hgrn': nrm((N_EVEN, DEC_BATCH, HG_HEADS, HG_DK, HG_DV), 1.0),
        'state_lru_h': nrm((N_ODD, DEC_BATCH, LRU_WIDTH), 0.5),
        'state_lru_conv': nrm((N_ODD, DEC_BATCH, CONV_W - 1, LRU_WIDTH), 1.0),
        'state_dn': nrm((N_ODD, DEC_BATCH, DN_HEADS, DN_DK, DN_DV), 0.3),
        'state_dn_conv': nrm((N_ODD, DEC_BATCH, CONV_W - 1, DN_CONV_CH), 1.0),
        'ffn1_norm': 1.0 + nrm((DEPTH, D_MODEL), 0.02),
        'ffn1_w_in': nrm((DEPTH, D_MODEL, 2 * FF_DIM), dsc),
        'ffn1_w_out': nrm((DEPTH, FF_DIM, D_MODEL), FF_DIM ** -0.5),
        'mix_norm': 1.0 + nrm((DEPTH, D_MODEL), 0.02),
        'ffn2_norm': 1.0 + nrm((DEPTH, D_MODEL), 0.02),
        'ffn2_w_in': nrm((DEPTH, D_MODEL, 2 * FF_DIM), dsc),
        'ffn2_w_out': nrm((DEPTH, FF_DIM, D_MODEL), FF_DIM ** -0.5),
        'final_norm': 1.0 + nrm((D_MODEL,), 0.02),
        'even_w_in': nrm((N_EVEN, D_MODEL, EVEN_IN), dsc),
        'even_w_out': nrm((N_EVEN, EVEN_MIX, D_MODEL), EVEN_MIX ** -0.5),
        'ret_out_norm': 1.0 + nrm((N_EVEN, RET_HEADS * RET_DV), 0.02),
        'hg_out_norm': 1.0 + nrm((N_EVEN, HG_HEADS * HG_DV), 0.02),
        'hg_lb_logits': nrm((N_EVEN + 1, HG_HEADS * HG_DK), 0.5),
        'odd_w_in': nrm((N_ODD, D_MODEL, ODD_IN), dsc),
        'odd_w_out': nrm((N_ODD, ODD_MIX, D_MODEL), ODD_MIX ** -0.5),
        'lru_conv_w': nrm((N_ODD, CONV_W, LRU_WIDTH), CONV_W ** -0.5),
        'lru_conv_b': nrm((N_ODD, LRU_WIDTH), 0.02),
        'lru_w_a': nrm((N_ODD, LRU_BLOCKS, LRU_BS, LRU_BS), LRU_BS ** -0.5),
        'lru_b_a': nrm((N_ODD, LRU_WIDTH), 0.02),
        'lru_w_x': nrm((N_ODD, LRU_BLOCKS, LRU_BS, LRU_BS), LRU_BS ** -0.5),
        'lru_b_x': nrm((N_ODD, LRU_WIDTH), 0.02),
        'lru_lambda': jnp.log(a) - jnp.log1p(-a),
        'dn_conv_w': nrm((N_ODD, CONV_W, DN_CONV_CH), CONV_W ** -0.5),
        'dn_a_log': jnp.log(a_heads),
        'dn_dt_bias': dt + jnp.log(-jnp.expm1(-dt)),
        'dn_out_norm': 1.0 + nrm((N_ODD, DN_HEADS * DN_DV), 0.02),
    }


def reference(x_prompt, x_sample, state_ret, state_hgrn, state_lru_h, state_lru_conv, state_dn,
              state_dn_conv, ffn1_norm, ffn1_w_in, ffn1_w_out, mix_norm, ffn2_norm, ffn2_w_in,
              ffn2_w_out, final_norm, even_w_in, even_w_out, ret_out_norm, hg_out_norm,
              hg_lb_logits, odd_w_in, odd_w_out, lru_conv_w, lru_conv_b, lru_w_a, lru_b_a, lru_w_x,
              lru_b_x, lru_lambda, dn_conv_w, dn_a_log, dn_dt_bias, dn_out_norm):
    p = {
        'ffn1_norm': ffn1_norm, 'ffn1_w_in': ffn1_w_in, 'ffn1_w_out': ffn1_w_out,
        'mix_norm': mix_norm, 'ffn2_norm': ffn2_norm, 'ffn2_w_in': ffn2_w_in,
        'ffn2_w_out': ffn2_w_out, 'final_norm': final_norm, 'even_w_in': even_w_in,
        'even_w_out': even_w_out, 'ret_out_norm': ret_out_norm, 'hg_out_norm': hg_out_norm,
        'hg_lb_logits': hg_lb_logits, 'odd_w_in': odd_w_in, 'odd_w_out': odd_w_out,
        'lru_conv_w': lru_conv_w, 'lru_conv_b': lru_conv_b, 'lru_w_a': lru_w_a,
        'lru_b_a': lru_b_a, 'lru_w_x': lru_w_x, 'lru_b_x': lru_b_x, 'lru_lambda': lru_lambda,
        'dn_conv_w': dn_conv_w, 'dn_a_log': dn_a_log, 'dn_dt_bias': dn_dt_bias,
        'dn_out_norm': dn_out_norm,
    }
    f32 = jnp.float32
    bp = x_prompt.shape[0]
    z_ret = jnp.zeros((N_EVEN, bp, RET_HEADS, RET_DK, RET_DV), f32)
    z_hg = jnp.zeros((N_EVEN, bp, HG_HEADS, HG_DK, HG_DV), f32)
    z_lh = jnp.zeros((N_ODD, bp, LRU_WIDTH), f32)
    z_lc = jnp.zeros((N_ODD, bp, CONV_W - 1, LRU_WIDTH), f32)
    z_dn = jnp.zeros((N_ODD, bp, DN_HEADS, DN_DK, DN_DV), f32)
    z_dc = jnp.zeros((N_ODD, bp, CONV_W - 1, DN_CONV_CH), f32)
    pos_p = jnp.arange(x_prompt.shape[1])
    y_prompt, ret_p, hg_p, lh_p, lc_p, dn_p, dc_p = trunk(
        x_prompt, pos_p, z_ret, z_hg, z_lh, z_lc, z_dn, z_dc, p)
    pos_s = PAST_LEN + jnp.arange(x_sample.shape[1])
    y_sample, ret_s, hg_s, lh_s, lc_s, dn_s, dc_s = trunk(
        x_sample, pos_s, state_ret, state_hgrn, state_lru_h, state_lru_conv, state_dn,
        state_dn_conv, p)
    return (y_prompt, y_sample, ret_p, ret_s, hg_p, hg_s, lh_p, lh_s, lc_p, lc_s, dn_p, dn_s,
            dc_p, dc_s)
```

```python
import contextlib
import math
import os
DBG = int(os.environ.get('KDBG', '99'))
import numpy as np
import concourse.bass as bass
import concourse.mybir as mybir
from concourse.bass_utils import run_bass_kernel_spmd

F32 = mybir.dt.float32
BF16 = mybir.dt.bfloat16
F32R = mybir.dt.float32r
AF = mybir.ActivationFunctionType
ALU = mybir.AluOpType

D = 1024
KT = 8
FF = 2816
FT = 22
EPS = 1e-6
LRU_C = 8.0
NCORES = 8
PROMPT_OF_CORE = [0, 1, None, None, 2, 3, None, None]
CORE_OF_PROMPT = [0, 1, 4, 5]


class Op:
    __slots__ = ("eng", "fn", "waits", "signal", "idx", "dma_sem", "sig_count")

    def __init__(self, eng, fn):
        self.eng = eng
        self.fn = fn
        self.waits = []
        self.signal = False
        self.dma_sem = None
        self.sig_count = 0
        self.idx = 0


class Sched:
    def __init__(self):
        self.ops = {e: [] for e in ("pe", "act", "dve", "pool", "sp")}
        self.last_write = {}
        self.readers = {}
        self.waited = {e: {} for e in self.ops}
        self.dma_counts = {}
        self.dma_last = {}
        self.misc_ctr = {}

    def _need(self, op, dep, is_dma=False):
        if dep is None:
            return
        kind, key, val = dep
        if kind == "eng" and key == op.eng and key == "pe" and not is_dma:
            return
        w = self.waited[op.eng]
        k = (kind, key)
        if w.get(k, -1) >= val:
            return
        w[k] = val
        op.waits.append(dep)

    def _deps(self, op, reads, writes, is_dma=False):
        for r in reads:
            self._need(op, self.last_write.get(r), is_dma)
        for r in writes:
            self._need(op, self.last_write.get(r), is_dma)
            for d in self.readers.get(r, ()):
                self._need(op, d, is_dma)

    def _commit(self, dep, reads, writes):
        for r in reads:
            self.readers.setdefault(r, []).append(dep)
        for r in writes:
            self.last_write[r] = dep
            self.readers[r] = []

    def op(self, eng, fn, reads=(), writes=()):
        psr = [r for r in reads if isinstance(r, tuple) and r[0] == "ps"]
        if psr:
            writes = list(writes) + [r for r in psr if r not in writes]
        o = Op(eng, fn)
        o.idx = len(self.ops[eng])
        self._deps(o, reads, writes)
        self.ops[eng].append(o)
        self._commit(("eng", eng, o.idx), reads, writes)
        return o

    NMISC = 24

    def dma(self, eng, fn, sem=None, reads=(), writes=()):
        o = Op(eng, fn)
        o.idx = len(self.ops[eng])
        if sem is None:
            mc = self.misc_ctr.get(eng, 0)
            sem = "m%s%d" % (eng, mc % self.NMISC)
            self.misc_ctr[eng] = mc + 1
        if not sem.startswith("cast") and not sem.startswith("w"):
            self._need(o, self.dma_last.get(sem), True)
        self._deps(o, reads, writes, True)
        c = self.dma_counts.get(sem, 0) + 1
        self.dma_counts[sem] = c
        o.dma_sem = sem
        self.ops[eng].append(o)
        dep = ("dma", sem, 16 * c)
        self.dma_last[sem] = dep
        self._commit(dep, reads, writes)
        return o

    def barrier(self):
        lasts = []
        for e in ("pe", "act", "dve", "pool"):
            if self.ops[e]:
                for o in reversed(self.ops[e]):
                    if o.fn is not None and o.dma_sem is None:
                        lasts.append(("eng", e, o.idx))
                        break
        dmas = [v for k, v in self.dma_last.items() if not (k.startswith('w') or k.startswith('cast'))]
        for e in ("pe", "act", "dve", "pool"):
            o = Op(e, None)
            o.idx = len(self.ops[e])
            for d in lasts + dmas:
                self._need(o, d)
            self.ops[e].append(o)

    def wait_deps(self, eng, deps):
        o = Op(eng, None)
        o.idx = len(self.ops[eng])
        for d in deps:
            self._need(o, d, True)
        self.ops[eng].append(o)

    def finalize(self):
        for e, lst in self.ops.items():
            for o in lst:
                for kind, key, val in o.waits:
                    if kind == "eng":
                        self.ops[key][val].signal = True
        for e, lst in self.ops.items():
            c = 0
            for o in lst:
                if o.signal:
                    c += 1
                o.sig_count = c

    def emit(self, regs, eng_sems, dma_sems):
        self.finalize()
        for e, reg in regs.items():
            lst = self.ops[e]
            if not lst:
                continue

            def body(engine, lst=lst, e=e):
                for o in lst:
                    for kind, key, val in o.waits:
                        if kind == "eng":
                            engine.wait_ge(eng_sems[key], self.ops[key][val].sig_count)
                        else:
                            engine.wait_ge(dma_sems[key], val)
                    if o.fn is None:
                        continue
                    ins = o.fn(engine)
                    if o.dma_sem is not None:
                        ins.then_inc(dma_sems[o.dma_sem], 16)
                    elif o.signal:
                        ins.then_inc(eng_sems[e], 1)

            reg(body)


def host_consts(TP, past_len):
    c = {}
    g = np.array([np.log1p(-2.0 ** (-5.0 - h)) for h in range(4)], np.float64)
    p = np.arange(64, dtype=np.float64)
    c128 = {}
    c128["ident"] = np.eye(128)
    rm = np.zeros((128, 128))
    for d in range(64):
        rm[d + 64, d] = -1.0
        rm[d, d + 64] = 1.0
    c128["rmat"] = rm
    qd = np.stack([(128.0 ** -0.5) * np.exp(g[h] * (p + 1.0)) for h in range(4)], 0)
    c128["qdec"] = np.broadcast_to(qd.reshape(1, 256), (128, 256))
    rs = np.ones(512)
    rs[0::64] = 0.0
    c128["reset"] = np.broadcast_to(rs.reshape(1, 512), (128, 512))
    c128["ones"] = np.ones((128, 128))
    names128 = ["ident", "rmat", "qdec", "reset", "ones"]
    c["c128"] = np.concatenate([np.asarray(c128[k], np.float64) for k in names128], 1).astype(np.float32)
    off = 0
    c["off128"] = {}
    for k in names128:
        c["off128"][k] = (off, c128[k].shape[1])
        off += c128[k].shape[1]
    c64 = {}
    s = p.reshape(64, 1)
    t = p.reshape(1, 64)
    c64["retmask"] = np.concatenate([np.exp(g[h] * (np.abs(t - s) - (t + 1.0))) for h in range(4)], 1)
    c64["kdec"] = np.concatenate([np.broadcast_to(np.exp(g[h] * (63.0 - s)), (64, 128)) for h in range(4)], 1)
    incl = (s <= t).astype(np.float64)
    strict = (s < t).astype(np.float64)
    c64["incl4"] = np.tile(incl, (1, 4))
    c64["neg4"] = np.tile((1.0 - incl) * -30000.0, (1, 4))
    c64["strict4"] = np.tile(strict, (1, 4))
    c64["i4"] = np.tile(np.eye(64), (1, 4))
    c64["U"] = incl
    c64["Urev"] = (s > t).astype(np.float64)
    c64["ones"] = np.ones((64, 128))
    names64 = ["retmask", "kdec", "incl4", "neg4", "strict4", "i4", "U", "Urev", "ones"]
    c["c64"] = np.concatenate([c64[k] for k in names64], 1).astype(np.float32)
    off = 0
    c["off64"] = {}
    for k in names64:
        c["off64"][k] = (off, c64[k].shape[1])
        off += c64[k].shape[1]
    c["sdec"] = [float(np.exp(g[h] * 64.0)) for h in range(4)]
    pos = np.concatenate([np.arange(TP), past_len + np.arange(64), past_len + np.arange(64)]).astype(np.float64)
    half = 64
    freq = 10000.0 ** (-np.arange(half, dtype=np.float64) / half)
    ang = (pos.astype(np.float32)[None, :] * freq.astype(np.float32)[:, None]).astype(np.float32).astype(np.float64)
    cos = np.cos(ang)
    sin = np.sin(ang)
    tab = np.zeros((128, 2, TP + 128), np.float32)
    tab[0:64, 0] = cos
    tab[64:128, 0] = cos
    tab[0:64, 1] = sin
    tab[64:128, 1] = sin
    c["rope"] = tab
    return c


def build_program(TP, stage=99):
    assert TP % 512 == 0
    hc = host_consts(TP, 0)
    off128, off64, SDEC = hc["off128"], hc["off64"], hc["sdec"]
    C128W = hc["c128"].shape[1]
    C64W = hc["c64"].shape[1]

    nc = bass.Bass("TRN2", target_bir_lowering=False)

    def din(name, shape, dt=F32):
        return nc.dram_tensor(name, list(shape), dt, kind="ExternalInput").ap()

    def dout(name, shape):
        return nc.dram_tensor(name, list(shape), F32, kind="ExternalOutput").ap()

    def dscr(name, shape, dt):
        return nc.dram_tensor(name, list(shape), dt, kind="Internal").ap()

    xp = din("xp", [TP, D])
    xs = din("xs", [128, D])
    st_ret = din("st_ret", [2, 4, 128, 128])
    st_hg = din("st_hg", [2, 4, 128, 128])
    st_dn = din("st_dn", [2, 4, 128, 128])
    st_lh = din("st_lh", [2, 512])
    st_lc = din("st_lc", [2, 3, 512])
    st_dc = din("st_dc", [2, 3, 1536])
    w_ffn_in = [din("ffn1_w_in", [2, D, 2 * FF]), din("ffn2_w_in", [2, D, 2 * FF])]
    w_ffn_out = [din("ffn1_w_out", [2, FF, D]), din("ffn2_w_out", [2, FF, D])]
    w_even_in = din("even_w_in", [D, 4096])
    w_even_out = din("even_w_out", [D, D])
    w_odd_in = din("odd_w_in", [D, 3080])
    w_odd_out = din("odd_w_out", [D, D])
    norms_d = din("norms", [7, D])
    hnorm_d = din("hnorm", [3, 512])
    lb_logits = din("hg_lb_logits", [2, 512])
    lru_vecs = din("lru_vecs", [8, 512])
    lru_wa = din("lru_w_a", [8, 64, 64])
    lru_wx = din("lru_w_x", [8, 64, 64])
    dn_conv_w = din("dn_conv_w", [4, 1536])
    dn_scal = din("dn_scal", [2, 4])
    c128_d = din("c128", [128, C128W])
    c64_d = din("c64", [64, C64W])
    rope_d = din("rope", [128, 2, TP + 128])

    yp = dout("yp", [TP, D])
    ys = dout("ys", [128, D])
    o_ret = [dout("ret_p", [4, 128, 128]), dout("ret_s", [2, 4, 128, 128])]
    o_hg = [dout("hg_p", [4, 128, 128]), dout("hg_s", [2, 4, 128, 128])]
    o_dn = [dout("dn_p", [4, 128, 128]), dout("dn_s", [2, 4, 128, 128])]
    o_lh = [dout("lh_p", [1, 512]), dout("lh_s", [2, 512])]
    o_lc = [dout("lc_p", [1, 3, 512]), dout("lc_s", [2, 3, 512])]
    o_dc = [dout("dc_p", [1, 3, 1536]), dout("dc_s", [2, 3, 1536])]

    wb_ffn_in = [dscr("b_ffn1_w_in", [2, D, 2 * FF], BF16), dscr("b_ffn2_w_in", [2, D, 2 * FF], BF16)]
    wb_ffn_out = [dscr("b_ffn1_w_out", [2, FF, D], BF16), dscr("b_ffn2_w_out", [2, FF, D], BF16)]
    wb_even_in = dscr("b_even_w_in", [D, 4096], BF16)
    wb_even_out = dscr("b_even_w_out", [D, D], BF16)
    wb_odd_in = dscr("b_odd_w_in", [D, 3080], BF16)
    wb_odd_out = dscr("b_odd_w_out", [D, D], BF16)

    S = Sched()
    es = contextlib.ExitStack()
    with es:
        def sb(name, shape, dt):
            return es.enter_context(nc.sbuf_tensor("sb_" + name, list(shape), dt))

        x_sb = sb("x_sb", [128, KT, 512], F32)
        hn = sb("hn", [128, KT, 512], BF16)
        NSLOT = 3
        SLOTSZ = FT * 256
        wring = sb("wring", [128, NSLOT, SLOTSZ], BF16)
        xin = sb("xin", [128, 2, D], F32)
        c128 = sb("c128", [128, C128W], F32)
        c64 = sb("c64", [64, C64W], F32)
        identb = sb("identb", [128, 128], BF16)
        rmatb = sb("rmatb", [128, 128], BF16)
        onesb = sb("onesb", [128, 128], BF16)
        normw = sb("normw", [128, 7, KT], F32)
        hgain = sb("hgain", [128, 3, 4], F32)
        lbt = sb("lbt", [128, 4, 4], F32)
        lruv = sb("lruv", [128, 8, 4], F32)
        lrud = sb("lrud", [128, 4, 4], F32)
        dncw = sb("dncw", [128, 4, 12], F32)
        bd = sb("bd", [128, 2, 4, 128], BF16)
        dnb = sb("dnb", [64, 2, 8, 4], F32)
        dnraw = sb("dnraw", [64, 2, 4], F32)
        s_ret = sb("s_ret", [128, 512], F32)
        s_hg = sb("s_hg", [128, 512], F32)
        s_dn = sb("s_dn", [128, 512], F32)
        hstate = sb("hstate", [128, 3, 4], F32)
        hcar = sb("hcar", [128, 16, 3], F32)
        rt = sb("rt", [128, 2, 512], F32)
        ARENA = 51800
        nmr = sb("nmr", [64, 3, 5, 256], F32)
        arena = sb("arena", [128, ARENA], BF16)

        banks = [es.enter_context(nc.psum_tensor("ps%d" % i, [128, 512], F32)) for i in range(8)]
        eng_sems = {e: es.enter_context(nc.semaphore("sem_" + e)) for e in ("pe", "act", "dve", "pool")}
        dma_names = ["w%d" % i for i in range(NSLOT)] + ["xin0", "xin1"] + ["cast%d" % i for i in range(6)] + ["m%s%d" % (q, i) for q in ("sp", "pool") for i in range(Sched.NMISC)]
        dma_sems = {k: es.enter_context(nc.semaphore("dsem_" + k)) for k in dma_names}
        block = es.enter_context(nc.Block())
        es.enter_context(nc.allow_non_contiguous_dma(reason="tiny per-channel vectors"))

        def C128(k):
            o, n = off128[k]
            return c128[:, o:o + n]

        def C64(k):
            o, n = off64[k]
            return c64[:, o:o + n]

        ident = C128("ident")

        bank_ctr = [0]

        sub_ctr = {}

        def P(sub=None):
            if sub is None:
                i = bank_ctr[0] % 8
                bank_ctr[0] += 1
                return banks[i], ("ps", i)
            k, n = sub
            mine = [b for b in range(8) if b % n == k]
            j = sub_ctr.get(sub, 0)
            sub_ctr[sub] = j + 1
            i = mine[j % len(mine)]
            return banks[i], ("ps", i)

        class Arena:
            def __init__(self):
                self.off = 0
                self.gen = 0
                self.raw = {}

            def reset(self):
                S.barrier()
                self.off = 0
                self.gen += 1

            def alloc(self, name, n, dt):
                if dt == F32:
                    ne = 2 * n
                else:
                    ne = n
                ne = (ne + 15) // 16 * 16
                assert self.off + ne <= ARENA, (name, self.off, ne)
                v = arena[:, self.off:self.off + ne]
                self.raw[name] = v
                self.off += ne
                if dt == F32:
                    v = v.bitcast(F32)[:, 0:n]
                else:
                    v = v[:, 0:n]
                return v, ("ar", self.gen, name)

        A = Arena()

        def pe(fn, r=(), w=()):
            return S.op("pe", fn, r, w)

        def act(fn, r=(), w=()):
            return S.op("act", fn, r, w)

        def dve(fn, r=(), w=()):
            return S.op("dve", fn, r, w)

        def pool(fn, r=(), w=()):
            return S.op("pool", fn, r, w)

        def mm(out, lhsT, rhs, start, stop, r, w):
            return pe(lambda e: e.matmul(out, lhsT=lhsT, rhs=rhs, start=start, stop=stop), r, w)

        def tr(out, in_, idn, r, w):
            return pe(lambda e: e.transpose(out=out, in_=in_, identity=idn), r, w)

        def a_act(out, in_, func, r, w, bias=None, scale=None):
            kw = {}
            if bias is not None:
                kw["bias"] = bias
            if scale is not None:
                kw["scale"] = scale
            return act(lambda e: e.activation(out=out, in_=in_, func=func, **kw), r, w)

        def v_tt(out, in0, in1, op, r, w, eng="dve"):
            return S.op(eng, lambda e: e.tensor_tensor(out=out, in0=in0, in1=in1, op=op), r, w)

        def v_ts(out, in0, s1, s2, op0, op1, r, w, eng="dve"):
            if op1 is None:
                return S.op(eng, lambda e: e.tensor_scalar(out=out, in0=in0, scalar1=s1, scalar2=None, op0=op0), r, w)
            return S.op(eng, lambda e: e.tensor_scalar(out=out, in0=in0, scalar1=s1, scalar2=s2, op0=op0, op1=op1), r, w)

        def v_stt(out, in0, sc, in1, op0, op1, r, w, eng="dve"):
            return S.op(eng, lambda e: e.scalar_tensor_tensor(out=out, in0=in0, scalar=sc, in1=in1, op0=op0, op1=op1), r, w)

        def v_copy(out, in_, r, w, eng="dve"):
            return S.op(eng, lambda e: e.tensor_copy(out=out, in_=in_), r, w)

        def v_memset(ap, val, w, eng="dve"):
            return S.op(eng, lambda e: e.memset(ap, val), (), w)

        class WStream:
            def __init__(self):
                self.units = []
                self.issued = 0
                self.consumed = 0

            def plan(self, tag, src, shape):
                self.units.append((tag, src, shape))

            @staticmethod
            def group_of(tag):
                if tag[0].startswith("ffn"):
                    which, layer = tag[1], tag[2]
                    return {(0, 0): 0, (1, 0): 2, (0, 1): 3, (1, 1): 5}[(which, layer)]
                return 1 if tag[0].startswith("even") else 4

            def _issue(self):
                i = self.issued
                tag, src, shape = self.units[i]
                S.wait_deps("sp", [cast_dep[self.group_of(tag)]])
                slot = i % NSLOT
                n = 1
                for d in shape:
                    n *= d
                dst = wring[:, slot, 0:n]
                if len(shape) == 2:
                    dst = dst.rearrange("p (a b) -> p a b", a=shape[0])
                srcs = src if isinstance(src, list) else [src]
                if len(srcs) == 1:
                    pairs = [(dst, srcs[0])]
                else:
                    g = len(srcs)
                    d4 = dst.rearrange("p a (g c) -> p a g c", g=g)
                    pairs = [(d4[:, :, k, :], srcs[k]) for k in range(g)]
                for dd, ss in pairs:
                    S.dma("sp", lambda e, dd=dd, ss=ss: e.dma_start(out=dd, in_=ss), "w%d" % slot,
                          reads=[], writes=[("wslot", slot)])
                self.issued += 1

            def next(self, tag):
                while self.issued < min(len(self.units), self.consumed + NSLOT):
                    self._issue()
                i = self.consumed
                t, src, shape = self.units[i]
                assert t == tag, (t, tag)
                slot = i % NSLOT
                n = 1
                for d in shape:
                    n *= d
                v = wring[:, slot, 0:n]
                if len(shape) == 2:
                    v = v.rearrange("p (a b) -> p a b", a=shape[0])
                self.consumed += 1
                return v, ("wslot", slot)

        W = WStream()

        def in_view(wap, c0, ncol):
            return wap.rearrange("(kt p) c -> p kt c", p=128)[:, :, c0:c0 + ncol]

        ntp = TP // 512
        tiles = []
        for i in range(ntp):
            tiles.append(dict(kind="p", tok0=i * 512, NT=512, segs=[(0, 512)], first=(i == 0), last=(i == ntp - 1)))
        tiles.append(dict(kind="s", tok0=TP, NT=128, segs=[(1, 64), (2, 64)], first=True, last=True))

        def plan_ffn(which, layer):
            wi = wb_ffn_in[which][layer]
            wo = wb_ffn_out[which][layer]
            for u in range(11):
                w4 = wi.rearrange("(kt p) (g c) -> p kt g c", p=128, g=2)
                src = [w4[:, :, 0, u * 256:(u + 1) * 256], w4[:, :, 1, u * 256:(u + 1) * 256]]
                W.plan(("ffn_in", which, layer, u), src, (KT, 512))
            for u in range(4):
                src = wo.rearrange("(kt p) c -> p kt c", p=128)[:, :, u * 256:(u + 1) * 256]
                W.plan(("ffn_out", which, layer, u), src, (FT, 256))

        def plan_tile():
            if stage >= 1:
                plan_ffn(0, 0)
            if stage >= 2:
                for u in range(8):
                    W.plan(("even_in", u), in_view(wb_even_in, u * 512, 512), (KT, 512))
                for u in range(4):
                    W.plan(("even_out", u), in_view(wb_even_out, u * 256, 256), (KT, 256))
            if stage >= 3:
                plan_ffn(1, 0)
                plan_ffn(0, 1)
            if stage >= 4:
                W.plan(("odd_in", "bda"), in_view(wb_odd_in, 3072, 8), (KT, 8))
                for u in range(6):
                    W.plan(("odd_in", u), in_view(wb_odd_in, u * 512, 512), (KT, 512))
                for u in range(4):
                    W.plan(("odd_out", u), in_view(wb_odd_out, u * 256, 256), (KT, 256))
            if stage >= 5:
                plan_ffn(1, 1)

        for _ in tiles:
            plan_tile()

        cast_dep = {}

        def cast_w(src2d, dst2d, rows, grp):
            r0 = 0
            while r0 < rows:
                rr = min(256, rows - r0)
                S.dma("pool", lambda e, a=dst2d[r0:r0 + rr, :], b=src2d[r0:r0 + rr, :]: e.dma_start(out=a, in_=b),
                      "cast%d" % grp, reads=[], writes=[("wcast", grp)])
                r0 += rr
            cast_dep[grp] = S.dma_last["cast%d" % grp]

        def cast_group(grp):
            if grp == 0:
                cast_w(w_ffn_in[0][0], wb_ffn_in[0][0], D, 0)
                cast_w(w_ffn_out[0][0], wb_ffn_out[0][0], FF, 0)
            elif grp == 1:
                cast_w(w_even_in, wb_even_in, D, 1)
                cast_w(w_even_out, wb_even_out, D, 1)
            elif grp == 2:
                cast_w(w_ffn_in[1][0], wb_ffn_in[1][0], D, 2)
                cast_w(w_ffn_out[1][0], wb_ffn_out[1][0], FF, 2)
            elif grp == 3:
                cast_w(w_ffn_in[0][1], wb_ffn_in[0][1], D, 3)
                cast_w(w_ffn_out[0][1], wb_ffn_out[0][1], FF, 3)
            elif grp == 4:
                cast_w(w_odd_in, wb_odd_in, D, 4)
                cast_w(w_odd_out, wb_odd_out, D, 4)
            elif grp == 5:
                cast_w(w_ffn_in[1][1], wb_ffn_in[1][1], D, 5)
                cast_w(w_ffn_out[1][1], wb_ffn_out[1][1], FF, 5)

        S.dma("sp", lambda e: e.dma_start(out=c128[:], in_=c128_d), None, writes=["c128"])
        S.dma("sp", lambda e: e.dma_start(out=c64[:], in_=c64_d), None, writes=["c64"])
        S.dma("sp", lambda e: e.dma_start(out=normw[:], in_=norms_d.rearrange("n (kt p) -> p n kt", p=128)), None, writes=["normw"])
        S.dma("sp", lambda e: e.dma_start(out=hgain[:], in_=hnorm_d.rearrange("n (h p) -> p n h", p=128)), None, writes=["hgain"])
        S.dma("sp", lambda e: e.dma_start(out=lbt[:, 0:2, :], in_=lb_logits.rearrange("n (h p) -> p n h", p=128)), None, writes=["lbt"])
        S.dma("sp", lambda e: e.dma_start(out=lruv[:], in_=lru_vecs.rearrange("n (j p) -> p n j", p=128)), None, writes=["lruv"])
        S.dma("sp", lambda e: e.dma_start(out=dncw[:], in_=dn_conv_w.rearrange("n (j p) -> p n j", p=128)), None, writes=["dncw"])
        S.dma("sp", lambda e: e.dma_start(out=dnraw[:], in_=dn_scal.rearrange("(o n) h -> o n h", o=1).to_broadcast([64, 2, 4])), None, writes=["dnraw"])
        bdst_v, _ = A.alloc("bdst", 2 * 4 * 128, F32)
        bdst = bdst_v.rearrange("p (g j c) -> p g j c", g=2, j=4)
        pool(lambda e: e.memset(bdst_v, 0.0), (), ["bdst"])
        for gi, wsrc in enumerate((lru_wa, lru_wx)):
            for n in range(8):
                j, hh = n // 2, n % 2
                S.dma("sp", lambda e, gi=gi, n=n, j=j, hh=hh, wsrc=wsrc: e.dma_start(
                    out=bdst[hh * 64:(hh + 1) * 64, gi, j, hh * 64:(hh + 1) * 64], in_=wsrc[n]), None,
                    reads=[], writes=["bdst"])
        cast_group(0)
        cast_group(1)

        v_copy(identb[:], ident, ["c128"], ["identb"])
        v_copy(rmatb[:], C128("rmat"), ["c128"], ["rmatb"])
        v_copy(onesb[:], C128("ones"), ["c128"], ["onesb"])
        v_copy(bd[:], bdst, ["bdst"], ["bd"])
        v_tt(lbt[:, 2, :], lbt[:, 0, :], lbt[:, 1, :], ALU.subtract, ["lbt"], ["lbt"])
        a_act(lbt[:, 2, :], lbt[:, 2, :], AF.Sigmoid, ["lbt"], ["lbt"])
        v_ts(lbt[:, 3, :], lbt[:, 2, :], -1.0, 1.0, ALU.mult, ALU.add, ["lbt"], ["lbt"])
        a_act(lrud[:, 2, :], lruv[:, 7, :], AF.Exp, ["lruv"], ["lrud"], scale=-1.0)
        a_act(lrud[:, 3, :], lrud[:, 2, :], AF.Ln, ["lrud"], ["lrud"], bias=1.0)
        v_ts(lrud[:, 0, :], lrud[:, 3, :], -LRU_C, None, ALU.mult, None, ["lrud"], ["lrud"])
        v_ts(lrud[:, 1, :], lrud[:, 3, :], -2.0 * LRU_C, None, ALU.mult, None, ["lrud"], ["lrud"])
        a_act(dnraw[:, 0, :], dnraw[:, 0, :], AF.Exp, ["dnraw"], ["dnraw"])
        v_ts(dnraw[:, 0, :], dnraw[:, 0, :], -1.0, None, ALU.mult, None, ["dnraw"], ["dnraw"])
        for cc in range(8):
            v_copy(dnb[:, 0, cc, :], dnraw[:, 1, :], ["dnraw"], ["dnb"])
            v_copy(dnb[:, 1, cc, :], dnraw[:, 0, :], ["dnraw"], ["dnb"])

        def interleave(gen_fns, width, extra=None):
            active = []
            nxt = 0
            ex = extra() if extra is not None else None
            while active or nxt < len(gen_fns) or ex is not None:
                while len(active) < width and nxt < len(gen_fns):
                    active.append(gen_fns[nxt]())
                    nxt += 1
                for g in list(active):
                    try:
                        next(g)
                    except StopIteration:
                        active.remove(g)
                if ex is not None:
                    try:
                        next(ex)
                    except StopIteration:
                        ex = None

        def load_x(tile):
            NT = tile["NT"]
            src = xp if tile["kind"] == "p" else xs
            t0 = tile["tok0"] if tile["kind"] == "p" else 0
            for b in range(NT // 128):
                sl = b % 2
                S.dma("pool", lambda e, sl=sl, b=b: e.dma_start(out=xin[:, sl, :], in_=src[t0 + b * 128:t0 + (b + 1) * 128, :]),
                      "xin%d" % sl, reads=[], writes=[("xin", sl)])
                for half in range(2):
                    bk, br = P()
                    for q in range(4):
                        kt = half * 4 + q
                        tr(bk[:, q * 128:(q + 1) * 128], xin[:, sl, kt * 128:(kt + 1) * 128], ident, [("xin", sl), "c128"], [br])
                    act(lambda e, bk=bk, half=half, b=b: e.copy(
                        out=x_sb[:, half * 4:half * 4 + 4, b * 128:(b + 1) * 128],
                        in_=bk[:, :].rearrange("p (q t) -> p q t", q=4)), [br], ["x"])

        def rmsnorm(NT, nidx):
            sq, sqr = A.alloc("nsq", KT * NT, BF16)
            sq3 = sq.rearrange("p (k t) -> p k t", k=KT)
            act(lambda e: e.activation(out=sq3, in_=x_sb[:, :, 0:NT], func=AF.Square), ["x"], [sqr])
            bk, br = P()
            for kt in range(KT):
                mm(bk[:, 0:NT], onesb[:], sq3[:, kt, :], kt == 0, kt == KT - 1, [sqr, "onesb"], [br])
            a_act(rt[:, 0, 0:NT], bk[:, 0:NT], AF.Ln, [br], ["rt0"], bias=EPS, scale=1.0 / D)
            a_act(rt[:, 1, 0:NT], rt[:, 0, 0:NT], AF.Exp, ["rt0"], ["rt1"], scale=-0.5)
            for kt in range(KT):
                v_stt(hn[:, kt, 0:NT], x_sb[:, kt, 0:NT], normw[:, nidx, kt:kt + 1], rt[:, 1, 0:NT], ALU.mult, ALU.mult,
                      ["x", "normw", "rt1"], ["hn"])

        def rstd_only(NT):
            sq, sqr = A.alloc("nsq", KT * NT, BF16)
            sq3 = sq.rearrange("p (k t) -> p k t", k=KT)
            act(lambda e: e.activation(out=sq3, in_=x_sb[:, :, 0:NT], func=AF.Square), ["x"], [sqr])
            bk, br = P()
            for kt in range(KT):
                mm(bk[:, 0:NT], onesb[:], sq3[:, kt, :], kt == 0, kt == KT - 1, [sqr, "onesb"], [br])
            a_act(rt[:, 0, 0:NT], bk[:, 0:NT], AF.Ln, [br], ["rt0"], bias=EPS, scale=1.0 / D)
            a_act(rt[:, 1, 0:NT], rt[:, 0, 0:NT], AF.Exp, ["rt0"], ["rt1"], scale=-0.5)

        def prenorm(NT, nidx, mo):
            v_ts(hn[:, mo, 0:NT], x_sb[:, mo, 0:NT], normw[:, nidx, mo:mo + 1], None, ALU.mult, None, ["x", "normw"], ["hn"])

        def ffn(tile, which, layer, next_norm=None):
            NT = tile["NT"]
            A.reset()
            rstd_only(NT)
            hid, hidr = A.alloc("hid", FT * NT, BF16)
            hid3 = hid.rearrange("p (m t) -> p m t", m=FT)
            sg = [A.alloc("sg%d" % i, NT, F32) for i in range(2)]
            su = [A.alloc("su%d" % i, NT, F32) for i in range(2)]
            for u in range(11):
                Wv, wr = W.next(("ffn_in", which, layer, u))
                for mi in range(2):
                    m = 2 * u + mi
                    bg, bgr = P()
                    for kt in range(KT):
                        mm(bg[:, 0:NT], Wv[:, kt, mi * 128:(mi + 1) * 128], hn[:, kt, 0:NT], kt == 0, kt == KT - 1, [wr, "hn"], [bgr])
                    bu, bur = P()
                    for kt in range(KT):
                        mm(bu[:, 0:NT], Wv[:, kt, 256 + mi * 128:256 + (mi + 1) * 128], hn[:, kt, 0:NT], kt == 0, kt == KT - 1,
                           [wr, "hn"], [bur])
                    sgv, sgr = sg[m % 2]
                    suv, sur = su[m % 2]
                    v_tt(sgv, bg[:, 0:NT], rt[:, 1, 0:NT], ALU.mult, [bgr, "rt1"], [sgr])
                    a_act(sgv, sgv, AF.Silu, [sgr], [sgr])
                    v_tt(suv, bu[:, 0:NT], rt[:, 1, 0:NT], ALU.mult, [bur, "rt1"], [sur])
                    v_tt(hid3[:, m, :], sgv, suv, ALU.mult, [sgr, sur], [(hidr, m)])
            for u in range(4):
                Wv, wr = W.next(("ffn_out", which, layer, u))
                for mi in range(2):
                    mo = 2 * u + mi
                    bk, br = P()
                    for kt in range(FT):
                        mm(bk[:, 0:NT], Wv[:, kt, mi * 128:(mi + 1) * 128], hid3[:, kt, :], kt == 0, kt == FT - 1,
                           [wr, (hidr, kt)], [br])
                    v_stt(x_sb[:, mo, 0:NT], bk[:, 0:NT], 0.5, x_sb[:, mo, 0:NT], ALU.mult, ALU.add, [br, "x"], ["x"])
                    if next_norm is not None:
                        prenorm(NT, next_norm, mo)

        def out_proj(tile, tagname, mix3, mixr, next_norm=None):
            NT = tile["NT"]
            for u in range(4):
                Wv, wr = W.next((tagname, u))
                for mi in range(2):
                    mo = 2 * u + mi
                    bk, br = P()
                    for kt in range(KT):
                        mm(bk[:, 0:NT], Wv[:, kt, mi * 128:(mi + 1) * 128], mix3[:, kt, :], kt == 0, kt == KT - 1, [wr, mixr], [br])
                    v_tt(x_sb[:, mo, 0:NT], bk[:, 0:NT], x_sb[:, mo, 0:NT], ALU.add, [br, "x"], ["x"])
                    if next_norm is not None:
                        prenorm(NT, next_norm, mo)

        def head_norm(NT, osb3, osbr, gate3, gater, gidx, mix3, mixr, koff, scratch=None):
            if scratch is None:
                sqb, sqbr = A.alloc("hsq%d" % koff, 4 * NT, BF16)
                tm, tmr = A.alloc("htm%d" % koff, NT, F32)
                sqbl = [sqbr]
            else:
                sqb, sqbl, tm, tmr = scratch
            sqb3 = sqb.rearrange("p (h t) -> p h t", h=4)
            osbl = osbr if isinstance(osbr, list) else [osbr]
            act(lambda e: e.activation(out=sqb3, in_=osb3, func=AF.Square), osbl, sqbl)
            for h in range(4):
                bk, br = P()
                mm(bk[:, 0:NT], onesb[:], sqb3[:, h, :], True, True, sqbl + ["onesb"], [br])
                a_act(rt[:, 0, 0:NT], bk[:, 0:NT], AF.Ln, [br], ["rt0"], bias=EPS, scale=1.0 / 128.0)
                a_act(rt[:, 1, 0:NT], rt[:, 0, 0:NT], AF.Exp, ["rt0"], ["rt1"], scale=-0.5)
                v_tt(tm, osb3[:, h, :], rt[:, 1, 0:NT], ALU.mult, osbl + ["rt1"], [tmr])
                v_stt(mix3[:, koff + h, :], tm, hgain[:, gidx, h:h + 1], gate3[:, h, :], ALU.mult, ALU.mult,
                      [tmr, "hgain", gater], [mixr])

        def seg_of_chunk(tile, c):
            if tile["kind"] == "p":
                return 0, 0, (c == 0), (c == tile["NT"] // 64 - 1)
            return 1 + c, c, True, True

        def state_io(tile, seq, start, end, s32, s32r, st_in, o_list, when):
            if when == "start":
                if seq == 0:
                    if tile["first"]:
                        v_memset(s32[:], 0.0, [s32r], eng="pool")
                else:
                    S.dma("pool", lambda e: e.dma_start(out=s32[:].rearrange("p (h e) -> p h e", h=4),
                                                        in_=st_in[seq - 1].rearrange("h d e -> d h e")),
                          None, reads=[], writes=[s32r])
            else:
                if seq == 0:
                    if tile["last"]:
                        S.dma("pool", lambda e: e.dma_start(out=o_list[0].rearrange("h d e -> d h e"),
                                                            in_=s32[:].rearrange("p (h e) -> p h e", h=4)),
                              None, reads=[s32r], writes=[])
                else:
                    S.dma("pool", lambda e: e.dma_start(out=o_list[1][seq - 1].rearrange("h d e -> d h e"),
                                                        in_=s32[:].rearrange("p (h e) -> p h e", h=4)),
                          None, reads=[s32r], writes=[])

        def linattn_core(tile, kind, qF, qFr, kF, kFr, kS, kSr, vtm, vtmr, osb3, osbr, s32, s32r, st_in, o_list,
                         ebend=None, ebendr=None, extra=None):
            NT = tile["NT"]
            NC = NT // 64
            sbf, sbfr = A.alloc("sbf_" + kind, (NC + 1) * 512, BF16)
            sbf3 = sbf.rearrange("p (c n) -> p c n", c=NC + 1)
            WD = 1 if tile["kind"] == "s" else 3
            attm = [A.alloc("attm%d_%s" % (i, kind), 256, BF16) for i in range(WD)]
            kdt = [A.alloc("kdt%d_%s" % (i, kind), 512, BF16) for i in range(WD)]
            mask = C64("retmask") if kind == "ret" else C64("incl4")
            done = {}

            def chunk_gen(c):
                seq, sgi, sstart, send = seg_of_chunk(tile, c)
                cs = slice(c * 64, (c + 1) * 64)
                bA, bAr = P((c % WD, WD + 1))
                for h in range(4):
                    mm(bA[0:64, h * 64:(h + 1) * 64], kF[:, h, cs], qF[:, h, cs], True, True, [kFr, qFr], [bAr])
                yield
                av, avr = attm[c % WD]
                v_tt(av[0:64, :], bA[0:64, 0:256], mask, ALU.mult, [bAr, "c64"], [avr])
                bT, bTr = P((c % WD, WD + 1))
                bTb = bT[0:64, 0:256].bitcast(BF16)
                for h in range(4):
                    tr(bTb[:, h * 128:(h + 1) * 128], kS[:, h, cs], identb[:], [kSr, "identb"], [bTr])
                yield
                kv, kvr = kdt[c % WD]
                if kind == "ret":
                    v_tt(kv[0:64, :], bTb, C64("kdec"), ALU.mult, [bTr, "c64"], [kvr])
                else:
                    act(lambda e, kv=kv, bTb=bTb: e.copy(out=kv[0:64, :], in_=bTb), [bTr], [kvr])
                yield
                while c > 0 and not done.get(c - 1):
                    yield
                if sstart:
                    state_io(tile, seq, sstart, send, s32, s32r, st_in, o_list, "start")
                    act(lambda e, c=c: e.copy(out=sbf3[:, c, :], in_=s32[:]), [s32r], [(sbfr, c)])
                bO, bOr = P((c % WD, WD + 1))
                for h in range(4):
                    mm(bO[:, h * 64:(h + 1) * 64], vtm[0:64, c, h * 128:(h + 1) * 128], av[0:64, h * 64:(h + 1) * 64], True, False,
                       [vtmr, avr], [bOr])
                    mm(bO[:, h * 64:(h + 1) * 64], sbf3[:, c, h * 128:(h + 1) * 128], qF[:, h, cs], False, True,
                       [(sbfr, c), qFr], [bOr])
                act(lambda e, bO=bO, cs=cs: e.copy(out=osb3[:, :, cs], in_=bO[:, 0:256].rearrange("p (h t) -> p h t", h=4)),
                    [bOr], [(osbr, c)])
                bS, bSr = P((c % WD, WD + 1))
                for h in range(4):
                    mm(bS[:, h * 128:(h + 1) * 128], kv[0:64, h * 128:(h + 1) * 128], vtm[0:64, c, h * 128:(h + 1) * 128], True, True,
                       [kvr, vtmr], [bSr])
                for h in range(4):
                    hs_ = slice(h * 128, (h + 1) * 128)
                    if kind == "ret":
                        sc = SDEC[h]
                        rr = [s32r, bSr]
                    else:
                        sc = ebend[:, h, c:c + 1]
                        rr = [s32r, bSr, ebendr]
                    v_stt(s32[:, hs_], s32[:, hs_], sc, bS[:, hs_], ALU.mult, ALU.add, rr, [s32r])
                if send:
                    state_io(tile, seq, sstart, send, s32, s32r, st_in, o_list, "end")
                else:
                    act(lambda e, c=c: e.copy(out=sbf3[:, c + 1, :], in_=s32[:]), [s32r], [(sbfr, c + 1)])
                done[c] = True

            gens = [(lambda c=c: chunk_gen(c)) for c in range(NC)]
            interleave(gens, WD, extra=(None if extra is None else (lambda: extra((WD, WD + 1)))))
            return [(osbr, c) for c in range(NC)]

        def gate_gen(NT, Wv, wr, gate3, gater, alloc):
            for h in range(4):
                bk, br = yield from alloc()
                for kt in range(KT):
                    mm(bk[:, 0:NT], Wv[:, kt, h * 128:(h + 1) * 128], hn[:, kt, 0:NT], kt == 0, kt == KT - 1, [wr, "hn"], [br])
                yield
                a_act(gate3[:, h, :], bk[:, 0:NT], AF.Silu, [br], [gater])
                yield br

        def tm_proj(tile, Wv, wr, vt3, vtr):
            NT = tile["NT"]
            for c in range(NT // 64):
                bk, br = P()
                for kt in range(KT):
                    mm(bk[0:64, :], hn[:, kt, c * 64:(c + 1) * 64], Wv[:, kt, 0:512], kt == 0, kt == KT - 1, [wr, "hn"], [br])
                act(lambda e, bk=bk, c=c: e.copy(out=vt3[0:64, c, :], in_=bk[0:64, :]), [br], [vtr])

        def fm_proj(NT, Wv, wr, h, sub=None):
            bk, br = P(sub)
            for kt in range(KT):
                mm(bk[:, 0:NT], Wv[:, kt, h * 128:(h + 1) * 128], hn[:, kt, 0:NT], kt == 0, kt == KT - 1, [wr, "hn"], [br])
            return bk, br

        def even_mixer(tile):
            NT = tile["NT"]
            NC = NT // 64
            A.reset()
            mix, mixr = A.alloc("mix", KT * NT, BF16)
            mix3 = mix.rearrange("p (k t) -> p k t", k=KT)
            rmsnorm(NT, 2)
            if DBG <= -1:
                for u in range(0, 8):
                    W.next(("even_in", u))
                for u in range(4):
                    W.next(("even_out", u))
                return
            tab, tabr = A.alloc("tab", 2 * NT, F32)
            tab3 = tab.rearrange("p (a t) -> p a t", a=2)
            S.dma("pool", lambda e: e.dma_start(out=tab3, in_=rope_d[:, :, tile["tok0"]:tile["tok0"] + NT]), None,
                  reads=[], writes=[tabr])
            if DBG <= 0:
                for u in range(0, 8):
                    W.next(("even_in", u))
                for u in range(4):
                    W.next(("even_out", u))
                return

            def al3(name, dt):
                v, r = A.alloc(name, 4 * NT, dt)
                return v.rearrange("p (h t) -> p h t", h=4), r

            qd3, qdr = al3("qd", BF16)
            kr3, krr = al3("krot", BF16)
            vt, vtr = A.alloc("vtm", NC * 512, BF16)
            vt3 = vt.rearrange("p (c n) -> p c n", c=NC)
            gate3, gater = al3("gate", F32)
            osb3, osbr = al3("osb", F32)
            xbf = [A.alloc("xbf%d" % i, NT, BF16) for i in range(2)]
            ta = [A.alloc("ta%d" % i, NT, F32) for i in range(2)]
            tb = [A.alloc("tb%d" % i, NT, F32) for i in range(2)]
            qdecv = C128("qdec").rearrange("p (h t) -> p h t", h=4)

            def rope_unit(Wv, wr, dst3, dstr, isq):
                for h in range(4):
                    bk, br = fm_proj(NT, Wv, wr, h)
                    xv, xr = xbf[h % 2]
                    LV = int(os.environ.get("KLV", "9"))
                    act(lambda e, xv=xv, bk=bk: e.copy(out=xv, in_=bk[:, 0:NT]), [br], [xr])
                    if LV <= 1:
                        continue
                    b2, b2r = P()
                    mm(b2[:, 0:NT], rmatb[:], xv, True, True, [xr, "rmatb"], [b2r])
                    if LV <= 2:
                        continue
                    tav, tar = ta[h % 2]
                    tbv, tbr = tb[h % 2]
                    v_tt(tav, bk[:, 0:NT], tab3[:, 0, :], ALU.mult, [br, tabr], [tar])
                    if LV <= 3:
                        continue
                    v_tt(tbv, b2[:, 0:NT], tab3[:, 1, :], ALU.mult, [b2r, tabr], [tbr])
                    if LV <= 4:
                        continue
                    if isq:
                        v_tt(tav, tav, tbv, ALU.add, [tar, tbr], [tar])
                        if os.environ.get("KVAR", "0") == "1":
                            for c in range(NC):
                                v_tt(dst3[:, h, c * 64:(c + 1) * 64], tav[:, c * 64:(c + 1) * 64], qdecv[:, h, :], ALU.mult, [tar, "c128"], [dstr])
                        else:
                            v_tt(dst3[:, h, :].rearrange("p (c t) -> p c t", t=64),
                                 tav.rearrange("p (c t) -> p c t", t=64),
                                 qdecv[:, h:h + 1, :].to_broadcast([128, NC, 64]), ALU.mult, [tar, "c128"], [dstr])
                    else:
                        v_tt(dst3[:, h, :], tav, tbv, ALU.add, [tar, tbr], [dstr])

            def bail(k):
                for u in range(k, 8):
                    W.next(("even_in", u))
                for u in range(4):
                    W.next(("even_out", u))

            Wv, wr = W.next(("even_in", 0))
            rope_unit(Wv, wr, qd3, qdr, True)
            if DBG <= 1:
                return bail(1)
            Wv, wr = W.next(("even_in", 1))
            rope_unit(Wv, wr, kr3, krr, False)
            Wv, wr = W.next(("even_in", 2))
            tm_proj(tile, Wv, wr, vt3, vtr)
            Wg, wgr = W.next(("even_in", 3))

            def sub_alloc(sub):
                def alloc():
                    return P(sub)
                    yield
                return alloc

            osbr = linattn_core(tile, "ret", qd3, qdr, kr3, krr, kr3, krr, vt3, vtr, osb3, osbr, s_ret, "s_ret", st_ret, o_ret,
                                extra=lambda sub: gate_gen(NT, Wg, wgr, gate3, gater, sub_alloc(sub)))
            if DBG <= 3:
                return bail(4)
            head_norm(NT, osb3, osbr, gate3, gater, 0, mix3, mixr, 0)
            if DBG <= 4:
                return bail(4)

            A.reset()
            mix, mixr = A.alloc("mix", KT * NT, BF16)
            mix3 = mix.rearrange("p (k t) -> p k t", k=KT)
            qe3, qer = al3("qe", BF16)
            ke3, ker = al3("ke", BF16)
            kn3, knr = al3("kend", BF16)
            vt2, vt2r = A.alloc("vtm2", NC * 512, BF16)
            vt23 = vt2.rearrange("p (c n) -> p c n", c=NC)
            gate23, gate2r = al3("gate2", F32)
            osb23, osb2r = al3("osb2", F32)
            qs3, qsr = al3("qsil", F32)
            eb3, ebr = al3("eb", F32)
            fb3, fbr = al3("fb", F32)
            ebend, ebendr = A.alloc("ebend", 4 * NC, F32)
            ebend3 = ebend.rearrange("p (h c) -> p h c", h=4)
            t1 = [A.alloc("t1_%d" % i, NT, F32) for i in range(2)]
            Wv, wr = W.next(("even_in", 4))
            for h in range(4):
                bk, br = fm_proj(NT, Wv, wr, h)
                a_act(qs3[:, h, :], bk[:, 0:NT], AF.Silu, [br], [qsr])
            Wv, wr = W.next(("even_in", 5))
            for h in range(4):
                bk, br = fm_proj(NT, Wv, wr, h)
                tv, tr_ = t1[h % 2]
                a_act(tv, bk[:, 0:NT], AF.Sigmoid, [br], [tr_])
                v_ts(fb3[:, h, :], tv, lbt[:, 3, h:h + 1], lbt[:, 2, h:h + 1], ALU.mult, ALU.add, [tr_, "lbt"], [(fbr, h)])
                v_ts(eb3[:, h, :], fb3[:, h, :], -1.0, 1.0, ALU.mult, ALU.add, [(fbr, h)], [(ebr, h)])
                a_act(fb3[:, h, :], fb3[:, h, :], AF.Ln, [(fbr, h)], [(fbr, h)])
                dve(lambda e, h=h: e.tensor_tensor_scan(out=fb3[:, h, :], data0=C128("reset")[:, 0:NT], data1=fb3[:, h, :],
                                                        initial=0.0, op0=ALU.mult, op1=ALU.add),
                    [(fbr, h), "c128"], [(fbr, h)])
                tv2, tr2 = t1[(h + 1) % 2]
                a_act(tv2, fb3[:, h, :], AF.Exp, [(fbr, h)], [tr2], scale=-1.0)
                v_tt(ke3[:, h, :], eb3[:, h, :], tv2, ALU.mult, [(ebr, h), tr2], [ker])
                a_act(eb3[:, h, :], fb3[:, h, :], AF.Exp, [(fbr, h)], [(ebr, h)])
                v_tt(qe3[:, h, :], qs3[:, h, :], eb3[:, h, :], ALU.mult, [qsr, (ebr, h)], [qer])
                act(lambda e, h=h: e.copy(out=ebend3[:, h, :], in_=eb3[:, h, :].rearrange("p (c t) -> p c t", t=64)[:, :, 63]),
                    [(ebr, h)], [ebendr])
                v_tt(kn3[:, h, :].rearrange("p (c t) -> p c t", t=64), ke3[:, h, :].rearrange("p (c t) -> p c t", t=64),
                     ebend3[:, h, :].unsqueeze(2).to_broadcast([128, NC, 64]), ALU.mult, [ker, ebendr], [knr])
            Wv, wr = W.next(("even_in", 6))
            tm_proj(tile, Wv, wr, vt23, vt2r)
            Wg2, wg2r = W.next(("even_in", 7))
            osb2r = linattn_core(tile, "hg", qe3, qer, ke3, ker, kn3, knr, vt23, vt2r, osb23, osb2r, s_hg, "s_hg", st_hg, o_hg,
                                 ebend=ebend3, ebendr=ebendr,
                                 extra=lambda sub: gate_gen(NT, Wg2, wg2r, gate23, gate2r, sub_alloc(sub)))
            head_norm(NT, osb23, osb2r, gate23, gate2r, 1, mix3, mixr, 4)
            out_proj(tile, "even_out", mix3, mixr, next_norm=4)

        def odd_mixer(tile):
            NT = tile["NT"]
            NC = NT // 64
            nseg = len(tile["segs"])
            L = tile["segs"][0][1]
            def al3(name, dt, n=4):
                v, r = A.alloc(name, n * NT, dt)
                return v.rearrange("p (h t) -> p h t", h=n), r

            def common():
                A.reset()
                mix, mixr = A.alloc("mix", KT * NT, BF16)
                beta, betar = A.alloc("beta", NC * 4, F32)
                loga, logar = A.alloc("loga", NC * 4, F32)
                egdk, egdkr = A.alloc("egdk", NC * 8, F32)
                egrow3, egr = al3("egrow", F32)
                return (mix.rearrange("p (k t) -> p k t", k=KT), mixr, beta.rearrange("p (c h) -> p c h", c=NC), betar,
                        loga.rearrange("p (c h) -> p c h", c=NC), logar, egdk, egdk.rearrange("p (c h) -> p c h", c=NC), egdkr,
                        egrow3, egr)

            mix3, mixr, beta3, betar, loga3, logar, egdk, egdk3, egdkr, egrow3, egr = common()
            rmsnorm(NT, 3)

            Wv, wr = W.next(("odd_in", "bda"))
            bk, br = P()
            for c in range(NC):
                for kt in range(KT):
                    mm(bk[0:64, c * 8:(c + 1) * 8], hn[:, kt, c * 64:(c + 1) * 64], Wv[:, kt, 0:8], kt == 0, kt == KT - 1, [wr, "hn"], [br])
            bk3 = bk[0:64, 0:NC * 8].rearrange("p (c h) -> p c h", c=NC)
            a_act(beta3[0:64], bk3[:, :, 0:4], AF.Sigmoid, [br], [betar])
            v_tt(loga3[0:64], bk3[:, :, 4:8], dnb[:, 0, 0:NC, :], ALU.add, [br, "dnb"], [logar])
            a_act(loga3[0:64], loga3[0:64], AF.Exp, [logar], [logar])
            a_act(loga3[0:64], loga3[0:64], AF.Ln, [logar], [logar], bias=1.0)
            v_tt(loga3[0:64], loga3[0:64], dnb[:, 1, 0:NC, :], ALU.mult, [logar, "dnb"], [logar])
            b2, b2r = P()
            for c in range(NC):
                mm(b2[0:64, c * 8:c * 8 + 4], C64("U"), loga3[0:64, c, :], True, True, ["c64", logar], [b2r])
                mm(b2[0:64, c * 8 + 4:c * 8 + 8], C64("Urev"), loga3[0:64, c, :], True, True, ["c64", logar], [b2r])
            a_act(egdk[0:64, :], b2[0:64, 0:NC * 8], AF.Exp, [b2r], [egdkr])
            def make_X(c, Xv, Xr):
                X3 = Xv[0:64, 0:256].rearrange("p (h t) -> p h t", h=4)
                v_tt(X3, C64("U").unsqueeze(1).to_broadcast([64, 4, 64]),
                     loga3[0:64, c, :].unsqueeze(2).to_broadcast([64, 4, 64]), ALU.mult, ["c64", logar], [Xr])
                return X3

            Xt = [A.alloc("Xt%d" % i, 256, F32) for i in range(2)]
            for c in range(NC):
                Xv, Xr = Xt[c % 2]
                X3 = make_X(c, Xv, Xr)
                bE, bEr = P()
                for h in range(4):
                    mm(bE[:, h * 64:(h + 1) * 64], C64("ones")[:, 0:128], X3[:, h, :], True, True, ["c64", Xr], [bEr])
                act(lambda e, bE=bE, c=c: e.activation(out=egrow3[:, :, c * 64:(c + 1) * 64],
                                                       in_=bE[:, 0:256].rearrange("p (h t) -> p h t", h=4), func=AF.Exp),
                    [bEr], [egr])

            cvs = {}

            def conv_alloc(nbuf):
                cvs["nbuf"] = nbuf
                xh, xhr = A.alloc("xh", 4 * nseg * (3 + L), F32)
                cvs["xh4"] = xh.rearrange("p (j s t) -> p j s t", j=4, s=nseg)
                cvs["xhr"] = xhr
                cvs["cst"] = [A.alloc("cst%d" % i, 128, F32) for i in range(2)]
                cvs["cv"] = [A.alloc("cv%d" % i, NT, F32) for i in range(nbuf)]
                cvs["n"] = 0

            def conv_unit(g, Wv, wr, consume):
                xh4, xhr = cvs["xh4"], cvs["xhr"]
                nb = cvs["nbuf"]

                def tile_gen(j):
                    sub = (j % nb, nb)
                    gj = g * 4 + j
                    bk, br = fm_proj(NT, Wv, wr, j, sub)
                    for si, (seq, LL) in enumerate(tile["segs"]):
                        if seq == 0:
                            if tile["first"]:
                                v_memset(xh4[:, j, si, 0:3], 0.0, [(xhr, j)], eng="pool")
                            else:
                                v_copy(xh4[:, j, si, 0:3], hcar[:, gj, :], [("hcar", gj)], [(xhr, j)], eng="pool")
                        else:
                            if g == 0:
                                src = st_lc[seq - 1][:, j * 128:(j + 1) * 128]
                            else:
                                src = st_dc[seq - 1][:, (gj - 4) * 128:(gj - 3) * 128]
                            S.dma("pool", lambda e, j=j, si=si, src=src: e.dma_start(out=xh4[:, j, si, 0:3], in_=src.rearrange("r p -> p r")),
                                  None, reads=[], writes=[(xhr, j)])
                    act(lambda e, bk=bk, j=j: e.copy(out=xh4[:, j, :, 3:3 + L], in_=bk[:, 0:NT].rearrange("p (s t) -> p s t", s=nseg)),
                        [br], [(xhr, j)])
                    if tile["kind"] == "p" and not tile["last"]:
                        v_copy(hcar[:, gj, :], xh4[:, j, 0, L:L + 3], [(xhr, j)], [("hcar", gj)], eng="pool")
                    yield
                    cvv, cvr = cvs["cv"][j % nb]
                    cv3 = cvv.rearrange("p (s t) -> p s t", s=nseg)
                    if g == 0:
                        wcol = lambda k, j=j: lruv[:, k, j:j + 1]
                        v_ts(cv3, xh4[:, j, :, 0:L], wcol(0), lruv[:, 4, j:j + 1], ALU.mult, ALU.add, [(xhr, j), "lruv"], [cvr])
                    else:
                        wcol = lambda k, gj=gj: dncw[:, k, gj - 4:gj - 3]
                        v_ts(cv3, xh4[:, j, :, 0:L], wcol(0), None, ALU.mult, None, [(xhr, j), "dncw"], [cvr])
                    for k in range(1, 4):
                        v_stt(cv3, xh4[:, j, :, k:k + L], wcol(k), cv3, ALU.mult, ALU.add, [(xhr, j), "lruv", "dncw", cvr], [cvr],
                              eng="dve")
                    for si, (seq, LL) in enumerate(tile["segs"]):
                        if seq != 0 or tile["last"]:
                            bt, btr = P(sub)
                            tr(bt[0:3, 0:128], xh4[:, j, si, L:L + 3], ident, [(xhr, j), "c128"], [btr])
                            cv_, cvr_ = cvs["cst"][cvs["n"] % 2]
                            cvs["n"] += 1
                            act(lambda e, bt=bt, cv_=cv_: e.copy(out=cv_[0:3, :], in_=bt[0:3, 0:128]), [btr], [cvr_])
                            if g == 0:
                                dd = (o_lc[0][0] if seq == 0 else o_lc[1][seq - 1])[:, j * 128:(j + 1) * 128]
                            else:
                                dd = (o_dc[0][0] if seq == 0 else o_dc[1][seq - 1])[:, (gj - 4) * 128:(gj - 3) * 128]
                            S.dma("pool", lambda e, dd=dd, cv_=cv_: e.dma_start(out=dd, in_=cv_[0:3, :]), None, reads=[cvr_], writes=[])
                    yield
                    yield from consume(j, cvv, cvr, sub, nb)

                interleave([(lambda j=j: tile_gen(j)) for j in range(4)], nb)

            conv_alloc(4)
            hs3, hsr = al3("hs", F32)
            tmpA = [A.alloc("lA%d" % i, NT, F32) for i in range(4)]
            tmpB = [A.alloc("lB%d" % i, NT, F32) for i in range(4)]
            tmpC = [A.alloc("lC%d" % i, NT, F32) for i in range(4)]
            lcb = [A.alloc("lcb%d" % i, NT, BF16) for i in range(4)]

            def lru_consume(j, cvv, cvr, sub, nb):
                lb_, lbr_ = lcb[j % nb]
                act(lambda e: e.copy(out=lb_, in_=cvv), [cvr], [lbr_])
                br_, brr = P(sub)
                mm(br_[:, 0:NT], bd[:, 0, j, :], lb_, True, True, ["bd", lbr_], [brr])
                bi_, bir = P(sub)
                mm(bi_[:, 0:NT], bd[:, 1, j, :], lb_, True, True, ["bd", lbr_], [bir])
                yield
                rv, rr = tmpA[j % nb]
                iv, ir = tmpB[j % nb]
                av, ar = tmpC[j % nb]
                a_act(rv, br_[:, 0:NT], AF.Sigmoid, [brr, "lruv"], [rr], bias=lruv[:, 5, j:j + 1])
                a_act(iv, bi_[:, 0:NT], AF.Sigmoid, [bir, "lruv"], [ir], bias=lruv[:, 6, j:j + 1])
                yield
                a_act(av, rv, AF.Exp, [rr, "lrud"], [ar], scale=lrud[:, 0, j:j + 1])
                a_act(rv, rv, AF.Exp, [rr, "lrud"], [rr], scale=lrud[:, 1, j:j + 1])
                v_tt(iv, iv, cvv, ALU.mult, [ir, cvr], [ir])
                yield
                a_act(rv, rv, AF.Sqrt, [rr], [rr], bias=1.0, scale=-1.0)
                yield
                v_tt(iv, iv, rv, ALU.mult, [ir, rr], [ir])
                for si, (seq, LL) in enumerate(tile["segs"]):
                    if seq == 0 and tile["first"]:
                        v_memset(hstate[:, 0, j:j + 1], 0.0, [("hstate", j)])
                    elif seq != 0:
                        S.dma("pool", lambda e, seq=seq, j=j: e.dma_start(
                            out=hstate[:, seq, j:j + 1], in_=st_lh[seq - 1:seq, j * 128:(j + 1) * 128].rearrange("o p -> p o")),
                            None, reads=[], writes=[("hstate", j)])
                    cols = slice(si * L, (si + 1) * L)
                    dve(lambda e, cols=cols, seq=seq: e.tensor_tensor_scan(out=hs3[:, j, cols], data0=av[:, cols], data1=iv[:, cols],
                                                                          initial=hstate[:, seq, j:j + 1], op0=ALU.mult, op1=ALU.add),
                        [ar, ir, ("hstate", j)], [(hsr, j)])
                    v_copy(hstate[:, seq, j:j + 1], hs3[:, j, (si + 1) * L - 1:(si + 1) * L], [(hsr, j)], [("hstate", j)])

            Wv, wr = W.next(("odd_in", 0))
            conv_unit(0, Wv, wr, lru_consume)
            Wv, wr = W.next(("odd_in", 1))
            for j in range(4):
                bk, br = fm_proj(NT, Wv, wr, j)
                x2, x2r = tmpA[j % 2]
                inn, innr = tmpB[j % 2]
                a_act(x2, bk[:, 0:NT], AF.Square, [br], [x2r])
                v_ts(x2, x2, 0.044715, 1.0, ALU.mult, ALU.add, [x2r], [x2r])
                v_tt(inn, x2, bk[:, 0:NT], ALU.mult, [x2r, br], [innr])
                a_act(inn, inn, AF.Sigmoid, [innr], [innr], scale=2.0 * math.sqrt(2.0 / math.pi))
                v_tt(inn, inn, bk[:, 0:NT], ALU.mult, [innr, br], [innr])
                v_tt(mix3[:, j, :], inn, hs3[:, j, :], ALU.mult, [innr, (hsr, j)], [mixr])
            for si, (seq, LL) in enumerate(tile["segs"]):
                if seq != 0 or tile["last"]:
                    dst = o_lh[0][0:1, :] if seq == 0 else o_lh[1][seq - 1:seq, :]
                    S.dma("pool", lambda e, dst=dst, seq=seq: e.dma_start(out=dst.rearrange("o (j p) -> p (o j)", p=128), in_=hstate[:, seq, :]),
                          None, reads=[("hstate", j) for j in range(4)] + [(hsr, j) for j in range(4)], writes=[])

            mix3, mixr, beta3, betar, loga3, logar, egdk, egdk3, egdkr, egrow3, egr = common()
            conv_alloc(2)
            tmpA = [A.alloc("lA%d" % i, NT, F32) for i in range(2)]
            qF3, qFr = al3("qF", BF16)
            qg3, qgr = al3("qg", BF16)
            kF3, kFr = al3("kF", BF16)
            vF3, vFr = al3("vF", BF16)
            sqt = [A.alloc("sqt%d" % i, NT, BF16) for i in range(2)]

            def qk_consume(isq):
                def f(j, cvv, cvr, sub, nb):
                    a_act(cvv, cvv, AF.Silu, [cvr], [cvr])
                    sv, sr = sqt[j % nb]
                    a_act(sv, cvv, AF.Square, [cvr], [sr])
                    bk, br = P(sub)
                    mm(bk[:, 0:NT], onesb[:], sv, True, True, [sr, "onesb"], [br])
                    yield
                    tv, tvr = tmpA[j % nb]
                    if isq:
                        a_act(tv, bk[:, 0:NT], AF.Ln, [br], [tvr], bias=EPS * 128.0, scale=128.0)
                    else:
                        a_act(tv, bk[:, 0:NT], AF.Ln, [br], [tvr], bias=EPS, scale=1.0)
                    a_act(tv, tv, AF.Exp, [tvr], [tvr], scale=-0.5)
                    yield
                    if isq:
                        v_tt(cvv, cvv, tv, ALU.mult, [cvr, tvr], [cvr])
                        v_copy(qF3[:, j, :], cvv, [cvr], [qFr], eng="pool")
                        v_tt(qg3[:, j, :], cvv, egrow3[:, j, :], ALU.mult, [cvr, egr], [qgr])
                    else:
                        v_tt(kF3[:, j, :], cvv, tv, ALU.mult, [cvr, tvr], [kFr])
                return f

            def v_consume(j, cvv, cvr, sub, nb):
                a_act(vF3[:, j, :], cvv, AF.Silu, [cvr], [vFr])
                yield

            Wv, wr = W.next(("odd_in", 2))
            conv_unit(1, Wv, wr, qk_consume(True))
            Wv, wr = W.next(("odd_in", 3))
            conv_unit(2, Wv, wr, qk_consume(False))
            Wv, wr = W.next(("odd_in", 4))
            conv_unit(3, Wv, wr, v_consume)
            gate3, gater = al3("gate", F32)
            Wdg, wdgr = W.next(("odd_in", 5))
            osb3, osbr = al3("osb", F32)

            def f64(name, n=256):
                v, r = A.alloc(name, n, F32)
                return v, r

            WDN = 1 if tile["kind"] == "s" else 3
            held = set()

            def galloc(n):
                while True:
                    free = [(bank_ctr[0] + k) % 8 for k in range(8) if ((bank_ctr[0] + k) % 8) not in held]
                    if len(free) >= n:
                        out = []
                        for i in free[:n]:
                            held.add(i)
                            out.append((banks[i], ("ps", i)))
                        bank_ctr[0] = free[n - 1] + 1
                        return out
                    yield

            def gfree(*ress):
                for r in ress:
                    held.discard(r[1])
            Xc, Xcr = f64("Xc")
            negX, negXr = f64("negX")
            dgB, dgBr = f64("dgB")
            Gm, Gmr = f64("Gm")
            sets = []
            for i in range(WDN):
                d = {}
                d["DT"] = f64("DT%d" % i)
                d["Bm"] = f64("Bm%d" % i)
                d["Nn"] = [(nmr[:, i, k, :], ("nmr", i, k)) for k in range(2)]
                d["Mm"] = [(nmr[:, i, 2 + k, :], ("nmr", i, 2 + k)) for k in range(2)]
                d["Rr"] = (nmr[:, i, 4, :], ("nmr", i, 4))
                d["QKm"] = A.alloc("QKm%d" % i, 256, BF16)
                d["Yb"] = A.alloc("Yb%d" % i, 256, BF16)
                d["kgt"] = A.alloc("kgt%d" % i, 512, BF16)
                d["kdc"] = A.alloc("kdc%d" % i, 512, BF16)
                d["vtm"] = A.alloc("vtmd%d" % i, 512, BF16)
                d["usb"] = f64("usb%d" % i, 512)
                d["WkT"] = A.alloc("WkT%d" % i, 256, BF16)
                d["wv"] = A.alloc("wv%d" % i, 512, BF16)
                sets.append(d)
            sbf, sbfr = A.alloc("sbfd", 512, BF16)
            ones64 = C64("ones")[:, 0:64]
            id64 = ident[0:64, 0:64]
            dn_done = {}

            def h4(v):
                return v[0:64, 0:256].rearrange("p (h t) -> p h t", h=4)

            def r4(v):
                return v[0:64, 0:256].bitcast(F32R).rearrange("p (h t) -> p h t", h=4)

            def rr(v):
                return v[0:64, 0:256].bitcast(F32R)

            def dn_gen(c):
                d = sets[c % WDN]
                DT, DTr = d["DT"]
                Bm, Bmr = d["Bm"]
                Nn, Mm = d["Nn"], d["Mm"]
                Rr, Rrr = d["Rr"]
                QKm, QKmr = d["QKm"]
                Yb, Ybr = d["Yb"]
                kgt, kgtr = d["kgt"]
                kdc, kdcr = d["kdc"]
                vtm, vtmr = d["vtm"]
                usb, usbr = d["usb"]
                WkT, WkTr = d["WkT"]
                wv_, wvr = d["wv"]
                seq, sgi, sstart, send = seg_of_chunk(tile, c)
                cs = slice(c * 64, (c + 1) * 64)
                X3 = make_X(c, Xc, Xcr)
                act(lambda e: e.mul(out=negX[0:64, :], in_=Xc[0:64, :], mul=-1.0), [Xcr], [negXr])
                v_tt(h4(dgB), C64("i4").rearrange("p (h t) -> p h t", h=4),
                     beta3[0:64, c, :].unsqueeze(2).to_broadcast([64, 4, 64]), ALU.mult, ["c64", betar], [dgBr])
                (bG, bGr), (bK, bKr), (bT, bTr), (bV, bVr) = yield from galloc(4)
                for h in range(4):
                    mm(bG[0:64, h * 128:h * 128 + 64], ones64, X3[:, h, :], True, False, ["c64", Xcr], [bGr])
                    mm(bG[0:64, h * 128:h * 128 + 64], h4(negX)[:, h, :], ones64, False, True, ["c64", negXr], [bGr])
                    mm(bG[0:64, h * 128 + 64:h * 128 + 128], ones64, h4(dgB)[:, h, :], True, True, ["c64", dgBr], [bGr])
                bG3 = bG[0:64, :].rearrange("p (h t) -> p h t", h=4)
                neg3 = C64("neg4").rearrange("p (h t) -> p h t", h=4)
                str3 = C64("strict4").rearrange("p (h t) -> p h t", h=4)
                for h in range(4):
                    mm(bK[0:64, h * 128:h * 128 + 64], kF3[:, h, cs], kF3[:, h, cs], True, True, [kFr], [bKr])
                    mm(bK[0:64, h * 128 + 64:h * 128 + 128], kF3[:, h, cs], qF3[:, h, cs], True, True, [kFr, qFr], [bKr])
                bK3 = bK[0:64, :].rearrange("p (h t) -> p h t", h=4)
                bTb = bT[0:64, 0:256].bitcast(BF16)
                for h in range(4):
                    tr(bTb[:, h * 128:(h + 1) * 128], kF3[:, h, cs], identb[:], [kFr, "identb"], [bTr])
                bT3 = bTb.rearrange("p (h d) -> p h d", h=4)
                bVb = bV[0:64, 0:256].bitcast(BF16)
                for h in range(4):
                    tr(bVb[:, h * 128:(h + 1) * 128], vF3[:, h, cs], identb[:], [vFr, "identb"], [bVr])
                yield
                v_tt(h4(Gm), bG3[:, :, 0:64], neg3, ALU.add, [bGr, "c64"], [Gmr])
                a_act(DT[0:64, :], Gm[0:64, :], AF.Exp, [Gmr], [DTr])
                v_tt(h4(Bm), bG3[:, :, 64:128], str3, ALU.mult, [bGr, "c64"], [Bmr])
                v_tt(kgt[0:64, :].rearrange("p (h d) -> p h d", h=4), bT3,
                     egdk3[0:64, c, 0:4].unsqueeze(2).to_broadcast([64, 4, 128]), ALU.mult, [bTr, egdkr], [kgtr])
                v_tt(kdc[0:64, :].rearrange("p (h d) -> p h d", h=4), bT3,
                     egdk3[0:64, c, 4:8].unsqueeze(2).to_broadcast([64, 4, 128]), ALU.mult, [bTr, egdkr], [kdcr])
                act(lambda e, bVb=bVb: e.copy(out=vtm[0:64, :], in_=bVb), [bVr], [vtmr])
                gfree(bGr, bTr, bVr)
                yield
                v_tt(Bm[0:64, :], Bm[0:64, :], DT[0:64, :], ALU.mult, [Bmr, DTr], [Bmr])
                N0, N0r = Nn[0]
                v_tt(r4(N0), bK3[:, :, 0:64], h4(Bm), ALU.mult, [bKr, Bmr], [N0r])
                v_tt(h4(QKm), bK3[:, :, 64:128], h4(DT), ALU.mult, [bKr, DTr], [QKmr])
                gfree(bKr)
                yield
                M0, M0r = Mm[0]
                ((bt, btr),) = yield from galloc(1)
                for h in range(4):
                    tr(bt[0:64, h * 64:(h + 1) * 64], h4(N0)[:, h, :], id64, [N0r, "c128"], [btr])
                act(lambda e, bt=bt, M0=M0: e.copy(out=rr(M0), in_=bt[0:64, 0:256]), [btr], [M0r])
                gfree(btr)
                v_tt(rr(Rr), C64("i4"), N0[0:64, :], ALU.subtract, [N0r, "c64"], [Rrr])
                yield
                cur = 0
                for stg in range(5):
                    Nc_, Ncr = Nn[cur]
                    Mc_, Mcr = Mm[cur]
                    Nx_, Nxr = Nn[1 - cur]
                    Mx_, Mxr = Mm[1 - cur]
                    last = (stg == 4)
                    if not last:
                        (bN, bNr), (bM, bMr) = yield from galloc(2)
                        for h in range(4):
                            mm(bN[0:64, h * 64:(h + 1) * 64], r4(Mc_)[:, h, :], r4(Nc_)[:, h, :], True, True, [Mcr, Ncr], [bNr])
                    else:
                        ((bM, bMr),) = yield from galloc(1)
                    for h in range(4):
                        mm(bM[0:64, h * 64:(h + 1) * 64], r4(Nc_)[:, h, :], r4(Mc_)[:, h, :], True, True, [Mcr, Ncr], [bMr])
                    yield
                    if not last:
                        v_tt(r4(Nx_), bN[0:64, 0:256].rearrange("p (h t) -> p h t", h=4),
                             C64("ones")[:, 0:64].unsqueeze(1).to_broadcast([64, 4, 64]), ALU.mult, [bNr, "c64"], [Nxr])
                    act(lambda e, bM=bM, Mx_=Mx_: e.copy(out=rr(Mx_), in_=bM[0:64, 0:256]), [bMr], [Mxr])
                    if not last:
                        gfree(bNr)
                    gfree(bMr)
                    ((bR, bRr),) = yield from galloc(1)
                    for h in range(4):
                        mm(bR[0:64, h * 64:(h + 1) * 64], r4(Mx_)[:, h, :], r4(Rr)[:, h, :], True, True, [Mxr, Rrr], [bRr])
                    yield
                    v_tt(rr(Rr), Rr[0:64, :], bR[0:64, 0:256], ALU.add, [Rrr, bRr], [Rrr])
                    gfree(bRr)
                    cur = 1 - cur
                for h in range(4):
                    act(lambda e, h=h, c=c: e.activation(out=h4(Yb)[:, h, :], in_=h4(Rr)[:, h, :], func=AF.Copy, scale=beta3[0:64, c, h:h + 1]),
                        [Rrr, betar], [Ybr])
                yield
                (bU, bUr), (bW, bWr) = yield from galloc(2)
                for h in range(4):
                    mm(bU[0:64, h * 128:(h + 1) * 128], h4(Yb)[:, h, :], vtm[0:64, h * 128:(h + 1) * 128], True, True, [Ybr, vtmr], [bUr])
                act(lambda e, bU=bU: e.copy(out=usb[0:64, :], in_=bU[0:64, :]), [bUr], [usbr])
                for h in range(4):
                    mm(bW[:, h * 64:(h + 1) * 64], kgt[0:64, h * 128:(h + 1) * 128], h4(Yb)[:, h, :], True, True, [kgtr, Ybr], [bWr])
                act(lambda e, bW=bW: e.copy(out=WkT, in_=bW[:, 0:256]), [bWr], [WkTr])
                gfree(bUr, bWr)
                yield
                while c > 0 and not dn_done.get(c - 1):
                    yield
                if sstart:
                    state_io(tile, seq, sstart, send, s_dn, "s_dn", st_dn, o_dn, "start")
                    act(lambda e: e.copy(out=sbf, in_=s_dn[:]), ["s_dn"], [sbfr])
                ((bWS, bWSr),) = yield from galloc(1)
                for h in range(4):
                    mm(bWS[0:64, h * 128:(h + 1) * 128], WkT[:, h * 64:(h + 1) * 64], sbf[:, h * 128:(h + 1) * 128], True, True,
                       [WkTr, sbfr], [bWSr])
                v_tt(wv_[0:64, :], usb[0:64, :], bWS[0:64, :], ALU.subtract, [usbr, bWSr], [wvr])
                gfree(bWSr)
                ((bO, bOr),) = yield from galloc(1)
                for h in range(4):
                    mm(bO[:, h * 64:(h + 1) * 64], wv_[0:64, h * 128:(h + 1) * 128], h4(QKm)[:, h, :], True, False, [wvr, QKmr], [bOr])
                    mm(bO[:, h * 64:(h + 1) * 64], sbf[:, h * 128:(h + 1) * 128], qg3[:, h, cs], False, True, [sbfr, qgr], [bOr])
                act(lambda e, bO=bO, cs=cs: e.copy(out=osb3[:, :, cs], in_=bO[:, 0:256].rearrange("p (h t) -> p h t", h=4)),
                    [bOr], [(osbr, c)])
                gfree(bOr)
                ((bS, bSr),) = yield from galloc(1)
                for h in range(4):
                    mm(bS[:, h * 128:(h + 1) * 128], kdc[0:64, h * 128:(h + 1) * 128], wv_[0:64, h * 128:(h + 1) * 128], True, True,
                       [kdcr, wvr], [bSr])
                for h in range(4):
                    hs_ = slice(h * 128, (h + 1) * 128)
                    v_stt(s_dn[:, hs_], s_dn[:, hs_], egrow3[:, h, c * 64 + 63:c * 64 + 64], bS[:, hs_], ALU.mult, ALU.add,
                          ["s_dn", bSr, egr], ["s_dn"])
                gfree(bSr)
                if send:
                    state_io(tile, seq, sstart, send, s_dn, "s_dn", st_dn, o_dn, "end")
                else:
                    act(lambda e: e.copy(out=sbf, in_=s_dn[:]), ["s_dn"], [sbfr])
                dn_done[c] = True

            def dn_gate():
                def alloc():
                    ((bk, br),) = yield from galloc(1)
                    return bk, br
                g = gate_gen(NT, Wdg, wdgr, gate3, gater, alloc)
                for r in g:
                    if r is not None:
                        gfree(r)
                    yield

            gens = [(lambda c=c: dn_gen(c)) for c in range(NC)]
            interleave(gens, WDN, extra=dn_gate)
            osbr = [(osbr, c) for c in range(NC)]
            cv0, cv0r = cvs["cv"][0]
            head_norm(NT, osb3, osbr, gate3, gater, 2, mix3, mixr, 4,
                      scratch=(A.raw["xh"][:, 0:4 * NT], [(cvs["xhr"], j) for j in range(4)], cv0, cv0r))
            out_proj(tile, "odd_out", mix3, mixr, next_norm=5)

        def final_out(tile):
            NT = tile["NT"]
            A.reset()
            rmsnorm_f32_out(tile)

        def rmsnorm_f32_out(tile):
            NT = tile["NT"]
            sq, sqr = A.alloc("nsq", KT * NT, BF16)
            sq3 = sq.rearrange("p (k t) -> p k t", k=KT)
            yv, yr = A.alloc("yfm", KT * NT, F32)
            y3 = yv.rearrange("p (k t) -> p k t", k=KT)
            act(lambda e: e.activation(out=sq3, in_=x_sb[:, :, 0:NT], func=AF.Square), ["x"], [sqr])
            bk, br = P()
            for kt in range(KT):
                mm(bk[:, 0:NT], onesb[:], sq3[:, kt, :], kt == 0, kt == KT - 1, [sqr, "onesb"], [br])
            a_act(rt[:, 0, 0:NT], bk[:, 0:NT], AF.Ln, [br], ["rt0"], bias=EPS, scale=1.0 / D)
            a_act(rt[:, 1, 0:NT], rt[:, 0, 0:NT], AF.Exp, ["rt0"], ["rt1"], scale=-0.5)
            for kt in range(KT):
                v_stt(y3[:, kt, :], x_sb[:, kt, 0:NT], normw[:, 6, kt:kt + 1], rt[:, 1, 0:NT], ALU.mult, ALU.mult,
                      ["x", "normw", "rt1"], [(yr, kt)])
            dst = yp if tile["kind"] == "p" else ys
            t0 = tile["tok0"] if tile["kind"] == "p" else 0
            for b in range(NT // 128):
                sl = b % 2
                for half in range(2):
                    bk, br = P()
                    for q in range(4):
                        kt = half * 4 + q
                        tr(bk[:, q * 128:(q + 1) * 128], y3[:, kt, b * 128:(b + 1) * 128], ident, [(yr, kt), "c128"], [br])
                    act(lambda e, bk=bk, half=half, sl=sl: e.copy(out=xin[:, sl, half * 512:(half + 1) * 512], in_=bk[:, :]),
                        [br], [("xin", sl)])
                S.dma("pool", lambda e, sl=sl, b=b: e.dma_start(out=dst[t0 + b * 128:t0 + (b + 1) * 128, :], in_=xin[:, sl, :]),
                      "xin%d" % sl, reads=[("xin", sl)], writes=[])

        for tile in tiles:
            t0_ = (tile is tiles[0])
            load_x(tile)
            for mo_ in range(KT):
                prenorm(tile["NT"], 0, mo_)
            if t0_:
                cast_group(2)
            if stage >= 1:
                ffn(tile, 0, 0)
            if t0_:
                cast_group(3)
            if stage >= 2:
                even_mixer(tile)
            if t0_:
                cast_group(4)
            if stage >= 3:
                ffn(tile, 1, 0, next_norm=1)
            if t0_:
                cast_group(5)
            if stage >= 3:
                ffn(tile, 0, 1)
            if stage >= 4:
                odd_mixer(tile)
            if stage >= 5:
                ffn(tile, 1, 1)
            final_out(tile)
        assert W.consumed == len(W.units)
        S.wait_deps("pool", [v for k, v in S.dma_last.items() if not (k.startswith("w") or k.startswith("cast"))])

        S.emit({"pe": block.tensor, "act": block.scalar, "dve": block.vector, "pool": block.gpsimd, "sp": block.sync},
               eng_sems, dma_sems)
    return nc


_PROG_CACHE = {}


def kernel(x_prompt, x_sample, state_ret, state_hgrn, state_lru_h, state_lru_conv, state_dn, state_dn_conv,
           ffn1_norm, ffn1_w_in, ffn1_w_out, mix_norm, ffn2_norm, ffn2_w_in, ffn2_w_out, final_norm,
           even_w_in, even_w_out, ret_out_norm, hg_out_norm, hg_lb_logits, odd_w_in, odd_w_out,
           lru_conv_w, lru_conv_b, lru_w_a, lru_b_a, lru_w_x, lru_b_x, lru_lambda, dn_conv_w, dn_a_log,
           dn_dt_bias, dn_out_norm, _past_len=2048, _stage=99, _ncores=NCORES):
    f = lambda a: np.ascontiguousarray(np.asarray(a, dtype=np.float32))
    x_prompt = f(x_prompt)
    x_sample = f(x_sample)
    B, TP, _ = x_prompt.shape
    assert B == 4 and x_sample.shape[0] == 16 and x_sample.shape[1] == 64
    hc = host_consts(TP, _past_len)
    if (TP, _stage) not in _PROG_CACHE:
        _PROG_CACHE[(TP, _stage)] = build_program(TP, _stage)
    nc = _PROG_CACHE[(TP, _stage)]
    shared = {
        "ffn1_w_in": f(ffn1_w_in), "ffn2_w_in": f(ffn2_w_in), "ffn1_w_out": f(ffn1_w_out), "ffn2_w_out": f(ffn2_w_out),
        "even_w_in": f(even_w_in)[0], "even_w_out": f(even_w_out)[0], "odd_w_in": f(odd_w_in)[0], "odd_w_out": f(odd_w_out)[0],
        "norms": np.ascontiguousarray(np.concatenate([f(ffn1_norm), f(mix_norm), f(ffn2_norm), f(final_norm)[None]], 0)),
        "hnorm": np.ascontiguousarray(np.concatenate([f(ret_out_norm), f(hg_out_norm), f(dn_out_norm)], 0)),
        "hg_lb_logits": f(hg_lb_logits),
        "lru_vecs": np.ascontiguousarray(np.concatenate([f(lru_conv_w)[0], f(lru_conv_b), f(lru_b_a), f(lru_b_x), f(lru_lambda)], 0)),
        "lru_w_a": f(lru_w_a)[0], "lru_w_x": f(lru_w_x)[0],
        "dn_conv_w": f(dn_conv_w)[0],
        "dn_scal": np.ascontiguousarray(np.concatenate([f(dn_a_log), f(dn_dt_bias)], 0)),
        "c128": hc["c128"], "c64": hc["c64"], "rope": hc["rope"],
    }
    sr, sh, sd = f(state_ret)[0], f(state_hgrn)[0], f(state_dn)[0]
    slh, slc, sdc = f(state_lru_h)[0], f(state_lru_conv)[0], f(state_dn_conv)[0]
    in_maps = []
    zero_prompt = np.zeros_like(x_prompt[0])
    for c in range(NCORES):
        m = dict(shared)
        m["xp"] = x_prompt[PROMPT_OF_CORE[c]] if PROMPT_OF_CORE[c] is not None else zero_prompt
        m["xs"] = np.ascontiguousarray(x_sample[2 * c:2 * c + 2].reshape(128, D))
        m["st_ret"] = np.ascontiguousarray(sr[2 * c:2 * c + 2])
        m["st_hg"] = np.ascontiguousarray(sh[2 * c:2 * c + 2])
        m["st_dn"] = np.ascontiguousarray(sd[2 * c:2 * c + 2])
        m["st_lh"] = np.ascontiguousarray(slh[2 * c:2 * c + 2])
        m["st_lc"] = np.ascontiguousarray(slc[2 * c:2 * c + 2])
        m["st_dc"] = np.ascontiguousarray(sdc[2 * c:2 * c + 2])
        in_maps.append(m)
    res = run_bass_kernel_spmd(nc, in_maps[:_ncores], core_ids=list(range(_ncores)))
    R = list(res.results)
    while len(R) < NCORES:
        R.append(R[0])
    y_prompt = np.stack([R[c]["yp"] for c in CORE_OF_PROMPT], 0)
    y_sample = np.concatenate([R[c]["ys"].reshape(2, 64, D) for c in range(NCORES)], 0)

    def gp(name, shape):
        return np.stack([R[c][name].reshape(shape) for c in CORE_OF_PROMPT], 0)[None]

    def gs(name, shape):
        return np.concatenate([R[c][name].reshape((2,) + shape) for c in range(NCORES)], 0)[None]

    return (y_prompt, y_sample,
            gp("ret_p", (4, 128, 128)), gs("ret_s", (4, 128, 128)),
            gp("hg_p", (4, 128, 128)), gs("hg_s", (4, 128, 128)),
            gp("lh_p", (512,)), gs("lh_s", (512,)),
            gp("lc_p", (3, 512)), gs("lc_s", (3, 512)),
            gp("dn_p", (4, 128, 128)), gs("dn_s", (4, 128, 128)),
            gp("dc_p", (3, 1536)), gs("dc_s", (3, 1536)))
```

```python
import contextlib
import math
import os
DBG = int(os.environ.get('KDBG', '99'))
import numpy as np
import concourse.bass as bass
import concourse.mybir as mybir
from concourse.bass_utils import run_bass_kernel_spmd

F32 = mybir.dt.float32
BF16 = mybir.dt.bfloat16
F32R = mybir.dt.float32r
AF = mybir.ActivationFunctionType
ALU = mybir.AluOpType

D = 1024
KT = 8
FF = 2816
FT = 22
EPS = 1e-6
LRU_C = 8.0
NCORES = 8
PROMPT_OF_CORE = [0, 1, None, None, 2, 3, None, None]
CORE_OF_PROMPT = [0, 1, 4, 5]


class Op:
    __slots__ = ("eng", "fn", "waits", "signal", "idx", "dma_sem", "sig_count")

    def __init__(self, eng, fn):
        self.eng = eng
        self.fn = fn
        self.waits = []
        self.signal = False
        self.dma_sem = None
        self.sig_count = 0
        self.idx = 0


class Sched:
    def __init__(self):
        self.ops = {e: [] for e in ("pe", "act", "dve", "pool", "sp")}
        self.last_write = {}
        self.readers = {}
        self.waited = {e: {} for e in self.ops}
        self.dma_counts = {}
        self.dma_last = {}
        self.misc_ctr = {}
        self.inherit = {}

    def _need(self, op, dep, is_dma=False):
        if dep is None:
            return
        kind, key, val = dep
        if kind == "eng" and key == op.eng and key == "pe" and not is_dma:
            return
        w = self.waited[op.eng]
        k = (kind, key)
        if w.get(k, -1) >= val:
            return
        w[k] = val
        op.waits.append(dep)

    def _deps(self, op, reads, writes, is_dma=False):
        if self.inherit:
            for r in list(reads) + list(writes):
                for base in ((r, r[0]) if (isinstance(r, tuple) and len(r) == 2 and isinstance(r[0], tuple)) else (r,)):
                    for d in self.inherit.get(base, ()):
                        self._need(op, d, is_dma)
        for r in reads:
            self._need(op, self.last_write.get(r), is_dma)
        for r in writes:
            self._need(op, self.last_write.get(r), is_dma)
            for d in self.readers.get(r, ()):
                self._need(op, d, is_dma)

    def _commit(self, dep, reads, writes):
        for r in reads:
            self.readers.setdefault(r, []).append(dep)
        for r in writes:
            self.last_write[r] = dep
            self.readers[r] = []

    def op(self, eng, fn, reads=(), writes=()):
        psr = [r for r in reads if isinstance(r, tuple) and r[0] == "ps"]
        if psr:
            writes = list(writes) + [r for r in psr if r not in writes]
        o = Op(eng, fn)
        o.idx = len(self.ops[eng])
        self._deps(o, reads, writes)
        self.ops[eng].append(o)
        self._commit(("eng", eng, o.idx), reads, writes)
        return o

    NMISC = 24

    def dma(self, eng, fn, sem=None, reads=(), writes=()):
        o = Op(eng, fn)
        o.idx = len(self.ops[eng])
        if sem is None:
            mc = self.misc_ctr.get(eng, 0)
            sem = "m%s%d" % (eng, mc % self.NMISC)
            self.misc_ctr[eng] = mc + 1
        if not sem.startswith("cast") and not sem.startswith("w"):
            self._need(o, self.dma_last.get(sem), True)
        self._deps(o, reads, writes, True)
        c = self.dma_counts.get(sem, 0) + 1
        self.dma_counts[sem] = c
        o.dma_sem = sem
        self.ops[eng].append(o)
        dep = ("dma", sem, 16 * c)
        self.dma_last[sem] = dep
        self._commit(dep, reads, writes)
        return o

    def barrier(self):
        lasts = []
        for e in ("pe", "act", "dve", "pool"):
            if self.ops[e]:
                for o in reversed(self.ops[e]):
                    if o.fn is not None and o.dma_sem is None:
                        lasts.append(("eng", e, o.idx))
                        break
        dmas = [v for k, v in self.dma_last.items() if not (k.startswith('w') or k.startswith('cast'))]
        for e in ("pe", "act", "dve", "pool"):
            o = Op(e, None)
            o.idx = len(self.ops[e])
            for d in lasts + dmas:
                self._need(o, d)
            self.ops[e].append(o)

    def wait_deps(self, eng, deps):
        o = Op(eng, None)
        o.idx = len(self.ops[eng])
        for d in deps:
            self._need(o, d, True)
        self.ops[eng].append(o)

    def finalize(self):
        for e, lst in self.ops.items():
            for o in lst:
                for kind, key, val in o.waits:
                    if kind == "eng":
                        self.ops[key][val].signal = True
        for e, lst in self.ops.items():
            c = 0
            for o in lst:
                if o.signal:
                    c += 1
                o.sig_count = c

    def emit(self, regs, eng_sems, dma_sems):
        self.finalize()
        for e, reg in regs.items():
            lst = self.ops[e]
            if not lst:
                continue

            def body(engine, lst=lst, e=e):
                for o in lst:
                    for kind, key, val in o.waits:
                        if kind == "eng":
                            engine.wait_ge(eng_sems[key], self.ops[key][val].sig_count)
                        else:
                            engine.wait_ge(dma_sems[key], val)
                    if o.fn is None:
                        continue
                    ins = o.fn(engine)
                    if o.dma_sem is not None:
                        ins.then_inc(dma_sems[o.dma_sem], 16)
                    elif o.signal:
                        ins.then_inc(eng_sems[e], 1)

            reg(body)


def host_consts(TP, past_len):
    c = {}
    g = np.array([np.log1p(-2.0 ** (-5.0 - h)) for h in range(4)], np.float64)
    p = np.arange(64, dtype=np.float64)
    c128 = {}
    c128["ident"] = np.eye(128)
    rm = np.zeros((128, 128))
    for d in range(64):
        rm[d + 64, d] = -1.0
        rm[d, d + 64] = 1.0
    c128["rmat"] = rm
    qd = np.stack([(128.0 ** -0.5) * np.exp(g[h] * (p + 1.0)) for h in range(4)], 0)
    c128["qdec"] = np.broadcast_to(qd.reshape(1, 256), (128, 256))
    rs = np.ones(512)
    rs[0::64] = 0.0
    c128["reset"] = np.broadcast_to(rs.reshape(1, 512), (128, 512))
    c128["ones"] = np.ones((128, 128))
    names128 = ["ident", "rmat", "qdec", "reset", "ones"]
    c["c128"] = np.concatenate([np.asarray(c128[k], np.float64) for k in names128], 1).astype(np.float32)
    off = 0
    c["off128"] = {}
    for k in names128:
        c["off128"][k] = (off, c128[k].shape[1])
        off += c128[k].shape[1]
    c64 = {}
    s = p.reshape(64, 1)
    t = p.reshape(1, 64)
    c64["retmask"] = np.concatenate([np.exp(g[h] * (np.abs(t - s) - (t + 1.0))) for h in range(4)], 1)
    c64["kdec"] = np.concatenate([np.broadcast_to(np.exp(g[h] * (63.0 - s)), (64, 128)) for h in range(4)], 1)
    incl = (s <= t).astype(np.float64)
    strict = (s < t).astype(np.float64)
    c64["incl4"] = np.tile(incl, (1, 4))
    c64["neg4"] = np.tile((1.0 - incl) * -30000.0, (1, 4))
    c64["strict4"] = np.tile(strict, (1, 4))
    c64["i4"] = np.tile(np.eye(64), (1, 4))
    c64["U"] = incl
    c64["Urev"] = (s > t).astype(np.float64)
    c64["ones"] = np.ones((64, 128))
    names64 = ["retmask", "kdec", "incl4", "neg4", "strict4", "i4", "U", "Urev", "ones"]
    c["c64"] = np.concatenate([c64[k] for k in names64], 1).astype(np.float32)
    off = 0
    c["off64"] = {}
    for k in names64:
        c["off64"][k] = (off, c64[k].shape[1])
        off += c64[k].shape[1]
    c["sdec"] = [float(np.exp(g[h] * 64.0)) for h in range(4)]
    pos = np.concatenate([np.arange(TP), past_len + np.arange(64), past_len + np.arange(64)]).astype(np.float64)
    half = 64
    freq = 10000.0 ** (-np.arange(half, dtype=np.float64) / half)
    ang = (pos.astype(np.float32)[None, :] * freq.astype(np.float32)[:, None]).astype(np.float32).astype(np.float64)
    cos = np.cos(ang)
    sin = np.sin(ang)
    tab = np.zeros((128, 2, TP + 128), np.float32)
    tab[0:64, 0] = cos
    tab[64:128, 0] = cos
    tab[0:64, 1] = sin
    tab[64:128, 1] = sin
    c["rope"] = tab
    return c


def build_program(TP, stage=99):
    assert TP % 512 == 0
    hc = host_consts(TP, 0)
    off128, off64, SDEC = hc["off128"], hc["off64"], hc["sdec"]
    C128W = hc["c128"].shape[1]
    C64W = hc["c64"].shape[1]

    nc = bass.Bass("TRN2", target_bir_lowering=False)

    def din(name, shape, dt=F32):
        return nc.dram_tensor(name, list(shape), dt, kind="ExternalInput").ap()

    def dout(name, shape):
        return nc.dram_tensor(name, list(shape), F32, kind="ExternalOutput").ap()

    def dscr(name, shape, dt):
        return nc.dram_tensor(name, list(shape), dt, kind="Internal").ap()

    xp = din("xp", [TP, D])
    xs = din("xs", [128, D])
    st_ret = din("st_ret", [2, 4, 128, 128])
    st_hg = din("st_hg", [2, 4, 128, 128])
    st_dn = din("st_dn", [2, 4, 128, 128])
    st_lh = din("st_lh", [2, 512])
    st_lc = din("st_lc", [2, 3, 512])
    st_dc = din("st_dc", [2, 3, 1536])
    w_ffn_in = [din("ffn1_w_in", [2, D, 2 * FF]), din("ffn2_w_in", [2, D, 2 * FF])]
    w_ffn_out = [din("ffn1_w_out", [2, FF, D]), din("ffn2_w_out", [2, FF, D])]
    w_even_in = din("even_w_in", [D, 4096])
    w_even_out = din("even_w_out", [D, D])
    w_odd_in = din("odd_w_in", [D, 3080])
    w_odd_out = din("odd_w_out", [D, D])
    norms_d = din("norms", [7, D])
    hnorm_d = din("hnorm", [3, 512])
    lb_logits = din("hg_lb_logits", [2, 512])
    lru_vecs = din("lru_vecs", [8, 512])
    lru_wa = din("lru_w_a", [8, 64, 64])
    lru_wx = din("lru_w_x", [8, 64, 64])
    dn_conv_w = din("dn_conv_w", [4, 1536])
    dn_scal = din("dn_scal", [2, 4])
    c128_d = din("c128", [128, C128W])
    c64_d = din("c64", [64, C64W])
    rope_d = din("rope", [128, 2, TP + 128])

    yp = dout("yp", [TP, D])
    ys = dout("ys", [128, D])
    o_ret = [dout("ret_p", [4, 128, 128]), dout("ret_s", [2, 4, 128, 128])]
    o_hg = [dout("hg_p", [4, 128, 128]), dout("hg_s", [2, 4, 128, 128])]
    o_dn = [dout("dn_p", [4, 128, 128]), dout("dn_s", [2, 4, 128, 128])]
    o_lh = [dout("lh_p", [1, 512]), dout("lh_s", [2, 512])]
    o_lc = [dout("lc_p", [1, 3, 512]), dout("lc_s", [2, 3, 512])]
    o_dc = [dout("dc_p", [1, 3, 1536]), dout("dc_s", [2, 3, 1536])]

    wb_ffn_in = [dscr("b_ffn1_w_in", [2, D, 2 * FF], BF16), dscr("b_ffn2_w_in", [2, D, 2 * FF], BF16)]
    wb_ffn_out = [dscr("b_ffn1_w_out", [2, FF, D], BF16), dscr("b_ffn2_w_out", [2, FF, D], BF16)]
    wb_even_in = dscr("b_even_w_in", [D, 4096], BF16)
    wb_even_out = dscr("b_even_w_out", [D, D], BF16)
    wb_odd_in = dscr("b_odd_w_in", [D, 3080], BF16)
    wb_odd_out = dscr("b_odd_w_out", [D, D], BF16)

    S = Sched()
    es = contextlib.ExitStack()
    with es:
        def sb(name, shape, dt):
            return es.enter_context(nc.sbuf_tensor("sb_" + name, list(shape), dt))

        x_sb = sb("x_sb", [128, KT, 512], F32)
        hn = sb("hn", [128, KT, 512], BF16)
        NSLOT = 3
        SLOTSZ = FT * 256
        wring = sb("wring", [128, NSLOT, SLOTSZ], BF16)
        xin = sb("xin", [128, 2, D], F32)
        c128 = sb("c128", [128, C128W], F32)
        c64 = sb("c64", [64, C64W], F32)
        identb = sb("identb", [128, 128], BF16)
        rmatb = sb("rmatb", [128, 128], BF16)
        onesb = sb("onesb", [128, 128], BF16)
        normw = sb("normw", [128, 7, KT], F32)
        hgain = sb("hgain", [128, 3, 4], F32)
        lbt = sb("lbt", [128, 4, 4], F32)
        lruv = sb("lruv", [128, 8, 4], F32)
        lrud = sb("lrud", [128, 4, 4], F32)
        dncw = sb("dncw", [128, 4, 12], F32)
        bd = sb("bd", [128, 2, 4, 128], BF16)
        dnb = sb("dnb", [64, 2, 8, 4], F32)
        dnraw = sb("dnraw", [64, 2, 4], F32)
        s_ret = sb("s_ret", [128, 512], F32)
        s_hg = sb("s_hg", [128, 512], F32)
        s_dn = sb("s_dn", [128, 512], F32)
        hstate = sb("hstate", [128, 3, 4], F32)
        hcar = sb("hcar", [128, 16, 3], F32)
        rt = sb("rt", [128, 2, 512], F32)
        ARENA = 51800
        nmr = sb("nmr", [64, 3, 5, 256], F32)
        arena = sb("arena", [128, ARENA], BF16)

        banks = [es.enter_context(nc.psum_tensor("ps%d" % i, [128, 512], F32)) for i in range(8)]
        eng_sems = {e: es.enter_context(nc.semaphore("sem_" + e)) for e in ("pe", "act", "dve", "pool")}
        dma_names = ["w%d" % i for i in range(NSLOT)] + ["xin0", "xin1"] + ["cast%d" % i for i in range(6)] + ["m%s%d" % (q, i) for q in ("sp", "pool") for i in range(Sched.NMISC)]
        dma_sems = {k: es.enter_context(nc.semaphore("dsem_" + k)) for k in dma_names}
        block = es.enter_context(nc.Block())
        es.enter_context(nc.allow_non_contiguous_dma(reason="tiny per-channel vectors"))

        def C128(k):
            o, n = off128[k]
            return c128[:, o:o + n]

        def C64(k):
            o, n = off64[k]
            return c64[:, o:o + n]

        ident = C128("ident")

        bank_ctr = [0]

        sub_ctr = {}

        def P(sub=None):
            if sub is None:
                i = bank_ctr[0] % 8
                bank_ctr[0] += 1
                return banks[i], ("ps", i)
            k, n = sub
            mine = [b for b in range(8) if b % n == k]
            j = sub_ctr.get(sub, 0)
            sub_ctr[sub] = j + 1
            i = mine[j % len(mine)]
            return banks[i], ("ps", i)

        class Arena:
            def __init__(self):
                self.off = 0
                self.gen = 0
                self.raw = {}
                self.live = []

            def reset(self):
                self.off = 0
                self.gen += 1

            def _inherit(self, lo, hi, newres):
                deps = []
                keep = []
                for (a, b, r) in self.live:
                    if r[1] == self.gen or b <= lo or a >= hi:
                        keep.append((a, b, r))
                        continue
                    for k, d in list(S.last_write.items()):
                        if k == r or (isinstance(k, tuple) and len(k) == 2 and k[0] == r):
                            if d is not None:
                                deps.append(d)
                            deps.extend(S.readers.get(k, ()))
                    if a < lo:
                        keep.append((a, lo, r))
                    if b > hi:
                        keep.append((hi, b, r))
                self.live = keep
                if deps:
                    S.inherit.setdefault(newres, []).extend(deps)

            def alloc(self, name, n, dt):
                if dt == F32:
                    ne = 2 * n
                else:
                    ne = n
                ne = (ne + 15) // 16 * 16
                assert self.off + ne <= ARENA, (name, self.off, ne)
                v = arena[:, self.off:self.off + ne]
                res = ("ar", self.gen, name)
                self._inherit(self.off, self.off + ne, res)
                self.live.append((self.off, self.off + ne, res))
                self.raw[name] = v
                self.off += ne
                if dt == F32:
                    v = v.bitcast(F32)[:, 0:n]
                else:
                    v = v[:, 0:n]
                return v, res

        A = Arena()

        def pe(fn, r=(), w=()):
            return S.op("pe", fn, r, w)

        def act(fn, r=(), w=()):
            return S.op("act", fn, r, w)

        def dve(fn, r=(), w=()):
            return S.op("dve", fn, r, w)

        def pool(fn, r=(), w=()):
            return S.op("pool", fn, r, w)

        def mm(out, lhsT, rhs, start, stop, r, w):
            return pe(lambda e: e.matmul(out, lhsT=lhsT, rhs=rhs, start=start, stop=stop), r, w)

        def tr(out, in_, idn, r, w):
            return pe(lambda e: e.transpose(out=out, in_=in_, identity=idn), r, w)

        def a_act(out, in_, func, r, w, bias=None, scale=None):
            kw = {}
            if bias is not None:
                kw["bias"] = bias
            if scale is not None:
                kw["scale"] = scale
            return act(lambda e: e.activation(out=out, in_=in_, func=func, **kw), r, w)

        def v_tt(out, in0, in1, op, r, w, eng="dve"):
            return S.op(eng, lambda e: e.tensor_tensor(out=out, in0=in0, in1=in1, op=op), r, w)

        def v_ts(out, in0, s1, s2, op0, op1, r, w, eng="dve"):
            if op1 is None:
                return S.op(eng, lambda e: e.tensor_scalar(out=out, in0=in0, scalar1=s1, scalar2=None, op0=op0), r, w)
            return S.op(eng, lambda e: e.tensor_scalar(out=out, in0=in0, scalar1=s1, scalar2=s2, op0=op0, op1=op1), r, w)

        def v_stt(out, in0, sc, in1, op0, op1, r, w, eng="dve"):
            return S.op(eng, lambda e: e.scalar_tensor_tensor(out=out, in0=in0, scalar=sc, in1=in1, op0=op0, op1=op1), r, w)

        def v_copy(out, in_, r, w, eng="dve"):
            return S.op(eng, lambda e: e.tensor_copy(out=out, in_=in_), r, w)

        def v_memset(ap, val, w, eng="dve"):
            return S.op(eng, lambda e: e.memset(ap, val), (), w)

        class WStream:
            def __init__(self):
                self.units = []
                self.issued = 0
                self.consumed = 0

            def plan(self, tag, src, shape):
                self.units.append((tag, src, shape))

            @staticmethod
            def group_of(tag):
                if tag[0].startswith("ffn"):
                    which, layer = tag[1], tag[2]
                    return {(0, 0): 0, (1, 0): 2, (0, 1): 3, (1, 1): 5}[(which, layer)]
                return 1 if tag[0].startswith("even") else 4

            def _issue(self):
                i = self.issued
                tag, src, shape = self.units[i]
                S.wait_deps("sp", [cast_dep[self.group_of(tag)]])
                slot = i % NSLOT
                n = 1
                for d in shape:
                    n *= d
                dst = wring[:, slot, 0:n]
                if len(shape) == 2:
                    dst = dst.rearrange("p (a b) -> p a b", a=shape[0])
                srcs = src if isinstance(src, list) else [src]
                if len(srcs) == 1:
                    pairs = [(dst, srcs[0])]
                else:
                    g = len(srcs)
                    d4 = dst.rearrange("p a (g c) -> p a g c", g=g)
                    pairs = [(d4[:, :, k, :], srcs[k]) for k in range(g)]
                for dd, ss in pairs:
                    S.dma("sp", lambda e, dd=dd, ss=ss: e.dma_start(out=dd, in_=ss), "w%d" % slot,
                          reads=[], writes=[("wslot", slot)])
                self.issued += 1

            def next(self, tag):
                while self.issued < min(len(self.units), self.consumed + NSLOT):
                    self._issue()
                i = self.consumed
                t, src, shape = self.units[i]
                assert t == tag, (t, tag)
                slot = i % NSLOT
                n = 1
                for d in shape:
                    n *= d
                v = wring[:, slot, 0:n]
                if len(shape) == 2:
                    v = v.rearrange("p (a b) -> p a b", a=shape[0])
                self.consumed += 1
                return v, ("wslot", slot)

        W = WStream()

        def in_view(wap, c0, ncol):
            return wap.rearrange("(kt p) c -> p kt c", p=128)[:, :, c0:c0 + ncol]

        if TP >= 2048:
            sizes = [512] * (TP // 512 - 2) + [384, 384, 256]
        else:
            sizes = [TP // 2, TP // 2]
        assert sum(sizes) == TP and all(z % 128 == 0 for z in sizes) and sizes[-1] <= 384
        tiles = []
        off = 0
        for i, sz in enumerate(sizes):
            lastp = (i == len(sizes) - 1)
            if not lastp:
                segs = [dict(seq=0, L=sz, start=(i == 0), end=False, prev_same=False)]
                NT_ = sz
            else:
                npc = sz // 64
                segs = [dict(seq=0, L=64, start=(i == 0 and k == 0), end=(k == npc - 1), prev_same=(k > 0)) for k in range(npc)]
                segs += [dict(seq=1, L=64, start=True, end=True, prev_same=False), dict(seq=2, L=64, start=True, end=True, prev_same=False)]
                NT_ = sz + 128
            tiles.append(dict(kind=("m" if lastp else "p"), tok0=off, ptok=sz, NT=NT_, segs=segs, first=(i == 0), last=lastp))
            off += sz

        def plan_ffn(which, layer):
            wi = wb_ffn_in[which][layer]
            wo = wb_ffn_out[which][layer]
            for u in range(11):
                w4 = wi.rearrange("(kt p) (g c) -> p kt g c", p=128, g=2)
                src = [w4[:, :, 0, u * 256:(u + 1) * 256], w4[:, :, 1, u * 256:(u + 1) * 256]]
                W.plan(("ffn_in", which, layer, u), src, (KT, 512))
            for u in range(4):
                src = wo.rearrange("(kt p) c -> p kt c", p=128)[:, :, u * 256:(u + 1) * 256]
                W.plan(("ffn_out", which, layer, u), src, (FT, 256))

        def plan_tile():
            if stage >= 1:
                plan_ffn(0, 0)
            if stage >= 2:
                for u in range(8):
                    W.plan(("even_in", u), in_view(wb_even_in, u * 512, 512), (KT, 512))
                for u in range(4):
                    W.plan(("even_out", u), in_view(wb_even_out, u * 256, 256), (KT, 256))
            if stage >= 3:
                plan_ffn(1, 0)
                plan_ffn(0, 1)
            if stage >= 4:
                W.plan(("odd_in", "bda"), in_view(wb_odd_in, 3072, 8), (KT, 8))
                for u in range(6):
                    W.plan(("odd_in", u), in_view(wb_odd_in, u * 512, 512), (KT, 512))
                for u in range(4):
                    W.plan(("odd_out", u), in_view(wb_odd_out, u * 256, 256), (KT, 256))
            if stage >= 5:
                plan_ffn(1, 1)

        for _ in tiles:
            plan_tile()

        cast_dep = {}

        def cast_w(src2d, dst2d, rows, grp):
            r0 = 0
            while r0 < rows:
                rr = min(256, rows - r0)
                S.dma("pool", lambda e, a=dst2d[r0:r0 + rr, :], b=src2d[r0:r0 + rr, :]: e.dma_start(out=a, in_=b),
                      "cast%d" % grp, reads=[], writes=[("wcast", grp)])
                r0 += rr
            cast_dep[grp] = S.dma_last["cast%d" % grp]

        def cast_group(grp):
            if grp == 0:
                cast_w(w_ffn_in[0][0], wb_ffn_in[0][0], D, 0)
                cast_w(w_ffn_out[0][0], wb_ffn_out[0][0], FF, 0)
            elif grp == 1:
                cast_w(w_even_in, wb_even_in, D, 1)
                cast_w(w_even_out, wb_even_out, D, 1)
            elif grp == 2:
                cast_w(w_ffn_in[1][0], wb_ffn_in[1][0], D, 2)
                cast_w(w_ffn_out[1][0], wb_ffn_out[1][0], FF, 2)
            elif grp == 3:
                cast_w(w_ffn_in[0][1], wb_ffn_in[0][1], D, 3)
                cast_w(w_ffn_out[0][1], wb_ffn_out[0][1], FF, 3)
            elif grp == 4:
                cast_w(w_odd_in, wb_odd_in, D, 4)
                cast_w(w_odd_out, wb_odd_out, D, 4)
            elif grp == 5:
                cast_w(w_ffn_in[1][1], wb_ffn_in[1][1], D, 5)
                cast_w(w_ffn_out[1][1], wb_ffn_out[1][1], FF, 5)

        S.dma("sp", lambda e: e.dma_start(out=c128[:], in_=c128_d), None, writes=["c128"])
        S.dma("sp", lambda e: e.dma_start(out=c64[:], in_=c64_d), None, writes=["c64"])
        S.dma("sp", lambda e: e.dma_start(out=normw[:], in_=norms_d.rearrange("n (kt p) -> p n kt", p=128)), None, writes=["normw"])
        S.dma("sp", lambda e: e.dma_start(out=hgain[:], in_=hnorm_d.rearrange("n (h p) -> p n h", p=128)), None, writes=["hgain"])
        S.dma("sp", lambda e: e.dma_start(out=lbt[:, 0:2, :], in_=lb_logits.rearrange("n (h p) -> p n h", p=128)), None, writes=["lbt"])
        S.dma("sp", lambda e: e.dma_start(out=lruv[:], in_=lru_vecs.rearrange("n (j p) -> p n j", p=128)), None, writes=["lruv"])
        S.dma("sp", lambda e: e.dma_start(out=dncw[:], in_=dn_conv_w.rearrange("n (j p) -> p n j", p=128)), None, writes=["dncw"])
        S.dma("sp", lambda e: e.dma_start(out=dnraw[:], in_=dn_scal.rearrange("(o n) h -> o n h", o=1).to_broadcast([64, 2, 4])), None, writes=["dnraw"])
        bdst_v, _ = A.alloc("bdst", 2 * 4 * 128, F32)
        bdst = bdst_v.rearrange("p (g j c) -> p g j c", g=2, j=4)
        pool(lambda e: e.memset(bdst_v, 0.0), (), ["bdst"])
        for gi, wsrc in enumerate((lru_wa, lru_wx)):
            for n in range(8):
                j, hh = n // 2, n % 2
                S.dma("sp", lambda e, gi=gi, n=n, j=j, hh=hh, wsrc=wsrc: e.dma_start(
                    out=bdst[hh * 64:(hh + 1) * 64, gi, j, hh * 64:(hh + 1) * 64], in_=wsrc[n]), None,
                    reads=[], writes=["bdst"])
        cast_group(0)
        cast_group(1)

        v_copy(identb[:], ident, ["c128"], ["identb"])
        v_copy(rmatb[:], C128("rmat"), ["c128"], ["rmatb"])
        v_copy(onesb[:], C128("ones"), ["c128"], ["onesb"])
        v_copy(bd[:], bdst, ["bdst"], ["bd"])
        v_tt(lbt[:, 2, :], lbt[:, 0, :], lbt[:, 1, :], ALU.subtract, ["lbt"], ["lbt"])
        a_act(lbt[:, 2, :], lbt[:, 2, :], AF.Sigmoid, ["lbt"], ["lbt"])
        v_ts(lbt[:, 3, :], lbt[:, 2, :], -1.0, 1.0, ALU.mult, ALU.add, ["lbt"], ["lbt"])
        a_act(lrud[:, 2, :], lruv[:, 7, :], AF.Exp, ["lruv"], ["lrud"], scale=-1.0)
        a_act(lrud[:, 3, :], lrud[:, 2, :], AF.Ln, ["lrud"], ["lrud"], bias=1.0)
        v_ts(lrud[:, 0, :], lrud[:, 3, :], -LRU_C, None, ALU.mult, None, ["lrud"], ["lrud"])
        v_ts(lrud[:, 1, :], lrud[:, 3, :], -2.0 * LRU_C, None, ALU.mult, None, ["lrud"], ["lrud"])
        a_act(dnraw[:, 0, :], dnraw[:, 0, :], AF.Exp, ["dnraw"], ["dnraw"])
        v_ts(dnraw[:, 0, :], dnraw[:, 0, :], -1.0, None, ALU.mult, None, ["dnraw"], ["dnraw"])
        for cc in range(8):
            v_copy(dnb[:, 0, cc, :], dnraw[:, 1, :], ["dnraw"], ["dnb"])
            v_copy(dnb[:, 1, cc, :], dnraw[:, 0, :], ["dnraw"], ["dnb"])

        def interleave(gen_fns, width, extra=None):
            active = []
            nxt = 0
            ex = extra() if extra is not None else None
            while active or nxt < len(gen_fns) or ex is not None:
                while len(active) < width and nxt < len(gen_fns):
                    active.append(gen_fns[nxt]())
                    nxt += 1
                for g in list(active):
                    try:
                        next(g)
                    except StopIteration:
                        active.remove(g)
                if ex is not None:
                    try:
                        next(ex)
                    except StopIteration:
                        ex = None

        def load_x(tile):
            NT = tile["NT"]
            for b in range(NT // 128):
                sl = b % 2
                if b * 128 < tile["ptok"]:
                    srcb = xp[tile["tok0"] + b * 128:tile["tok0"] + (b + 1) * 128, :]
                else:
                    srcb = xs[b * 128 - tile["ptok"]:(b + 1) * 128 - tile["ptok"], :]
                S.dma("pool", lambda e, sl=sl, srcb=srcb: e.dma_start(out=xin[:, sl, :], in_=srcb),
                      "xin%d" % sl, reads=[], writes=[("xin", sl)])
                for half in range(2):
                    bk, br = P()
                    for q in range(4):
                        kt = half * 4 + q
                        tr(bk[:, q * 128:(q + 1) * 128], xin[:, sl, kt * 128:(kt + 1) * 128], ident, [("xin", sl), "c128"], [br])
                    act(lambda e, bk=bk, half=half, b=b: e.copy(
                        out=x_sb[:, half * 4:half * 4 + 4, b * 128:(b + 1) * 128],
                        in_=bk[:, :].rearrange("p (q t) -> p q t", q=4)), [br], ["x"])

        def rmsnorm(NT, nidx):
            sq, sqr = A.alloc("nsq", KT * NT, BF16)
            sq3 = sq.rearrange("p (k t) -> p k t", k=KT)
            act(lambda e: e.activation(out=sq3, in_=x_sb[:, :, 0:NT], func=AF.Square), ["x"], [sqr])
            bk, br = P()
            for kt in range(KT):
                mm(bk[:, 0:NT], onesb[:], sq3[:, kt, :], kt == 0, kt == KT - 1, [sqr, "onesb"], [br])
            a_act(rt[:, 0, 0:NT], bk[:, 0:NT], AF.Ln, [br], ["rt0"], bias=EPS, scale=1.0 / D)
            a_act(rt[:, 1, 0:NT], rt[:, 0, 0:NT], AF.Exp, ["rt0"], ["rt1"], scale=-0.5)
            for kt in range(KT):
                v_stt(hn[:, kt, 0:NT], x_sb[:, kt, 0:NT], normw[:, nidx, kt:kt + 1], rt[:, 1, 0:NT], ALU.mult, ALU.mult,
                      ["x", "normw", "rt1"], ["hn"])

        def rstd_only(NT):
            sq, sqr = A.alloc("nsq", KT * NT, BF16)
            sq3 = sq.rearrange("p (k t) -> p k t", k=KT)
            act(lambda e: e.activation(out=sq3, in_=x_sb[:, :, 0:NT], func=AF.Square), ["x"], [sqr])
            bk, br = P()
            for kt in range(KT):
                mm(bk[:, 0:NT], onesb[:], sq3[:, kt, :], kt == 0, kt == KT - 1, [sqr, "onesb"], [br])
            a_act(rt[:, 0, 0:NT], bk[:, 0:NT], AF.Ln, [br], ["rt0"], bias=EPS, scale=1.0 / D)
            a_act(rt[:, 1, 0:NT], rt[:, 0, 0:NT], AF.Exp, ["rt0"], ["rt1"], scale=-0.5)

        def prenorm(NT, nidx, mo):
            v_ts(hn[:, mo, 0:NT], x_sb[:, mo, 0:NT], normw[:, nidx, mo:mo + 1], None, ALU.mult, None, ["x", "normw"], ["hn"])

        def ffn(tile, which, layer, next_norm=None):
            NT = tile["NT"]
            A.reset()
            hid, hidr = A.alloc("hid", FT * NT, BF16)
            hid3 = hid.rearrange("p (m t) -> p m t", m=FT)
            sg = [A.alloc("sg%d" % i, NT, F32) for i in range(2)]
            su = [A.alloc("su%d" % i, NT, F32) for i in range(2)]
            for u in range(11):
                Wv, wr = W.next(("ffn_in", which, layer, u))
                for mi in range(2):
                    m = 2 * u + mi
                    bg, bgr = P()
                    for kt in range(KT):
                        mm(bg[:, 0:NT], Wv[:, kt, mi * 128:(mi + 1) * 128], hn[:, kt, 0:NT], kt == 0, kt == KT - 1, [wr, "hn"], [bgr])
                    bu, bur = P()
                    for kt in range(KT):
                        mm(bu[:, 0:NT], Wv[:, kt, 256 + mi * 128:256 + (mi + 1) * 128], hn[:, kt, 0:NT], kt == 0, kt == KT - 1,
                           [wr, "hn"], [bur])
                    if m == 0:
                        rstd_only(NT)
                    sgv, sgr = sg[m % 2]
                    suv, sur = su[m % 2]
                    v_tt(sgv, bg[:, 0:NT], rt[:, 1, 0:NT], ALU.mult, [bgr, "rt1"], [sgr])
                    a_act(sgv, sgv, AF.Silu, [sgr], [sgr])
                    v_tt(suv, bu[:, 0:NT], rt[:, 1, 0:NT], ALU.mult, [bur, "rt1"], [sur])
                    v_tt(hid3[:, m, :], sgv, suv, ALU.mult, [sgr, sur], [(hidr, m)])
            for u in range(4):
                Wv, wr = W.next(("ffn_out", which, layer, u))
                for mi in range(2):
                    mo = 2 * u + mi
                    bk, br = P()
                    for kt in range(FT):
                        mm(bk[:, 0:NT], Wv[:, kt, mi * 128:(mi + 1) * 128], hid3[:, kt, :], kt == 0, kt == FT - 1,
                           [wr, (hidr, kt)], [br])
                    v_stt(x_sb[:, mo, 0:NT], bk[:, 0:NT], 0.5, x_sb[:, mo, 0:NT], ALU.mult, ALU.add, [br, "x"], ["x"])
                    if next_norm is not None:
                        prenorm(NT, next_norm, mo)

        def out_proj(tile, tagname, mix3, mixr, next_norm=None):
            NT = tile["NT"]
            for u in range(4):
                Wv, wr = W.next((tagname, u))
                for mi in range(2):
                    mo = 2 * u + mi
                    bk, br = P()
                    for kt in range(KT):
                        mm(bk[:, 0:NT], Wv[:, kt, mi * 128:(mi + 1) * 128], mix3[:, kt, :], kt == 0, kt == KT - 1, [wr, mixr], [br])
                    v_tt(x_sb[:, mo, 0:NT], bk[:, 0:NT], x_sb[:, mo, 0:NT], ALU.add, [br, "x"], ["x"])
                    if next_norm is not None:
                        prenorm(NT, next_norm, mo)

        def head_norm(NT, osb3, osbr, gate3, gater, gidx, mix3, mixr, koff, scratch=None):
            if scratch is None:
                sqb, sqbr = A.alloc("hsq%d" % koff, 4 * NT, BF16)
                tm, tmr = A.alloc("htm%d" % koff, NT, F32)
                sqbl = [sqbr]
            else:
                sqb, sqbl, tm, tmr = scratch
            sqb3 = sqb.rearrange("p (h t) -> p h t", h=4)
            osbl = osbr if isinstance(osbr, list) else [osbr]
            act(lambda e: e.activation(out=sqb3, in_=osb3, func=AF.Square), osbl, sqbl)
            for h in range(4):
                bk, br = P()
                mm(bk[:, 0:NT], onesb[:], sqb3[:, h, :], True, True, sqbl + ["onesb"], [br])
                a_act(rt[:, 0, 0:NT], bk[:, 0:NT], AF.Ln, [br], ["rt0"], bias=EPS, scale=1.0 / 128.0)
                a_act(rt[:, 1, 0:NT], rt[:, 0, 0:NT], AF.Exp, ["rt0"], ["rt1"], scale=-0.5)
                v_tt(tm, osb3[:, h, :], rt[:, 1, 0:NT], ALU.mult, osbl + ["rt1"], [tmr])
                v_stt(mix3[:, koff + h, :], tm, hgain[:, gidx, h:h + 1], gate3[:, h, :], ALU.mult, ALU.mult,
                      [tmr, "hgain", gater], [mixr])

        def seg_of_chunk(tile, c):
            if tile["kind"] == "p":
                return 0, 0, (c == 0), (c == tile["NT"] // 64 - 1)
            sg = tile["segs"][c]
            if sg["seq"] == 0:
                return 0, c, (c == 0), sg["end"]
            return sg["seq"], c, True, True

        def state_io(tile, seq, start, end, s32, s32r, st_in, o_list, when):
            if when == "start":
                if seq == 0:
                    if tile["first"]:
                        v_memset(s32[:], 0.0, [s32r], eng="pool")
                else:
                    S.dma("pool", lambda e: e.dma_start(out=s32[:].rearrange("p (h e) -> p h e", h=4),
                                                        in_=st_in[seq - 1].rearrange("h d e -> d h e")),
                          None, reads=[], writes=[s32r])
            else:
                if seq == 0:
                    if tile["last"]:
                        S.dma("pool", lambda e: e.dma_start(out=o_list[0].rearrange("h d e -> d h e"),
                                                            in_=s32[:].rearrange("p (h e) -> p h e", h=4)),
                              None, reads=[s32r], writes=[])
                else:
                    S.dma("pool", lambda e: e.dma_start(out=o_list[1][seq - 1].rearrange("h d e -> d h e"),
                                                        in_=s32[:].rearrange("p (h e) -> p h e", h=4)),
                          None, reads=[s32r], writes=[])

        def linattn_core(tile, kind, qF, qFr, kF, kFr, kS, kSr, vtm, vtmr, osb3, osbr, s32, s32r, st_in, o_list,
                         ebend=None, ebendr=None, extra=None):
            NT = tile["NT"]
            NC = NT // 64
            sbf, sbfr = A.alloc("sbf_" + kind, (NC + 1) * 512, BF16)
            sbf3 = sbf.rearrange("p (c n) -> p c n", c=NC + 1)
            WD = 3
            attm = [A.alloc("attm%d_%s" % (i, kind), 256, BF16) for i in range(WD)]
            kdt = [A.alloc("kdt%d_%s" % (i, kind), 512, BF16) for i in range(WD)]
            mask = C64("retmask") if kind == "ret" else C64("incl4")
            done = {}

            def chunk_gen(c):
                seq, sgi, sstart, send = seg_of_chunk(tile, c)
                cs = slice(c * 64, (c + 1) * 64)
                bA, bAr = P((c % WD, WD + 1))
                for h in range(4):
                    mm(bA[0:64, h * 64:(h + 1) * 64], kF[:, h, cs], qF[:, h, cs], True, True, [kFr, qFr], [bAr])
                yield
                av, avr = attm[c % WD]
                v_tt(av[0:64, :], bA[0:64, 0:256], mask, ALU.mult, [bAr, "c64"], [avr])
                bT, bTr = P((c % WD, WD + 1))
                bTb = bT[0:64, 0:256].bitcast(BF16)
                for h in range(4):
                    tr(bTb[:, h * 128:(h + 1) * 128], kS[:, h, cs], identb[:], [kSr, "identb"], [bTr])
                yield
                kv, kvr = kdt[c % WD]
                if kind == "ret":
                    v_tt(kv[0:64, :], bTb, C64("kdec"), ALU.mult, [bTr, "c64"], [kvr])
                else:
                    act(lambda e, kv=kv, bTb=bTb: e.copy(out=kv[0:64, :], in_=bTb), [bTr], [kvr])
                yield
                while c > 0 and not done.get(c - 1):
                    yield
                if sstart:
                    state_io(tile, seq, sstart, send, s32, s32r, st_in, o_list, "start")
                    act(lambda e, c=c: e.copy(out=sbf3[:, c, :], in_=s32[:]), [s32r], [(sbfr, c)])
                bO, bOr = P((c % WD, WD + 1))
                for h in range(4):
                    mm(bO[:, h * 64:(h + 1) * 64], vtm[0:64, c, h * 128:(h + 1) * 128], av[0:64, h * 64:(h + 1) * 64], True, False,
                       [vtmr, avr], [bOr])
                    mm(bO[:, h * 64:(h + 1) * 64], sbf3[:, c, h * 128:(h + 1) * 128], qF[:, h, cs], False, True,
                       [(sbfr, c), qFr], [bOr])
                act(lambda e, bO=bO, cs=cs: e.copy(out=osb3[:, :, cs], in_=bO[:, 0:256].rearrange("p (h t) -> p h t", h=4)),
                    [bOr], [(osbr, c)])
                bS, bSr = P((c % WD, WD + 1))
                for h in range(4):
                    mm(bS[:, h * 128:(h + 1) * 128], kv[0:64, h * 128:(h + 1) * 128], vtm[0:64, c, h * 128:(h + 1) * 128], True, True,
                       [kvr, vtmr], [bSr])
                for h in range(4):
                    hs_ = slice(h * 128, (h + 1) * 128)
                    if kind == "ret":
                        sc = SDEC[h]
                        rr = [s32r, bSr]
                    else:
                        sc = ebend[:, h, c:c + 1]
                        rr = [s32r, bSr, ebendr]
                    v_stt(s32[:, hs_], s32[:, hs_], sc, bS[:, hs_], ALU.mult, ALU.add, rr, [s32r])
                if send:
                    state_io(tile, seq, sstart, send, s32, s32r, st_in, o_list, "end")
                else:
                    act(lambda e, c=c: e.copy(out=sbf3[:, c + 1, :], in_=s32[:]), [s32r], [(sbfr, c + 1)])
                done[c] = True

            gens = [(lambda c=c: chunk_gen(c)) for c in range(NC)]
            interleave(gens, WD, extra=(None if extra is None else (lambda: extra((WD, WD + 1)))))
            return [(osbr, c) for c in range(NC)]

        def gate_gen(NT, Wv, wr, gate3, gater, alloc):
            for h in range(4):
                bk, br = yield from alloc()
                for kt in range(KT):
                    mm(bk[:, 0:NT], Wv[:, kt, h * 128:(h + 1) * 128], hn[:, kt, 0:NT], kt == 0, kt == KT - 1, [wr, "hn"], [br])
                yield
                a_act(gate3[:, h, :], bk[:, 0:NT], AF.Silu, [br], [gater])
                yield br

        def tm_proj(tile, Wv, wr, vt3, vtr):
            NT = tile["NT"]
            for c in range(NT // 64):
                bk, br = P()
                for kt in range(KT):
                    mm(bk[0:64, :], hn[:, kt, c * 64:(c + 1) * 64], Wv[:, kt, 0:512], kt == 0, kt == KT - 1, [wr, "hn"], [br])
                act(lambda e, bk=bk, c=c: e.copy(out=vt3[0:64, c, :], in_=bk[0:64, :]), [br], [vtr])

        def fm_proj(NT, Wv, wr, h, sub=None):
            bk, br = P(sub)
            for kt in range(KT):
                mm(bk[:, 0:NT], Wv[:, kt, h * 128:(h + 1) * 128], hn[:, kt, 0:NT], kt == 0, kt == KT - 1, [wr, "hn"], [br])
            return bk, br

        def even_mixer(tile):
            NT = tile["NT"]
            NC = NT // 64
            A.reset()
            mix, mixr = A.alloc("mix", KT * NT, BF16)
            mix3 = mix.rearrange("p (k t) -> p k t", k=KT)
            rmsnorm(NT, 2)
            if DBG <= -1:
                for u in range(0, 8):
                    W.next(("even_in", u))
                for u in range(4):
                    W.next(("even_out", u))
                return
            tab, tabr = A.alloc("tab", 2 * NT, F32)
            tab3 = tab.rearrange("p (a t) -> p a t", a=2)
            S.dma("pool", lambda e: e.dma_start(out=tab3, in_=rope_d[:, :, tile["tok0"]:tile["tok0"] + NT]), None,
                  reads=[], writes=[tabr])
            if DBG <= 0:
                for u in range(0, 8):
                    W.next(("even_in", u))
                for u in range(4):
                    W.next(("even_out", u))
                return

            def al3(name, dt):
                v, r = A.alloc(name, 4 * NT, dt)
                return v.rearrange("p (h t) -> p h t", h=4), r

            qd3, qdr = al3("qd", BF16)
            kr3, krr = al3("krot", BF16)
            vt, vtr = A.alloc("vtm", NC * 512, BF16)
            vt3 = vt.rearrange("p (c n) -> p c n", c=NC)
            gate3, gater = al3("gate", F32)
            osb3, osbr = al3("osb", F32)
            xbf = [A.alloc("xbf%d" % i, NT, BF16) for i in range(2)]
            ta = [A.alloc("ta%d" % i, NT, F32) for i in range(2)]
            tb = [A.alloc("tb%d" % i, NT, F32) for i in range(2)]
            qdecv = C128("qdec").rearrange("p (h t) -> p h t", h=4)

            def rope_unit(Wv, wr, dst3, dstr, isq):
                for h in range(4):
                    bk, br = fm_proj(NT, Wv, wr, h)
                    xv, xr = xbf[h % 2]
                    LV = int(os.environ.get("KLV", "9"))
                    act(lambda e, xv=xv, bk=bk: e.copy(out=xv, in_=bk[:, 0:NT]), [br], [xr])
                    if LV <= 1:
                        continue
                    b2, b2r = P()
                    mm(b2[:, 0:NT], rmatb[:], xv, True, True, [xr, "rmatb"], [b2r])
                    if LV <= 2:
                        continue
                    tav, tar = ta[h % 2]
                    tbv, tbr = tb[h % 2]
                    v_tt(tav, bk[:, 0:NT], tab3[:, 0, :], ALU.mult, [br, tabr], [tar])
                    if LV <= 3:
                        continue
                    v_tt(tbv, b2[:, 0:NT], tab3[:, 1, :], ALU.mult, [b2r, tabr], [tbr])
                    if LV <= 4:
                        continue
                    if isq:
                        v_tt(tav, tav, tbv, ALU.add, [tar, tbr], [tar])
                        if os.environ.get("KVAR", "0") == "1":
                            for c in range(NC):
                                v_tt(dst3[:, h, c * 64:(c + 1) * 64], tav[:, c * 64:(c + 1) * 64], qdecv[:, h, :], ALU.mult, [tar, "c128"], [dstr])
                        else:
                            v_tt(dst3[:, h, :].rearrange("p (c t) -> p c t", t=64),
                                 tav.rearrange("p (c t) -> p c t", t=64),
                                 qdecv[:, h:h + 1, :].to_broadcast([128, NC, 64]), ALU.mult, [tar, "c128"], [dstr])
                    else:
                        v_tt(dst3[:, h, :], tav, tbv, ALU.add, [tar, tbr], [dstr])

            def bail(k):
                for u in range(k, 8):
                    W.next(("even_in", u))
                for u in range(4):
                    W.next(("even_out", u))

            Wv, wr = W.next(("even_in", 0))
            rope_unit(Wv, wr, qd3, qdr, True)
            if DBG <= 1:
                return bail(1)
            Wv, wr = W.next(("even_in", 1))
            rope_unit(Wv, wr, kr3, krr, False)
            Wv, wr = W.next(("even_in", 2))
            tm_proj(tile, Wv, wr, vt3, vtr)
            Wg, wgr = W.next(("even_in", 3))

            def sub_alloc(sub):
                def alloc():
                    return P(sub)
                    yield
                return alloc

            osbr = linattn_core(tile, "ret", qd3, qdr, kr3, krr, kr3, krr, vt3, vtr, osb3, osbr, s_ret, "s_ret", st_ret, o_ret,
                                extra=lambda sub: gate_gen(NT, Wg, wgr, gate3, gater, sub_alloc(sub)))
            if DBG <= 3:
                return bail(4)
            head_norm(NT, osb3, osbr, gate3, gater, 0, mix3, mixr, 0)
            if DBG <= 4:
                return bail(4)

            A.reset()
            mix, mixr = A.alloc("mix", KT * NT, BF16)
            mix3 = mix.rearrange("p (k t) -> p k t", k=KT)
            qe3, qer = al3("qe", BF16)
            ke3, ker = al3("ke", BF16)
            kn3, knr = al3("kend", BF16)
            vt2, vt2r = A.alloc("vtm2", NC * 512, BF16)
            vt23 = vt2.rearrange("p (c n) -> p c n", c=NC)
            gate23, gate2r = al3("gate2", F32)
            osb23, osb2r = al3("osb2", F32)
            qs3, qsr = al3("qsil", F32)
            eb3, ebr = al3("eb", F32)
            fb3, fbr = al3("fb", F32)
            ebend, ebendr = A.alloc("ebend", 4 * NC, F32)
            ebend3 = ebend.rearrange("p (h c) -> p h c", h=4)
            t1 = [A.alloc("t1_%d" % i, NT, F32) for i in range(2)]
            Wv, wr = W.next(("even_in", 4))
            for h in range(4):
                bk, br = fm_proj(NT, Wv, wr, h)
                a_act(qs3[:, h, :], bk[:, 0:NT], AF.Silu, [br], [qsr])
            Wv, wr = W.next(("even_in", 5))
            for h in range(4):
                bk, br = fm_proj(NT, Wv, wr, h)
                tv, tr_ = t1[h % 2]
                a_act(tv, bk[:, 0:NT], AF.Sigmoid, [br], [tr_])
                v_ts(fb3[:, h, :], tv, lbt[:, 3, h:h + 1], lbt[:, 2, h:h + 1], ALU.mult, ALU.add, [tr_, "lbt"], [(fbr, h)])
                v_ts(eb3[:, h, :], fb3[:, h, :], -1.0, 1.0, ALU.mult, ALU.add, [(fbr, h)], [(ebr, h)])
                a_act(fb3[:, h, :], fb3[:, h, :], AF.Ln, [(fbr, h)], [(fbr, h)])
                dve(lambda e, h=h: e.tensor_tensor_scan(out=fb3[:, h, :], data0=C128("reset")[:, 0:NT], data1=fb3[:, h, :],
                                                        initial=0.0, op0=ALU.mult, op1=ALU.add),
                    [(fbr, h), "c128"], [(fbr, h)])
                tv2, tr2 = t1[(h + 1) % 2]
                a_act(tv2, fb3[:, h, :], AF.Exp, [(fbr, h)], [tr2], scale=-1.0)
                v_tt(ke3[:, h, :], eb3[:, h, :], tv2, ALU.mult, [(ebr, h), tr2], [ker])
                a_act(eb3[:, h, :], fb3[:, h, :], AF.Exp, [(fbr, h)], [(ebr, h)])
                v_tt(qe3[:, h, :], qs3[:, h, :], eb3[:, h, :], ALU.mult, [qsr, (ebr, h)], [qer])
                act(lambda e, h=h: e.copy(out=ebend3[:, h, :], in_=eb3[:, h, :].rearrange("p (c t) -> p c t", t=64)[:, :, 63]),
                    [(ebr, h)], [ebendr])
                v_tt(kn3[:, h, :].rearrange("p (c t) -> p c t", t=64), ke3[:, h, :].rearrange("p (c t) -> p c t", t=64),
                     ebend3[:, h, :].unsqueeze(2).to_broadcast([128, NC, 64]), ALU.mult, [ker, ebendr], [knr])
            Wv, wr = W.next(("even_in", 6))
            tm_proj(tile, Wv, wr, vt23, vt2r)
            Wg2, wg2r = W.next(("even_in", 7))
            osb2r = linattn_core(tile, "hg", qe3, qer, ke3, ker, kn3, knr, vt23, vt2r, osb23, osb2r, s_hg, "s_hg", st_hg, o_hg,
                                 ebend=ebend3, ebendr=ebendr,
                                 extra=lambda sub: gate_gen(NT, Wg2, wg2r, gate23, gate2r, sub_alloc(sub)))
            head_norm(NT, osb23, osb2r, gate23, gate2r, 1, mix3, mixr, 4)
            out_proj(tile, "even_out", mix3, mixr, next_norm=4)

        def odd_mixer(tile):
            NT = tile["NT"]
            NC = NT // 64
            nseg = len(tile["segs"])
            L = tile["segs"][0]["L"]
            SEGS = tile["segs"]
            def al3(name, dt, n=4):
                v, r = A.alloc(name, n * NT, dt)
                return v.rearrange("p (h t) -> p h t", h=n), r

            def common():
                A.reset()
                mix, mixr = A.alloc("mix", KT * NT, BF16)
                beta, betar = A.alloc("beta", NC * 4, F32)
                loga, logar = A.alloc("loga", NC * 4, F32)
                egdk, egdkr = A.alloc("egdk", NC * 8, F32)
                egrow3, egr = al3("egrow", F32)
                return (mix.rearrange("p (k t) -> p k t", k=KT), mixr, beta.rearrange("p (c h) -> p c h", c=NC), betar,
                        loga.rearrange("p (c h) -> p c h", c=NC), logar, egdk, egdk.rearrange("p (c h) -> p c h", c=NC), egdkr,
                        egrow3, egr)

            mix3, mixr, beta3, betar, loga3, logar, egdk, egdk3, egdkr, egrow3, egr = common()
            rmsnorm(NT, 3)

            Wv, wr = W.next(("odd_in", "bda"))
            bk, br = P()
            for c in range(NC):
                for kt in range(KT):
                    mm(bk[0:64, c * 8:(c + 1) * 8], hn[:, kt, c * 64:(c + 1) * 64], Wv[:, kt, 0:8], kt == 0, kt == KT - 1, [wr, "hn"], [br])
            bk3 = bk[0:64, 0:NC * 8].rearrange("p (c h) -> p c h", c=NC)
            a_act(beta3[0:64], bk3[:, :, 0:4], AF.Sigmoid, [br], [betar])
            v_tt(loga3[0:64], bk3[:, :, 4:8], dnb[:, 0, 0:NC, :], ALU.add, [br, "dnb"], [logar])
            a_act(loga3[0:64], loga3[0:64], AF.Exp, [logar], [logar])
            a_act(loga3[0:64], loga3[0:64], AF.Ln, [logar], [logar], bias=1.0)
            v_tt(loga3[0:64], loga3[0:64], dnb[:, 1, 0:NC, :], ALU.mult, [logar, "dnb"], [logar])
            b2, b2r = P()
            for c in range(NC):
                mm(b2[0:64, c * 8:c * 8 + 4], C64("U"), loga3[0:64, c, :], True, True, ["c64", logar], [b2r])
                mm(b2[0:64, c * 8 + 4:c * 8 + 8], C64("Urev"), loga3[0:64, c, :], True, True, ["c64", logar], [b2r])
            a_act(egdk[0:64, :], b2[0:64, 0:NC * 8], AF.Exp, [b2r], [egdkr])
            def make_X(c, Xv, Xr):
                X3 = Xv[0:64, 0:256].rearrange("p (h t) -> p h t", h=4)
                v_tt(X3, C64("U").unsqueeze(1).to_broadcast([64, 4, 64]),
                     loga3[0:64, c, :].unsqueeze(2).to_broadcast([64, 4, 64]), ALU.mult, ["c64", logar], [Xr])
                return X3

            Xt = [A.alloc("Xt%d" % i, 256, F32) for i in range(2)]
            for c in range(NC):
                Xv, Xr = Xt[c % 2]
                X3 = make_X(c, Xv, Xr)
                bE, bEr = P()
                for h in range(4):
                    mm(bE[:, h * 64:(h + 1) * 64], C64("ones")[:, 0:128], X3[:, h, :], True, True, ["c64", Xr], [bEr])
                act(lambda e, bE=bE, c=c: e.activation(out=egrow3[:, :, c * 64:(c + 1) * 64],
                                                       in_=bE[:, 0:256].rearrange("p (h t) -> p h t", h=4), func=AF.Exp),
                    [bEr], [egr])

            cvs = {}

            def conv_alloc(nbuf):
                cvs["nbuf"] = nbuf
                xh, xhr = A.alloc("xh", 4 * nseg * (3 + L), F32)
                cvs["xh4"] = xh.rearrange("p (j s t) -> p j s t", j=4, s=nseg)
                cvs["xhr"] = xhr
                cvs["cst"] = [A.alloc("cst%d" % i, 128, F32) for i in range(2)]
                cvs["cv"] = [A.alloc("cv%d" % i, NT, F32) for i in range(nbuf)]
                cvs["n"] = 0

            def conv_unit(g, Wv, wr, consume):
                xh4, xhr = cvs["xh4"], cvs["xhr"]
                nb = cvs["nbuf"]

                def tile_gen(j):
                    sub = (j % nb, nb)
                    gj = g * 4 + j
                    bk, br = fm_proj(NT, Wv, wr, j, sub)
                    for si, sg in enumerate(SEGS):
                        seq = sg["seq"]
                        if sg["prev_same"]:
                            continue
                        if seq == 0:
                            if sg["start"]:
                                v_memset(xh4[:, j, si, 0:3], 0.0, [(xhr, j)], eng="pool")
                            else:
                                v_copy(xh4[:, j, si, 0:3], hcar[:, gj, :], [("hcar", gj)], [(xhr, j)], eng="pool")
                        else:
                            if g == 0:
                                src = st_lc[seq - 1][:, j * 128:(j + 1) * 128]
                            else:
                                src = st_dc[seq - 1][:, (gj - 4) * 128:(gj - 3) * 128]
                            S.dma("pool", lambda e, j=j, si=si, src=src: e.dma_start(out=xh4[:, j, si, 0:3], in_=src.rearrange("r p -> p r")),
                                  None, reads=[], writes=[(xhr, j)])
                    act(lambda e, bk=bk, j=j: e.copy(out=xh4[:, j, :, 3:3 + L], in_=bk[:, 0:NT].rearrange("p (s t) -> p s t", s=nseg)),
                        [br], [(xhr, j)])
                    ks = [si for si, sg in enumerate(SEGS) if sg["prev_same"]]
                    if ks:
                        k0, k1 = ks[0], ks[-1] + 1
                        v_copy(xh4[:, j, k0:k1, 0:3], xh4[:, j, k0 - 1:k1 - 1, L:L + 3], [(xhr, j)], [(xhr, j)], eng="pool")
                    p_last = max([si for si, sg in enumerate(SEGS) if sg["seq"] == 0])
                    if not SEGS[p_last]["end"]:
                        v_copy(hcar[:, gj, :], xh4[:, j, p_last, L:L + 3], [(xhr, j)], [("hcar", gj)], eng="pool")
                    yield
                    cvv, cvr = cvs["cv"][j % nb]
                    cv3 = cvv.rearrange("p (s t) -> p s t", s=nseg)
                    if g == 0:
                        wcol = lambda k, j=j: lruv[:, k, j:j + 1]
                        v_ts(cv3, xh4[:, j, :, 0:L], wcol(0), lruv[:, 4, j:j + 1], ALU.mult, ALU.add, [(xhr, j), "lruv"], [cvr])
                    else:
                        wcol = lambda k, gj=gj: dncw[:, k, gj - 4:gj - 3]
                        v_ts(cv3, xh4[:, j, :, 0:L], wcol(0), None, ALU.mult, None, [(xhr, j), "dncw"], [cvr])
                    for k in range(1, 4):
                        v_stt(cv3, xh4[:, j, :, k:k + L], wcol(k), cv3, ALU.mult, ALU.add, [(xhr, j), "lruv", "dncw", cvr], [cvr],
                              eng="dve")
                    for si, sg in enumerate(SEGS):
                        seq = sg["seq"]
                        if sg["end"]:
                            bt, btr = P(sub)
                            tr(bt[0:3, 0:128], xh4[:, j, si, L:L + 3], ident, [(xhr, j), "c128"], [btr])
                            cv_, cvr_ = cvs["cst"][cvs["n"] % 2]
                            cvs["n"] += 1
                            act(lambda e, bt=bt, cv_=cv_: e.copy(out=cv_[0:3, :], in_=bt[0:3, 0:128]), [btr], [cvr_])
                            if g == 0:
                                dd = (o_lc[0][0] if seq == 0 else o_lc[1][seq - 1])[:, j * 128:(j + 1) * 128]
                            else:
                                dd = (o_dc[0][0] if seq == 0 else o_dc[1][seq - 1])[:, (gj - 4) * 128:(gj - 3) * 128]
                            S.dma("pool", lambda e, dd=dd, cv_=cv_: e.dma_start(out=dd, in_=cv_[0:3, :]), None, reads=[cvr_], writes=[])
                    yield
                    yield from consume(j, cvv, cvr, sub, nb)

                interleave([(lambda j=j: tile_gen(j)) for j in range(4)], nb)

            conv_alloc(4)
            hs3, hsr = al3("hs", F32)
            tmpA = [A.alloc("lA%d" % i, NT, F32) for i in range(4)]
            tmpB = [A.alloc("lB%d" % i, NT, F32) for i in range(4)]
            tmpC = [A.alloc("lC%d" % i, NT, F32) for i in range(4)]
            lcb = [A.alloc("lcb%d" % i, NT, BF16) for i in range(4)]

            def lru_consume(j, cvv, cvr, sub, nb):
                lb_, lbr_ = lcb[j % nb]
                act(lambda e: e.copy(out=lb_, in_=cvv), [cvr], [lbr_])
                br_, brr = P(sub)
                mm(br_[:, 0:NT], bd[:, 0, j, :], lb_, True, True, ["bd", lbr_], [brr])
                bi_, bir = P(sub)
                mm(bi_[:, 0:NT], bd[:, 1, j, :], lb_, True, True, ["bd", lbr_], [bir])
                yield
                rv, rr = tmpA[j % nb]
                iv, ir = tmpB[j % nb]
                av, ar = tmpC[j % nb]
                a_act(rv, br_[:, 0:NT], AF.Sigmoid, [brr, "lruv"], [rr], bias=lruv[:, 5, j:j + 1])
                a_act(iv, bi_[:, 0:NT], AF.Sigmoid, [bir, "lruv"], [ir], bias=lruv[:, 6, j:j + 1])
                yield
                a_act(av, rv, AF.Exp, [rr, "lrud"], [ar], scale=lrud[:, 0, j:j + 1])
                a_act(rv, rv, AF.Exp, [rr, "lrud"], [rr], scale=lrud[:, 1, j:j + 1])
                v_tt(iv, iv, cvv, ALU.mult, [ir, cvr], [ir])
                yield
                a_act(rv, rv, AF.Sqrt, [rr], [rr], bias=1.0, scale=-1.0)
                yield
                v_tt(iv, iv, rv, ALU.mult, [ir, rr], [ir])
                for si, sg in enumerate(SEGS):
                    seq = sg["seq"]
                    if seq == 0 and sg["start"]:
                        v_memset(hstate[:, 0, j:j + 1], 0.0, [("hstate", j)])
                    elif seq != 0:
                        S.dma("pool", lambda e, seq=seq, j=j: e.dma_start(
                            out=hstate[:, seq, j:j + 1], in_=st_lh[seq - 1:seq, j * 128:(j + 1) * 128].rearrange("o p -> p o")),
                            None, reads=[], writes=[("hstate", j)])
                    cols = slice(si * L, (si + 1) * L)
                    dve(lambda e, cols=cols, seq=seq: e.tensor_tensor_scan(out=hs3[:, j, cols], data0=av[:, cols], data1=iv[:, cols],
                                                                          initial=hstate[:, seq, j:j + 1], op0=ALU.mult, op1=ALU.add),
                        [ar, ir, ("hstate", j)], [(hsr, j)])
                    v_copy(hstate[:, seq, j:j + 1], hs3[:, j, (si + 1) * L - 1:(si + 1) * L], [(hsr, j)], [("hstate", j)])

            Wv, wr = W.next(("odd_in", 0))
            conv_unit(0, Wv, wr, lru_consume)
            Wv, wr = W.next(("odd_in", 1))
            for j in range(4):
                bk, br = fm_proj(NT, Wv, wr, j)
                x2, x2r = tmpA[j % 2]
                inn, innr = tmpB[j % 2]
                a_act(x2, bk[:, 0:NT], AF.Square, [br], [x2r])
                v_ts(x2, x2, 0.044715, 1.0, ALU.mult, ALU.add, [x2r], [x2r])
                v_tt(inn, x2, bk[:, 0:NT], ALU.mult, [x2r, br], [innr])
                a_act(inn, inn, AF.Sigmoid, [innr], [innr], scale=2.0 * math.sqrt(2.0 / math.pi))
                v_tt(inn, inn, bk[:, 0:NT], ALU.mult, [innr, br], [innr])
                v_tt(mix3[:, j, :], inn, hs3[:, j, :], ALU.mult, [innr, (hsr, j)], [mixr])
            for si, sg in enumerate(SEGS):
                seq = sg["seq"]
                if sg["end"]:
                    dst = o_lh[0][0:1, :] if seq == 0 else o_lh[1][seq - 1:seq, :]
                    S.dma("pool", lambda e, dst=dst, seq=seq: e.dma_start(out=dst.rearrange("o (j p) -> p (o j)", p=128), in_=hstate[:, seq, :]),
                          None, reads=[("hstate", j) for j in range(4)] + [(hsr, j) for j in range(4)], writes=[])

            mix3, mixr, beta3, betar, loga3, logar, egdk, egdk3, egdkr, egrow3, egr = common()
            conv_alloc(2)
            tmpA = [A.alloc("lA%d" % i, NT, F32) for i in range(2)]
            qF3, qFr = al3("qF", BF16)
            qg3, qgr = al3("qg", BF16)
            kF3, kFr = al3("kF", BF16)
            vF3, vFr = al3("vF", BF16)
            sqt = [A.alloc("sqt%d" % i, NT, BF16) for i in range(2)]

            def qk_consume(isq):
                def f(j, cvv, cvr, sub, nb):
                    a_act(cvv, cvv, AF.Silu, [cvr], [cvr])
                    sv, sr = sqt[j % nb]
                    a_act(sv, cvv, AF.Square, [cvr], [sr])
                    bk, br = P(sub)
                    mm(bk[:, 0:NT], onesb[:], sv, True, True, [sr, "onesb"], [br])
                    yield
                    tv, tvr = tmpA[j % nb]
                    if isq:
                        a_act(tv, bk[:, 0:NT], AF.Ln, [br], [tvr], bias=EPS * 128.0, scale=128.0)
                    else:
                        a_act(tv, bk[:, 0:NT], AF.Ln, [br], [tvr], bias=EPS, scale=1.0)
                    a_act(tv, tv, AF.Exp, [tvr], [tvr], scale=-0.5)
                    yield
                    if isq:
                        v_tt(cvv, cvv, tv, ALU.mult, [cvr, tvr], [cvr])
                        v_copy(qF3[:, j, :], cvv, [cvr], [qFr], eng="pool")
                        v_tt(qg3[:, j, :], cvv, egrow3[:, j, :], ALU.mult, [cvr, egr], [qgr])
                    else:
                        v_tt(kF3[:, j, :], cvv, tv, ALU.mult, [cvr, tvr], [kFr])
                return f

            def v_consume(j, cvv, cvr, sub, nb):
                a_act(vF3[:, j, :], cvv, AF.Silu, [cvr], [vFr])
                yield

            Wv, wr = W.next(("odd_in", 2))
            conv_unit(1, Wv, wr, qk_consume(True))
            Wv, wr = W.next(("odd_in", 3))
            conv_unit(2, Wv, wr, qk_consume(False))
            Wv, wr = W.next(("odd_in", 4))
            conv_unit(3, Wv, wr, v_consume)
            gate3, gater = al3("gate", F32)
            Wdg, wdgr = W.next(("odd_in", 5))
            osb3, osbr = al3("osb", F32)

            def f64(name, n=256):
                v, r = A.alloc(name, n, F32)
                return v, r

            WDN = 3
            held = set()

            def galloc(n):
                while True:
                    free = [(bank_ctr[0] + k) % 8 for k in range(8) if ((bank_ctr[0] + k) % 8) not in held]
                    if len(free) >= n:
                        out = []
                        for i in free[:n]:
                            held.add(i)
                            out.append((banks[i], ("ps", i)))
                        bank_ctr[0] = free[n - 1] + 1
                        return out
                    yield

            def gfree(*ress):
                for r in ress:
                    held.discard(r[1])
            Xc, Xcr = f64("Xc")
            negX, negXr = f64("negX")
            dgB, dgBr = f64("dgB")
            Gm, Gmr = f64("Gm")
            sets = []
            for i in range(WDN):
                d = {}
                d["DT"] = f64("DT%d" % i)
                d["Bm"] = f64("Bm%d" % i)
                d["Nn"] = [(nmr[:, i, k, :], ("nmr", i, k)) for k in range(2)]
                d["Mm"] = [(nmr[:, i, 2 + k, :], ("nmr", i, 2 + k)) for k in range(2)]
                d["Rr"] = (nmr[:, i, 4, :], ("nmr", i, 4))
                d["QKm"] = A.alloc("QKm%d" % i, 256, BF16)
                d["Yb"] = A.alloc("Yb%d" % i, 256, BF16)
                d["kgt"] = A.alloc("kgt%d" % i, 512, BF16)
                d["kdc"] = A.alloc("kdc%d" % i, 512, BF16)
                d["vtm"] = A.alloc("vtmd%d" % i, 512, BF16)
                d["usb"] = f64("usb%d" % i, 512)
                d["WkT"] = A.alloc("WkT%d" % i, 256, BF16)
                d["wv"] = A.alloc("wv%d" % i, 512, BF16)
                sets.append(d)
            sbf, sbfr = A.alloc("sbfd", 512, BF16)
            ones64 = C64("ones")[:, 0:64]
            id64 = ident[0:64, 0:64]
            dn_done = {}

            def h4(v):
                return v[0:64, 0:256].rearrange("p (h t) -> p h t", h=4)

            def r4(v):
                return v[0:64, 0:256].bitcast(F32R).rearrange("p (h t) -> p h t", h=4)

            def rr(v):
                return v[0:64, 0:256].bitcast(F32R)

            def dn_gen(c):
                d = sets[c % WDN]
                DT, DTr = d["DT"]
                Bm, Bmr = d["Bm"]
                Nn, Mm = d["Nn"], d["Mm"]
                Rr, Rrr = d["Rr"]
                QKm, QKmr = d["QKm"]
                Yb, Ybr = d["Yb"]
                kgt, kgtr = d["kgt"]
                kdc, kdcr = d["kdc"]
                vtm, vtmr = d["vtm"]
                usb, usbr = d["usb"]
                WkT, WkTr = d["WkT"]
                wv_, wvr = d["wv"]
                seq, sgi, sstart, send = seg_of_chunk(tile, c)
                cs = slice(c * 64, (c + 1) * 64)
                X3 = make_X(c, Xc, Xcr)
                act(lambda e: e.mul(out=negX[0:64, :], in_=Xc[0:64, :], mul=-1.0), [Xcr], [negXr])
                v_tt(h4(dgB), C64("i4").rearrange("p (h t) -> p h t", h=4),
                     beta3[0:64, c, :].unsqueeze(2).to_broadcast([64, 4, 64]), ALU.mult, ["c64", betar], [dgBr])
                (bG, bGr), (bK, bKr), (bT, bTr), (bV, bVr) = yield from galloc(4)
                for h in range(4):
                    mm(bG[0:64, h * 128:h * 128 + 64], ones64, X3[:, h, :], True, False, ["c64", Xcr], [bGr])
                    mm(bG[0:64, h * 128:h * 128 + 64], h4(negX)[:, h, :], ones64, False, True, ["c64", negXr], [bGr])
                    mm(bG[0:64, h * 128 + 64:h * 128 + 128], ones64, h4(dgB)[:, h, :], True, True, ["c64", dgBr], [bGr])
                bG3 = bG[0:64, :].rearrange("p (h t) -> p h t", h=4)
                neg3 = C64("neg4").rearrange("p (h t) -> p h t", h=4)
                str3 = C64("strict4").rearrange("p (h t) -> p h t", h=4)
                for h in range(4):
                    mm(bK[0:64, h * 128:h * 128 + 64], kF3[:, h, cs], kF3[:, h, cs], True, True, [kFr], [bKr])
                    mm(bK[0:64, h * 128 + 64:h * 128 + 128], kF3[:, h, cs], qF3[:, h, cs], True, True, [kFr, qFr], [bKr])
                bK3 = bK[0:64, :].rearrange("p (h t) -> p h t", h=4)
                bTb = bT[0:64, 0:256].bitcast(BF16)
                for h in range(4):
                    tr(bTb[:, h * 128:(h + 1) * 128], kF3[:, h, cs], identb[:], [kFr, "identb"], [bTr])
                bT3 = bTb.rearrange("p (h d) -> p h d", h=4)
                bVb = bV[0:64, 0:256].bitcast(BF16)
                for h in range(4):
                    tr(bVb[:, h * 128:(h + 1) * 128], vF3[:, h, cs], identb[:], [vFr, "identb"], [bVr])
                yield
                v_tt(h4(Gm), bG3[:, :, 0:64], neg3, ALU.add, [bGr, "c64"], [Gmr])
                a_act(DT[0:64, :], Gm[0:64, :], AF.Exp, [Gmr], [DTr])
                v_tt(h4(Bm), bG3[:, :, 64:128], str3, ALU.mult, [bGr, "c64"], [Bmr])
                v_tt(kgt[0:64, :].rearrange("p (h d) -> p h d", h=4), bT3,
                     egdk3[0:64, c, 0:4].unsqueeze(2).to_broadcast([64, 4, 128]), ALU.mult, [bTr, egdkr], [kgtr])
                v_tt(kdc[0:64, :].rearrange("p (h d) -> p h d", h=4), bT3,
                     egdk3[0:64, c, 4:8].unsqueeze(2).to_broadcast([64, 4, 128]), ALU.mult, [bTr, egdkr], [kdcr])
                act(lambda e, bVb=bVb: e.copy(out=vtm[0:64, :], in_=bVb), [bVr], [vtmr])
                gfree(bGr, bTr, bVr)
                yield
                v_tt(Bm[0:64, :], Bm[0:64, :], DT[0:64, :], ALU.mult, [Bmr, DTr], [Bmr])
                N0, N0r = Nn[0]
                v_tt(r4(N0), bK3[:, :, 0:64], h4(Bm), ALU.mult, [bKr, Bmr], [N0r])
                v_tt(h4(QKm), bK3[:, :, 64:128], h4(DT), ALU.mult, [bKr, DTr], [QKmr])
                gfree(bKr)
                yield
                M0, M0r = Mm[0]
                ((bt, btr),) = yield from galloc(1)
                for h in range(4):
                    tr(bt[0:64, h * 64:(h + 1) * 64], h4(N0)[:, h, :], id64, [N0r, "c128"], [btr])
                act(lambda e, bt=bt, M0=M0: e.copy(out=rr(M0), in_=bt[0:64, 0:256]), [btr], [M0r])
                gfree(btr)
                v_tt(rr(Rr), C64("i4"), N0[0:64, :], ALU.subtract, [N0r, "c64"], [Rrr])
                yield
                cur = 0
                for stg in range(5):
                    Nc_, Ncr = Nn[cur]
                    Mc_, Mcr = Mm[cur]
                    Nx_, Nxr = Nn[1 - cur]
                    Mx_, Mxr = Mm[1 - cur]
                    last = (stg == 4)
                    if not last:
                        (bN, bNr), (bM, bMr) = yield from galloc(2)
                        for h in range(4):
                            mm(bN[0:64, h * 64:(h + 1) * 64], r4(Mc_)[:, h, :], r4(Nc_)[:, h, :], True, True, [Mcr, Ncr], [bNr])
                    else:
                        ((bM, bMr),) = yield from galloc(1)
                    for h in range(4):
                        mm(bM[0:64, h * 64:(h + 1) * 64], r4(Nc_)[:, h, :], r4(Mc_)[:, h, :], True, True, [Mcr, Ncr], [bMr])
                    yield
                    if not last:
                        v_tt(r4(Nx_), bN[0:64, 0:256].rearrange("p (h t) -> p h t", h=4),
                             C64("ones")[:, 0:64].unsqueeze(1).to_broadcast([64, 4, 64]), ALU.mult, [bNr, "c64"], [Nxr])
                    act(lambda e, bM=bM, Mx_=Mx_: e.copy(out=rr(Mx_), in_=bM[0:64, 0:256]), [bMr], [Mxr])
                    if not last:
                        gfree(bNr)
                    gfree(bMr)
                    ((bR, bRr),) = yield from galloc(1)
                    for h in range(4):
                        mm(bR[0:64, h * 64:(h + 1) * 64], r4(Mx_)[:, h, :], r4(Rr)[:, h, :], True, True, [Mxr, Rrr], [bRr])
                    yield
                    v_tt(rr(Rr), Rr[0:64, :], bR[0:64, 0:256], ALU.add, [Rrr, bRr], [Rrr])
                    gfree(bRr)
                    cur = 1 - cur
                for h in range(4):
                    act(lambda e, h=h, c=c: e.activation(out=h4(Yb)[:, h, :], in_=h4(Rr)[:, h, :], func=AF.Copy, scale=beta3[0:64, c, h:h + 1]),
                        [Rrr, betar], [Ybr])
                yield
                (bU, bUr), (bW, bWr) = yield from galloc(2)
                for h in range(4):
                    mm(bU[0:64, h * 128:(h + 1) * 128], h4(Yb)[:, h, :], vtm[0:64, h * 128:(h + 1) * 128], True, True, [Ybr, vtmr], [bUr])
                act(lambda e, bU=bU: e.copy(out=usb[0:64, :], in_=bU[0:64, :]), [bUr], [usbr])
                for h in range(4):
                    mm(bW[:, h * 64:(h + 1) * 64], kgt[0:64, h * 128:(h + 1) * 128], h4(Yb)[:, h, :], True, True, [kgtr, Ybr], [bWr])
                act(lambda e, bW=bW: e.copy(out=WkT, in_=bW[:, 0:256]), [bWr], [WkTr])
                gfree(bUr, bWr)
                yield
                while c > 0 and not dn_done.get(c - 1):
                    yield
                if sstart:
                    state_io(tile, seq, sstart, send, s_dn, "s_dn", st_dn, o_dn, "start")
                    act(lambda e: e.copy(out=sbf, in_=s_dn[:]), ["s_dn"], [sbfr])
                ((bWS, bWSr),) = yield from galloc(1)
                for h in range(4):
                    mm(bWS[0:64, h * 128:(h + 1) * 128], WkT[:, h * 64:(h + 1) * 64], sbf[:, h * 128:(h + 1) * 128], True, True,
                       [WkTr, sbfr], [bWSr])
                v_tt(wv_[0:64, :], usb[0:64, :], bWS[0:64, :], ALU.subtract, [usbr, bWSr], [wvr])
                gfree(bWSr)
                ((bO, bOr),) = yield from galloc(1)
                for h in range(4):
                    mm(bO[:, h * 64:(h + 1) * 64], wv_[0:64, h * 128:(h + 1) * 128], h4(QKm)[:, h, :], True, False, [wvr, QKmr], [bOr])
                    mm(bO[:, h * 64:(h + 1) * 64], sbf[:, h * 128:(h + 1) * 128], qg3[:, h, cs], False, True, [sbfr, qgr], [bOr])
                act(lambda e, bO=bO, cs=cs: e.copy(out=osb3[:, :, cs], in_=bO[:, 0:256].rearrange("p (h t) -> p h t", h=4)),
                    [bOr], [(osbr, c)])
                gfree(bOr)
                ((bS, bSr),) = yield from galloc(1)
                for h in range(4):
                    mm(bS[:, h * 128:(h + 1) * 128], kdc[0:64, h * 128:(h + 1) * 128], wv_[0:64, h * 128:(h + 1) * 128], True, True,
                       [kdcr, wvr], [bSr])
                for h in range(4):
                    hs_ = slice(h * 128, (h + 1) * 128)
                    v_stt(s_dn[:, hs_], s_dn[:, hs_], egrow3[:, h, c * 64 + 63:c * 64 + 64], bS[:, hs_], ALU.mult, ALU.add,
                          ["s_dn", bSr, egr], ["s_dn"])
                gfree(bSr)
                if send:
                    state_io(tile, seq, sstart, send, s_dn, "s_dn", st_dn, o_dn, "end")
                else:
                    act(lambda e: e.copy(out=sbf, in_=s_dn[:]), ["s_dn"], [sbfr])
                dn_done[c] = True

            def dn_gate():
                def alloc():
                    ((bk, br),) = yield from galloc(1)
                    return bk, br
                g = gate_gen(NT, Wdg, wdgr, gate3, gater, alloc)
                for r in g:
                    if r is not None:
                        gfree(r)
                    yield

            gens = [(lambda c=c: dn_gen(c)) for c in range(NC)]
            interleave(gens, WDN, extra=dn_gate)
            osbr = [(osbr, c) for c in range(NC)]
            cv0, cv0r = cvs["cv"][0]
            head_norm(NT, osb3, osbr, gate3, gater, 2, mix3, mixr, 4,
                      scratch=(A.raw["xh"][:, 0:4 * NT], [(cvs["xhr"], j) for j in range(4)], cv0, cv0r))
            out_proj(tile, "odd_out", mix3, mixr, next_norm=5)

        def final_out(tile):
            NT = tile["NT"]
            A.reset()
            rmsnorm_f32_out(tile)

        def rmsnorm_f32_out(tile):
            NT = tile["NT"]
            sq, sqr = A.alloc("nsq", KT * NT, BF16)
            sq3 = sq.rearrange("p (k t) -> p k t", k=KT)
            yv, yr = A.alloc("yfm", KT * NT, F32)
            y3 = yv.rearrange("p (k t) -> p k t", k=KT)
            act(lambda e: e.activation(out=sq3, in_=x_sb[:, :, 0:NT], func=AF.Square), ["x"], [sqr])
            bk, br = P()
            for kt in range(KT):
                mm(bk[:, 0:NT], onesb[:], sq3[:, kt, :], kt == 0, kt == KT - 1, [sqr, "onesb"], [br])
            a_act(rt[:, 0, 0:NT], bk[:, 0:NT], AF.Ln, [br], ["rt0"], bias=EPS, scale=1.0 / D)
            a_act(rt[:, 1, 0:NT], rt[:, 0, 0:NT], AF.Exp, ["rt0"], ["rt1"], scale=-0.5)
            for kt in range(KT):
                v_stt(y3[:, kt, :], x_sb[:, kt, 0:NT], normw[:, 6, kt:kt + 1], rt[:, 1, 0:NT], ALU.mult, ALU.mult,
                      ["x", "normw", "rt1"], [(yr, kt)])
            for b in range(NT // 128):
                sl = b % 2
                for half in range(2):
                    bk, br = P()
                    for q in range(4):
                        kt = half * 4 + q
                        tr(bk[:, q * 128:(q + 1) * 128], y3[:, kt, b * 128:(b + 1) * 128], ident, [(yr, kt), "c128"], [br])
                    act(lambda e, bk=bk, half=half, sl=sl: e.copy(out=xin[:, sl, half * 512:(half + 1) * 512], in_=bk[:, :]),
                        [br], [("xin", sl)])
                if b * 128 < tile["ptok"]:
                    dstb = yp[tile["tok0"] + b * 128:tile["tok0"] + (b + 1) * 128, :]
                else:
                    dstb = ys[b * 128 - tile["ptok"]:(b + 1) * 128 - tile["ptok"], :]
                S.dma("pool", lambda e, sl=sl, dstb=dstb: e.dma_start(out=dstb, in_=xin[:, sl, :]),
                      "xin%d" % sl, reads=[("xin", sl)], writes=[])

        for tile in tiles:
            t0_ = (tile is tiles[0])
            load_x(tile)
            for mo_ in range(KT):
                prenorm(tile["NT"], 0, mo_)
            if t0_:
                cast_group(2)
            if stage >= 1:
                ffn(tile, 0, 0)
            if t0_:
                cast_group(3)
            if stage >= 2:
                even_mixer(tile)
            if t0_:
                cast_group(4)
            if stage >= 3:
                ffn(tile, 1, 0, next_norm=1)
            if t0_:
                cast_group(5)
            if stage >= 3:
                ffn(tile, 0, 1)
            if stage >= 4:
                odd_mixer(tile)
            if stage >= 5:
                ffn(tile, 1, 1)
            final_out(tile)
        assert W.consumed == len(W.units)
        S.wait_deps("pool", [v for k, v in S.dma_last.items() if not (k.startswith("w") or k.startswith("cast"))])

        S.emit({"pe": block.tensor, "act": block.scalar, "dve": block.vector, "pool": block.gpsimd, "sp": block.sync},
               eng_sems, dma_sems)
    return nc


_PROG_CACHE = {}


def kernel(x_prompt, x_sample, state_ret, state_hgrn, state_lru_h, state_lru_conv, state_dn, state_dn_conv,
           ffn1_norm, ffn1_w_in, ffn1_w_out, mix_norm, ffn2_norm, ffn2_w_in, ffn2_w_out, final_norm,
           even_w_in, even_w_out, ret_out_norm, hg_out_norm, hg_lb_logits, odd_w_in, odd_w_out,
           lru_conv_w, lru_conv_b, lru_w_a, lru_b_a, lru_w_x, lru_b_x, lru_lambda, dn_conv_w, dn_a_log,
           dn_dt_bias, dn_out_norm, _past_len=2048, _stage=99, _ncores=NCORES):
    f = lambda a: np.ascontiguousarray(np.asarray(a, dtype=np.float32))
    x_prompt = f(x_prompt)
    x_sample = f(x_sample)
    B, TP, _ = x_prompt.shape
    assert B == 4 and x_sample.shape[0] == 16 and x_sample.shape[1] == 64
    hc = host_consts(TP, _past_len)
    if (TP, _stage) not in _PROG_CACHE:
        _PROG_CACHE[(TP, _stage)] = build_program(TP, _stage)
    nc = _PROG_CACHE[(TP, _stage)]
    shared = {
        "ffn1_w_in": f(ffn1_w_in), "ffn2_w_in": f(ffn2_w_in), "ffn1_w_out": f(ffn1_w_out), "ffn2_w_out": f(ffn2_w_out),
        "even_w_in": f(even_w_in)[0], "even_w_out": f(even_w_out)[0], "odd_w_in": f(odd_w_in)[0], "odd_w_out": f(odd_w_out)[0],
        "norms": np.ascontiguousarray(np.concatenate([f(ffn1_norm), f(mix_norm), f(ffn2_norm), f(final_norm)[None]], 0)),
        "hnorm": np.ascontiguousarray(np.concatenate([f(ret_out_norm), f(hg_out_norm), f(dn_out_norm)], 0)),
        "hg_lb_logits": f(hg_lb_logits),
        "lru_vecs": np.ascontiguousarray(np.concatenate([f(lru_conv_w)[0], f(lru_conv_b), f(lru_b_a), f(lru_b_x), f(lru_lambda)], 0)),
        "lru_w_a": f(lru_w_a)[0], "lru_w_x": f(lru_w_x)[0],
        "dn_conv_w": f(dn_conv_w)[0],
        "dn_scal": np.ascontiguousarray(np.concatenate([f(dn_a_log), f(dn_dt_bias)], 0)),
        "c128": hc["c128"], "c64": hc["c64"], "rope": hc["rope"],
    }
    sr, sh, sd = f(state_ret)[0], f(state_hgrn)[0], f(state_dn)[0]
    slh, slc, sdc = f(state_lru_h)[0], f(state_lru_conv)[0], f(state_dn_conv)[0]
    in_maps = []
    zero_prompt = np.zeros_like(x_prompt[0])
    for c in range(NCORES):
        m = dict(shared)
        m["xp"] = x_prompt[PROMPT_OF_CORE[c]] if PROMPT_OF_CORE[c] is not None else zero_prompt
        m["xs"] = np.ascontiguousarray(x_sample[2 * c:2 * c + 2].reshape(128, D))
        m["st_ret"] = np.ascontiguousarray(sr[2 * c:2 * c + 2])
        m["st_hg"] = np.ascontiguousarray(sh[2 * c:2 * c + 2])
        m["st_dn"] = np.ascontiguousarray(sd[2 * c:2 * c + 2])
        m["st_lh"] = np.ascontiguousarray(slh[2 * c:2 * c + 2])
        m["st_lc"] = np.ascontiguousarray(slc[2 * c:2 * c + 2])
        m["st_dc"] = np.ascontiguousarray(sdc[2 * c:2 * c + 2])
        in_maps.append(m)
    res = run_bass_kernel_spmd(nc, in_maps[:_ncores], core_ids=list(range(_ncores)))
    R = list(res.results)
    while len(R) < NCORES:
        R.append(R[0])
    y_prompt = np.stack([R[c]["yp"] for c in CORE_OF_PROMPT], 0)
    y_sample = np.concatenate([R[c]["ys"].reshape(2, 64, D) for c in range(NCORES)], 0)

    def gp(name, shape):
        return np.stack([R[c][name].reshape(shape) for c in CORE_OF_PROMPT], 0)[None]

    def gs(name, shape):
        return np.concatenate([R[c][name].reshape((2,) + shape) for c in range(NCORES)], 0)[None]

    return (y_prompt, y_sample,
            gp("ret_p", (4, 128, 128)), gs("ret_s", (4, 128, 128)),
            gp("hg_p", (4, 128, 128)), gs("hg_s", (4, 128, 128)),
            gp("lh_p", (512,)), gs("lh_s", (512,)),
            gp("lc_p", (3, 512)), gs("lc_s", (3, 512)),
            gp("dn_p", (4, 128, 128)), gs("dn_s", (4, 128, 128)),
            gp("dc_p", (3, 1536)), gs("dc_s", (3, 1536)))
```

```python
import contextlib
import math
import numpy as np
import concourse.bass as bass
import concourse.mybir as mybir
from concourse.bass_utils import run_bass_kernel_spmd

F32 = mybir.dt.float32
BF16 = mybir.dt.bfloat16
F32R = mybir.dt.float32r
AF = mybir.ActivationFunctionType
ALU = mybir.AluOpType

D = 1024
KT = 8
FF = 2816
FT = 22
EPS = 1e-6
LRU_C = 8.0
NCORES = 8
PROMPT_OF_CORE = [0, 1, None, None, 2, 3, None, None]
CORE_OF_PROMPT = [0, 1, 4, 5]


class Op:
    __slots__ = ("eng", "fn", "waits", "signal", "idx", "dma_sem", "sig_count")

    def __init__(self, eng, fn):
        self.eng = eng
        self.fn = fn
        self.waits = []
        self.signal = False
        self.dma_sem = None
        self.sig_count = 0
        self.idx = 0


class Sched:
    def __init__(self):
        self.ops = {e: [] for e in ("pe", "act", "dve", "pool", "sp")}
        self.last_write = {}
        self.readers = {}
        self.waited = {e: {} for e in self.ops}
        self.dma_counts = {}
        self.dma_last = {}
        self.misc_ctr = {}
        self.inherit = {}

    def _need(self, op, dep, is_dma=False):
        if dep is None:
            return
        kind, key, val = dep
        if kind == "eng" and key == op.eng and key == "pe" and not is_dma:
            return
        w = self.waited[op.eng]
        k = (kind, key)
        if w.get(k, -1) >= val:
            return
        w[k] = val
        op.waits.append(dep)

    def _deps(self, op, reads, writes, is_dma=False):
        if self.inherit:
            for r in list(reads) + list(writes):
                for base in ((r, r[0]) if (isinstance(r, tuple) and len(r) == 2 and isinstance(r[0], tuple)) else (r,)):
                    for d in self.inherit.get(base, ()):
                        self._need(op, d, is_dma)
        for r in reads:
            self._need(op, self.last_write.get(r), is_dma)
        for r in writes:
            self._need(op, self.last_write.get(r), is_dma)
            for d in self.readers.get(r, ()):
                self._need(op, d, is_dma)

    def _commit(self, dep, reads, writes):
        for r in reads:
            self.readers.setdefault(r, []).append(dep)
        for r in writes:
            self.last_write[r] = dep
            self.readers[r] = []

    def op(self, eng, fn, reads=(), writes=()):
        psr = [r for r in reads if isinstance(r, tuple) and r[0] == "ps"]
        if psr:
            writes = list(writes) + [r for r in psr if r not in writes]
        o = Op(eng, fn)
        o.idx = len(self.ops[eng])
        self._deps(o, reads, writes)
        self.ops[eng].append(o)
        self._commit(("eng", eng, o.idx), reads, writes)
        return o

    NMISC = 24

    def dma(self, eng, fn, sem=None, reads=(), writes=()):
        o = Op(eng, fn)
        o.idx = len(self.ops[eng])
        if sem is None:
            mc = self.misc_ctr.get(eng, 0)
            sem = "m%s%d" % (eng, mc % self.NMISC)
            self.misc_ctr[eng] = mc + 1
        if not sem.startswith("cast") and not sem.startswith("w"):
            self._need(o, self.dma_last.get(sem), True)
        self._deps(o, reads, writes, True)
        c = self.dma_counts.get(sem, 0) + 1
        self.dma_counts[sem] = c
        o.dma_sem = sem
        self.ops[eng].append(o)
        dep = ("dma", sem, 16 * c)
        self.dma_last[sem] = dep
        self._commit(dep, reads, writes)
        return o

    def barrier(self):
        lasts = []
        for e in ("pe", "act", "dve", "pool"):
            if self.ops[e]:
                for o in reversed(self.ops[e]):
                    if o.fn is not None and o.dma_sem is None:
                        lasts.append(("eng", e, o.idx))
                        break
        dmas = [v for k, v in self.dma_last.items() if not (k.startswith('w') or k.startswith('cast'))]
        for e in ("pe", "act", "dve", "pool"):
            o = Op(e, None)
            o.idx = len(self.ops[e])
            for d in lasts + dmas:
                self._need(o, d)
            self.ops[e].append(o)

    def wait_deps(self, eng, deps):
        o = Op(eng, None)
        o.idx = len(self.ops[eng])
        for d in deps:
            self._need(o, d, True)
        self.ops[eng].append(o)

    def finalize(self):
        for e, lst in self.ops.items():
            for o in lst:
                for kind, key, val in o.waits:
                    if kind == "eng":
                        self.ops[key][val].signal = True
        for e, lst in self.ops.items():
            c = 0
            for o in lst:
                if o.signal:
                    c += 1
                o.sig_count = c

    def emit(self, regs, eng_sems, dma_sems):
        self.finalize()
        for e, reg in regs.items():
            lst = self.ops[e]
            if not lst:
                continue

            def body(engine, lst=lst, e=e):
                for o in lst:
                    for kind, key, val in o.waits:
                        if kind == "eng":
                            engine.wait_ge(eng_sems[key], self.ops[key][val].sig_count)
                        else:
                            engine.wait_ge(dma_sems[key], val)
                    if o.fn is None:
                        continue
                    ins = o.fn(engine)
                    if o.dma_sem is not None:
                        ins.then_inc(dma_sems[o.dma_sem], 16)
                    elif o.signal:
                        ins.then_inc(eng_sems[e], 1)

            reg(body)


def host_consts(TP, past_len):
    c = {}
    g = np.array([np.log1p(-2.0 ** (-5.0 - h)) for h in range(4)], np.float64)
    p = np.arange(64, dtype=np.float64)
    c128 = {}
    c128["ident"] = np.eye(128)
    rm = np.zeros((128, 128))
    for d in range(64):
        rm[d + 64, d] = -1.0
        rm[d, d + 64] = 1.0
    c128["rmat"] = rm
    qd = np.stack([(128.0 ** -0.5) * np.exp(g[h] * (p + 1.0)) for h in range(4)], 0)
    c128["qdec"] = np.broadcast_to(qd.reshape(1, 256), (128, 256))
    rs = np.ones(512)
    rs[0::64] = 0.0
    c128["reset"] = np.broadcast_to(rs.reshape(1, 512), (128, 512))
    c128["ones"] = np.ones((128, 128))
    names128 = ["ident", "rmat", "qdec", "reset", "ones"]
    c["c128"] = np.concatenate([np.asarray(c128[k], np.float64) for k in names128], 1).astype(np.float32)
    off = 0
    c["off128"] = {}
    for k in names128:
        c["off128"][k] = (off, c128[k].shape[1])
        off += c128[k].shape[1]
    c64 = {}
    s = p.reshape(64, 1)
    t = p.reshape(1, 64)
    c64["retmask"] = np.concatenate([np.exp(g[h] * (np.abs(t - s) - (t + 1.0))) for h in range(4)], 1)
    c64["kdec"] = np.concatenate([np.broadcast_to(np.exp(g[h] * (63.0 - s)), (64, 128)) for h in range(4)], 1)
    incl = (s <= t).astype(np.float64)
    strict = (s < t).astype(np.float64)
    c64["incl4"] = np.tile(incl, (1, 4))
    c64["neg4"] = np.tile((1.0 - incl) * -30000.0, (1, 4))
    c64["strict4"] = np.tile(strict, (1, 4))
    c64["i4"] = np.tile(np.eye(64), (1, 4))
    c64["U"] = incl
    c64["Urev"] = (s > t).astype(np.float64)
    c64["ones"] = np.ones((64, 128))
    names64 = ["retmask", "kdec", "incl4", "neg4", "strict4", "i4", "U", "Urev", "ones"]
    c["c64"] = np.concatenate([c64[k] for k in names64], 1).astype(np.float32)
    off = 0
    c["off64"] = {}
    for k in names64:
        c["off64"][k] = (off, c64[k].shape[1])
        off += c64[k].shape[1]
    c["sdec"] = [float(np.exp(g[h] * 64.0)) for h in range(4)]
    pos = np.concatenate([np.arange(TP), past_len + np.arange(64), past_len + np.arange(64)]).astype(np.float64)
    half = 64
    freq = 10000.0 ** (-np.arange(half, dtype=np.float64) / half)
    ang = (pos.astype(np.float32)[None, :] * freq.astype(np.float32)[:, None]).astype(np.float32).astype(np.float64)
    cos = np.cos(ang)
    sin = np.sin(ang)
    tab = np.zeros((128, 2, TP + 128), np.float32)
    tab[0:64, 0] = cos
    tab[64:128, 0] = cos
    tab[0:64, 1] = sin
    tab[64:128, 1] = sin
    c["rope"] = tab
    return c


def build_program(TP, stage=99):
    assert TP % 512 == 0
    hc = host_consts(TP, 0)
    off128, off64, SDEC = hc["off128"], hc["off64"], hc["sdec"]
    C128W = hc["c128"].shape[1]
    C64W = hc["c64"].shape[1]

    nc = bass.Bass("TRN2", target_bir_lowering=False)

    def din(name, shape, dt=F32):
        return nc.dram_tensor(name, list(shape), dt, kind="ExternalInput").ap()

    def dout(name, shape):
        return nc.dram_tensor(name, list(shape), F32, kind="ExternalOutput").ap()

    def dscr(name, shape, dt):
        return nc.dram_tensor(name, list(shape), dt, kind="Internal").ap()

    xp = din("xp", [TP, D])
    xs = din("xs", [128, D])
    st_ret = din("st_ret", [2, 4, 128, 128])
    st_hg = din("st_hg", [2, 4, 128, 128])
    st_dn = din("st_dn", [2, 4, 128, 128])
    st_lh = din("st_lh", [2, 512])
    st_lc = din("st_lc", [2, 3, 512])
    st_dc = din("st_dc", [2, 3, 1536])
    w_ffn_in = [din("ffn1_w_in", [2, D, 2 * FF]), din("ffn2_w_in", [2, D, 2 * FF])]
    w_ffn_out = [din("ffn1_w_out", [2, FF, D]), din("ffn2_w_out", [2, FF, D])]
    w_even_in = din("even_w_in", [D, 4096])
    w_even_out = din("even_w_out", [D, D])
    w_odd_in = din("odd_w_in", [D, 3080])
    w_odd_out = din("odd_w_out", [D, D])
    norms_d = din("norms", [7, D])
    hnorm_d = din("hnorm", [3, 512])
    lb_logits = din("hg_lb_logits", [2, 512])
    lru_vecs = din("lru_vecs", [8, 512])
    lru_wa = din("lru_w_a", [8, 64, 64])
    lru_wx = din("lru_w_x", [8, 64, 64])
    dn_conv_w = din("dn_conv_w", [4, 1536])
    dn_scal = din("dn_scal", [2, 4])
    c128_d = din("c128", [128, C128W])
    c64_d = din("c64", [64, C64W])
    rope_d = din("rope", [128, 2, TP + 128])

    yp = dout("yp", [TP, D])
    ys = dout("ys", [128, D])
    o_ret = [dout("ret_p", [4, 128, 128]), dout("ret_s", [2, 4, 128, 128])]
    o_hg = [dout("hg_p", [4, 128, 128]), dout("hg_s", [2, 4, 128, 128])]
    o_dn = [dout("dn_p", [4, 128, 128]), dout("dn_s", [2, 4, 128, 128])]
    o_lh = [dout("lh_p", [1, 512]), dout("lh_s", [2, 512])]
    o_lc = [dout("lc_p", [1, 3, 512]), dout("lc_s", [2, 3, 512])]
    o_dc = [dout("dc_p", [1, 3, 1536]), dout("dc_s", [2, 3, 1536])]

    wb_ffn_in = [dscr("b_ffn1_w_in", [2, D, 2 * FF], BF16), dscr("b_ffn2_w_in", [2, D, 2 * FF], BF16)]
    wb_ffn_out = [dscr("b_ffn1_w_out", [2, FF, D], BF16), dscr("b_ffn2_w_out", [2, FF, D], BF16)]
    wb_even_in = dscr("b_even_w_in", [D, 4096], BF16)
    wb_even_out = dscr("b_even_w_out", [D, D], BF16)
    wb_odd_in = dscr("b_odd_w_in", [D, 3080], BF16)
    wb_odd_out = dscr("b_odd_w_out", [D, D], BF16)

    S = Sched()
    es = contextlib.ExitStack()
    with es:
        def sb(name, shape, dt):
            return es.enter_context(nc.sbuf_tensor("sb_" + name, list(shape), dt))

        x_sb = sb("x_sb", [128, KT, 512], F32)
        hn = sb("hn", [128, KT, 512], BF16)
        NSLOT = 3
        SLOTSZ = FT * 256
        wring = sb("wring", [128, NSLOT, SLOTSZ], BF16)
        xin = sb("xin", [128, 2, D], F32)
        c128 = sb("c128", [128, C128W], F32)
        c64 = sb("c64", [64, C64W], F32)
        identb = sb("identb", [128, 128], BF16)
        rmatb = sb("rmatb", [128, 128], BF16)
        onesb = sb("onesb", [128, 128], BF16)
        normw = sb("normw", [128, 7, KT], F32)
        hgain = sb("hgain", [128, 3, 4], F32)
        lbt = sb("lbt", [128, 4, 4], F32)
        lruv = sb("lruv", [128, 8, 4], F32)
        lrud = sb("lrud", [128, 4, 4], F32)
        dncw = sb("dncw", [128, 4, 12], F32)
        bd = sb("bd", [128, 2, 4, 128], BF16)
        dnb = sb("dnb", [64, 2, 8, 4], F32)
        dnraw = sb("dnraw", [64, 2, 4], F32)
        s_ret = sb("s_ret", [128, 512], F32)
        s_hg = sb("s_hg", [128, 512], F32)
        s_dn = sb("s_dn", [128, 512], F32)
        hstate = sb("hstate", [128, 3, 4], F32)
        hcar = sb("hcar", [128, 16, 3], F32)
        rt = sb("rt", [128, 2, 512], F32)
        ARENA = 51800
        nmr = sb("nmr", [64, 3, 5, 256], F32)
        arena = sb("arena", [128, ARENA], BF16)

        banks = [es.enter_context(nc.psum_tensor("ps%d" % i, [128, 512], F32)) for i in range(8)]
        eng_sems = {e: es.enter_context(nc.semaphore("sem_" + e)) for e in ("pe", "act", "dve", "pool")}
        dma_names = ["w%d" % i for i in range(NSLOT)] + ["xin0", "xin1"] + ["cast%d" % i for i in range(6)] + ["m%s%d" % (q, i) for q in ("sp", "pool") for i in range(Sched.NMISC)]
        dma_sems = {k: es.enter_context(nc.semaphore("dsem_" + k)) for k in dma_names}
        block = es.enter_context(nc.Block())
        es.enter_context(nc.allow_non_contiguous_dma(reason="tiny per-channel vectors"))

        def C128(k):
            o, n = off128[k]
            return c128[:, o:o + n]

        def C64(k):
            o, n = off64[k]
            return c64[:, o:o + n]

        ident = C128("ident")

        bank_ctr = [0]

        sub_ctr = {}

        def P(sub=None):
            if sub is None:
                i = bank_ctr[0] % 8
                bank_ctr[0] += 1
                return banks[i], ("ps", i)
            k, n = sub
            mine = [b for b in range(8) if b % n == k]
            j = sub_ctr.get(sub, 0)
            sub_ctr[sub] = j + 1
            i = mine[j % len(mine)]
            return banks[i], ("ps", i)

        class Arena:
            def __init__(self):
                self.off = 0
                self.gen = 0
                self.raw = {}
                self.live = []

            def reset(self):
                self.off = 0
                self.gen += 1

            def _inherit(self, lo, hi, newres):
                deps = []
                keep = []
                for (a, b, r) in self.live:
                    if r[1] == self.gen or b <= lo or a >= hi:
                        keep.append((a, b, r))
                        continue
                    for k, d in list(S.last_write.items()):
                        if k == r or (isinstance(k, tuple) and len(k) == 2 and k[0] == r):
                            if d is not None:
                                deps.append(d)
                            deps.extend(S.readers.get(k, ()))
                    if a < lo:
                        keep.append((a, lo, r))
                    if b > hi:
                        keep.append((hi, b, r))
                self.live = keep
                if deps:
                    S.inherit.setdefault(newres, []).extend(deps)

            def alloc(self, name, n, dt):
                if dt == F32:
                    ne = 2 * n
                else:
                    ne = n
                ne = (ne + 15) // 16 * 16
                assert self.off + ne <= ARENA, (name, self.off, ne)
                v = arena[:, self.off:self.off + ne]
                res = ("ar", self.gen, name)
                self._inherit(self.off, self.off + ne, res)
                self.live.append((self.off, self.off + ne, res))
                self.raw[name] = v
                self.off += ne
                if dt == F32:
                    v = v.bitcast(F32)[:, 0:n]
                else:
                    v = v[:, 0:n]
                return v, res

        A = Arena()

        def pe(fn, r=(), w=()):
            return S.op("pe", fn, r, w)

        def act(fn, r=(), w=()):
            return S.op("act", fn, r, w)

        def dve(fn, r=(), w=()):
            return S.op("dve", fn, r, w)

        def pool(fn, r=(), w=()):
            return S.op("pool", fn, r, w)

        def mm(out, lhsT, rhs, start, stop, r, w):
            return pe(lambda e: e.matmul(out, lhsT=lhsT, rhs=rhs, start=start, stop=stop), r, w)

        def tr(out, in_, idn, r, w):
            return pe(lambda e: e.transpose(out=out, in_=in_, identity=idn), r, w)

        def a_act(out, in_, func, r, w, bias=None, scale=None):
            kw = {}
            if bias is not None:
                kw["bias"] = bias
            if scale is not None:
                kw["scale"] = scale
            return act(lambda e: e.activation(out=out, in_=in_, func=func, **kw), r, w)

        def v_tt(out, in0, in1, op, r, w, eng="dve"):
            return S.op(eng, lambda e: e.tensor_tensor(out=out, in0=in0, in1=in1, op=op), r, w)

        def v_ts(out, in0, s1, s2, op0, op1, r, w, eng="dve"):
            if op1 is None:
                return S.op(eng, lambda e: e.tensor_scalar(out=out, in0=in0, scalar1=s1, scalar2=None, op0=op0), r, w)
            return S.op(eng, lambda e: e.tensor_scalar(out=out, in0=in0, scalar1=s1, scalar2=s2, op0=op0, op1=op1), r, w)

        def v_stt(out, in0, sc, in1, op0, op1, r, w, eng="dve"):
            return S.op(eng, lambda e: e.scalar_tensor_tensor(out=out, in0=in0, scalar=sc, in1=in1, op0=op0, op1=op1), r, w)

        def v_copy(out, in_, r, w, eng="dve"):
            return S.op(eng, lambda e: e.tensor_copy(out=out, in_=in_), r, w)

        def v_memset(ap, val, w, eng="dve"):
            return S.op(eng, lambda e: e.memset(ap, val), (), w)

        class WStream:
            def __init__(self):
                self.units = []
                self.issued = 0
                self.consumed = 0

            def plan(self, tag, src, shape):
                self.units.append((tag, src, shape))

            @staticmethod
            def group_of(tag):
                if tag[0].startswith("ffn"):
                    which, layer = tag[1], tag[2]
                    return {(0, 0): 0, (1, 0): 2, (0, 1): 3, (1, 1): 5}[(which, layer)]
                return 1 if tag[0].startswith("even") else 4

            def _issue(self):
                i = self.issued
                tag, src, shape = self.units[i]
                S.wait_deps("sp", [cast_dep[self.group_of(tag)]])
                slot = i % NSLOT
                n = 1
                for d in shape:
                    n *= d
                dst = wring[:, slot, 0:n]
                if len(shape) == 2:
                    dst = dst.rearrange("p (a b) -> p a b", a=shape[0])
                srcs = src if isinstance(src, list) else [src]
                if len(srcs) == 1:
                    pairs = [(dst, srcs[0])]
                else:
                    g = len(srcs)
                    d4 = dst.rearrange("p a (g c) -> p a g c", g=g)
                    pairs = [(d4[:, :, k, :], srcs[k]) for k in range(g)]
                for dd, ss in pairs:
                    S.dma("sp", lambda e, dd=dd, ss=ss: e.dma_start(out=dd, in_=ss), "w%d" % slot,
                          reads=[], writes=[("wslot", slot)])
                self.issued += 1

            def next(self, tag):
                while self.issued < min(len(self.units), self.consumed + NSLOT):
                    self._issue()
                i = self.consumed
                t, src, shape = self.units[i]
                assert t == tag, (t, tag)
                slot = i % NSLOT
                n = 1
                for d in shape:
                    n *= d
                v = wring[:, slot, 0:n]
                if len(shape) == 2:
                    v = v.rearrange("p (a b) -> p a b", a=shape[0])
                self.consumed += 1
                return v, ("wslot", slot)

        W = WStream()

        def in_view(wap, c0, ncol):
            return wap.rearrange("(kt p) c -> p kt c", p=128)[:, :, c0:c0 + ncol]

        if TP >= 2048:
            sizes = [512] * (TP // 512 - 2) + [384, 384, 256]
        else:
            sizes = [TP // 2, TP // 2]
        assert sum(sizes) == TP and all(z % 128 == 0 for z in sizes) and sizes[-1] <= 384
        tiles = []
        off = 0
        for i, sz in enumerate(sizes):
            lastp = (i == len(sizes) - 1)
            if not lastp:
                segs = [dict(seq=0, L=sz, start=(i == 0), end=False, prev_same=False)]
                NT_ = sz
            else:
                npc = sz // 64
                segs = [dict(seq=0, L=64, start=(i == 0 and k == 0), end=(k == npc - 1), prev_same=(k > 0)) for k in range(npc)]
                segs += [dict(seq=1, L=64, start=True, end=True, prev_same=False), dict(seq=2, L=64, start=True, end=True, prev_same=False)]
                NT_ = sz + 128
            tiles.append(dict(kind=("m" if lastp else "p"), tok0=off, ptok=sz, NT=NT_, segs=segs, first=(i == 0), last=lastp))
            off += sz

        def plan_ffn(which, layer):
            wi = wb_ffn_in[which][layer]
            wo = wb_ffn_out[which][layer]
            for u in range(11):
                w4 = wi.rearrange("(kt p) (g c) -> p kt g c", p=128, g=2)
                src = [w4[:, :, 0, u * 256:(u + 1) * 256], w4[:, :, 1, u * 256:(u + 1) * 256]]
                W.plan(("ffn_in", which, layer, u), src, (KT, 512))
            for u in range(4):
                src = wo.rearrange("(kt p) c -> p kt c", p=128)[:, :, u * 256:(u + 1) * 256]
                W.plan(("ffn_out", which, layer, u), src, (FT, 256))

        def plan_tile():
            if stage >= 1:
                plan_ffn(0, 0)
            if stage >= 2:
                for u in range(8):
                    W.plan(("even_in", u), in_view(wb_even_in, u * 512, 512), (KT, 512))
                for u in range(4):
                    W.plan(("even_out", u), in_view(wb_even_out, u * 256, 256), (KT, 256))
            if stage >= 3:
                plan_ffn(1, 0)
                plan_ffn(0, 1)
            if stage >= 4:
                W.plan(("odd_in", "bda"), in_view(wb_odd_in, 3072, 8), (KT, 8))
                for u in range(6):
                    W.plan(("odd_in", u), in_view(wb_odd_in, u * 512, 512), (KT, 512))
                for u in range(4):
                    W.plan(("odd_out", u), in_view(wb_odd_out, u * 256, 256), (KT, 256))
            if stage >= 5:
                plan_ffn(1, 1)

        for _ in tiles:
            plan_tile()

        cast_dep = {}

        def cast_w(src2d, dst2d, rows, grp):
            r0 = 0
            while r0 < rows:
                rr = min(256, rows - r0)
                S.dma("pool", lambda e, a=dst2d[r0:r0 + rr, :], b=src2d[r0:r0 + rr, :]: e.dma_start(out=a, in_=b),
                      "cast%d" % grp, reads=[], writes=[("wcast", grp)])
                r0 += rr
            cast_dep[grp] = S.dma_last["cast%d" % grp]

        def cast_group(grp):
            if grp == 0:
                cast_w(w_ffn_in[0][0], wb_ffn_in[0][0], D, 0)
                cast_w(w_ffn_out[0][0], wb_ffn_out[0][0], FF, 0)
            elif grp == 1:
                cast_w(w_even_in, wb_even_in, D, 1)
                cast_w(w_even_out, wb_even_out, D, 1)
            elif grp == 2:
                cast_w(w_ffn_in[1][0], wb_ffn_in[1][0], D, 2)
                cast_w(w_ffn_out[1][0], wb_ffn_out[1][0], FF, 2)
            elif grp == 3:
                cast_w(w_ffn_in[0][1], wb_ffn_in[0][1], D, 3)
                cast_w(w_ffn_out[0][1], wb_ffn_out[0][1], FF, 3)
            elif grp == 4:
                cast_w(w_odd_in, wb_odd_in, D, 4)
                cast_w(w_odd_out, wb_odd_out, D, 4)
            elif grp == 5:
                cast_w(w_ffn_in[1][1], wb_ffn_in[1][1], D, 5)
                cast_w(w_ffn_out[1][1], wb_ffn_out[1][1], FF, 5)

        S.dma("sp", lambda e: e.dma_start(out=c128[:], in_=c128_d), None, writes=["c128"])
        S.dma("sp", lambda e: e.dma_start(out=c64[:], in_=c64_d), None, writes=["c64"])
        S.dma("sp", lambda e: e.dma_start(out=normw[:], in_=norms_d.rearrange("n (kt p) -> p n kt", p=128)), None, writes=["normw"])
        S.dma("sp", lambda e: e.dma_start(out=hgain[:], in_=hnorm_d.rearrange("n (h p) -> p n h", p=128)), None, writes=["hgain"])
        S.dma("sp", lambda e: e.dma_start(out=lbt[:, 0:2, :], in_=lb_logits.rearrange("n (h p) -> p n h", p=128)), None, writes=["lbt"])
        S.dma("sp", lambda e: e.dma_start(out=lruv[:], in_=lru_vecs.rearrange("n (j p) -> p n j", p=128)), None, writes=["lruv"])
        S.dma("sp", lambda e: e.dma_start(out=dncw[:], in_=dn_conv_w.rearrange("n (j p) -> p n j", p=128)), None, writes=["dncw"])
        S.dma("sp", lambda e: e.dma_start(out=dnraw[:], in_=dn_scal.rearrange("(o n) h -> o n h", o=1).to_broadcast([64, 2, 4])), None, writes=["dnraw"])
        bdst_v, _ = A.alloc("bdst", 2 * 4 * 128, F32)
        bdst = bdst_v.rearrange("p (g j c) -> p g j c", g=2, j=4)
        pool(lambda e: e.memset(bdst_v, 0.0), (), ["bdst"])
        for gi, wsrc in enumerate((lru_wa, lru_wx)):
            for n in range(8):
                j, hh = n // 2, n % 2
                S.dma("sp", lambda e, gi=gi, n=n, j=j, hh=hh, wsrc=wsrc: e.dma_start(
                    out=bdst[hh * 64:(hh + 1) * 64, gi, j, hh * 64:(hh + 1) * 64], in_=wsrc[n]), None,
                    reads=[], writes=["bdst"])
        cast_group(0)
        cast_group(1)

        v_copy(identb[:], ident, ["c128"], ["identb"])
        v_copy(rmatb[:], C128("rmat"), ["c128"], ["rmatb"])
        v_copy(onesb[:], C128("ones"), ["c128"], ["onesb"])
        v_copy(bd[:], bdst, ["bdst"], ["bd"])
        v_tt(lbt[:, 2, :], lbt[:, 0, :], lbt[:, 1, :], ALU.subtract, ["lbt"], ["lbt"])
        a_act(lbt[:, 2, :], lbt[:, 2, :], AF.Sigmoid, ["lbt"], ["lbt"])
        v_ts(lbt[:, 3, :], lbt[:, 2, :], -1.0, 1.0, ALU.mult, ALU.add, ["lbt"], ["lbt"])
        a_act(lrud[:, 2, :], lruv[:, 7, :], AF.Exp, ["lruv"], ["lrud"], scale=-1.0)
        a_act(lrud[:, 3, :], lrud[:, 2, :], AF.Ln, ["lrud"], ["lrud"], bias=1.0)
        v_ts(lrud[:, 0, :], lrud[:, 3, :], -LRU_C, None, ALU.mult, None, ["lrud"], ["lrud"])
        v_ts(lrud[:, 1, :], lrud[:, 3, :], -2.0 * LRU_C, None, ALU.mult, None, ["lrud"], ["lrud"])
        a_act(dnraw[:, 0, :], dnraw[:, 0, :], AF.Exp, ["dnraw"], ["dnraw"])
        v_ts(dnraw[:, 0, :], dnraw[:, 0, :], -1.0, None, ALU.mult, None, ["dnraw"], ["dnraw"])
        for cc in range(8):
            v_copy(dnb[:, 0, cc, :], dnraw[:, 1, :], ["dnraw"], ["dnb"])
            v_copy(dnb[:, 1, cc, :], dnraw[:, 0, :], ["dnraw"], ["dnb"])

        def interleave(gen_fns, width, extra=None):
            active = []
            nxt = 0
            ex = extra() if extra is not None else None
            while active or nxt < len(gen_fns) or ex is not None:
                while len(active) < width and nxt < len(gen_fns):
                    active.append(gen_fns[nxt]())
                    nxt += 1
                for g in list(active):
                    try:
                        next(g)
                    except StopIteration:
                        active.remove(g)
                if ex is not None:
                    try:
                        next(ex)
                    except StopIteration:
                        ex = None

        def load_x(tile):
            NT = tile["NT"]
            for b in range(NT // 128):
                sl = b % 2
                if b * 128 < tile["ptok"]:
                    srcb = xp[tile["tok0"] + b * 128:tile["tok0"] + (b + 1) * 128, :]
                else:
                    srcb = xs[b * 128 - tile["ptok"]:(b + 1) * 128 - tile["ptok"], :]
                S.dma("pool", lambda e, sl=sl, srcb=srcb: e.dma_start(out=xin[:, sl, :], in_=srcb),
                      "xin%d" % sl, reads=[], writes=[("xin", sl)])
                for half in range(2):
                    bk, br = P()
                    for q in range(4):
                        kt = half * 4 + q
                        tr(bk[:, q * 128:(q + 1) * 128], xin[:, sl, kt * 128:(kt + 1) * 128], ident, [("xin", sl), "c128"], [br])
                    act(lambda e, bk=bk, half=half, b=b: e.copy(
                        out=x_sb[:, half * 4:half * 4 + 4, b * 128:(b + 1) * 128],
                        in_=bk[:, :].rearrange("p (q t) -> p q t", q=4)), [br], ["x"])

        def rmsnorm(NT, nidx):
            sq, sqr = A.alloc("nsq", KT * NT, BF16)
            sq3 = sq.rearrange("p (k t) -> p k t", k=KT)
            act(lambda e: e.activation(out=sq3, in_=x_sb[:, :, 0:NT], func=AF.Square), ["x"], [sqr])
            bk, br = P()
            for kt in range(KT):
                mm(bk[:, 0:NT], onesb[:], sq3[:, kt, :], kt == 0, kt == KT - 1, [sqr, "onesb"], [br])
            a_act(rt[:, 0, 0:NT], bk[:, 0:NT], AF.Ln, [br], ["rt0"], bias=EPS, scale=1.0 / D)
            a_act(rt[:, 1, 0:NT], rt[:, 0, 0:NT], AF.Exp, ["rt0"], ["rt1"], scale=-0.5)
            for kt in range(KT):
                v_stt(hn[:, kt, 0:NT], x_sb[:, kt, 0:NT], normw[:, nidx, kt:kt + 1], rt[:, 1, 0:NT], ALU.mult, ALU.mult,
                      ["x", "normw", "rt1"], ["hn"])

        def rstd_only(NT):
            sq, sqr = A.alloc("nsq", KT * NT, BF16)
            sq3 = sq.rearrange("p (k t) -> p k t", k=KT)
            act(lambda e: e.activation(out=sq3, in_=x_sb[:, :, 0:NT], func=AF.Square), ["x"], [sqr])
            bk, br = P()
            for kt in range(KT):
                mm(bk[:, 0:NT], onesb[:], sq3[:, kt, :], kt == 0, kt == KT - 1, [sqr, "onesb"], [br])
            a_act(rt[:, 0, 0:NT], bk[:, 0:NT], AF.Ln, [br], ["rt0"], bias=EPS, scale=1.0 / D)
            a_act(rt[:, 1, 0:NT], rt[:, 0, 0:NT], AF.Exp, ["rt0"], ["rt1"], scale=-0.5)

        def prenorm(NT, nidx, mo):
            v_ts(hn[:, mo, 0:NT], x_sb[:, mo, 0:NT], normw[:, nidx, mo:mo + 1], None, ALU.mult, None, ["x", "normw"], ["hn"])

        def ffn(tile, which, layer, next_norm=None):
            NT = tile["NT"]
            A.reset()
            hid, hidr = A.alloc("hid", FT * NT, BF16)
            hid3 = hid.rearrange("p (m t) -> p m t", m=FT)
            sg = [A.alloc("sg%d" % i, NT, F32) for i in range(2)]
            su = [A.alloc("su%d" % i, NT, F32) for i in range(2)]
            for u in range(11):
                Wv, wr = W.next(("ffn_in", which, layer, u))
                for mi in range(2):
                    m = 2 * u + mi
                    bg, bgr = P()
                    for kt in range(KT):
                        mm(bg[:, 0:NT], Wv[:, kt, mi * 128:(mi + 1) * 128], hn[:, kt, 0:NT], kt == 0, kt == KT - 1, [wr, "hn"], [bgr])
                    bu, bur = P()
                    for kt in range(KT):
                        mm(bu[:, 0:NT], Wv[:, kt, 256 + mi * 128:256 + (mi + 1) * 128], hn[:, kt, 0:NT], kt == 0, kt == KT - 1,
                           [wr, "hn"], [bur])
                    if m == 0:
                        rstd_only(NT)
                    sgv, sgr = sg[m % 2]
                    suv, sur = su[m % 2]
                    v_tt(sgv, bg[:, 0:NT], rt[:, 1, 0:NT], ALU.mult, [bgr, "rt1"], [sgr])
                    a_act(sgv, sgv, AF.Silu, [sgr], [sgr])
                    v_tt(suv, bu[:, 0:NT], rt[:, 1, 0:NT], ALU.mult, [bur, "rt1"], [sur])
                    v_tt(hid3[:, m, :], sgv, suv, ALU.mult, [sgr, sur], [(hidr, m)])
            for u in range(4):
                Wv, wr = W.next(("ffn_out", which, layer, u))
                for mi in range(2):
                    mo = 2 * u + mi
                    bk, br = P()
                    for kt in range(FT):
                        mm(bk[:, 0:NT], Wv[:, kt, mi * 128:(mi + 1) * 128], hid3[:, kt, :], kt == 0, kt == FT - 1,
                           [wr, (hidr, kt)], [br])
                    v_stt(x_sb[:, mo, 0:NT], bk[:, 0:NT], 0.5, x_sb[:, mo, 0:NT], ALU.mult, ALU.add, [br, "x"], ["x"])
                    if next_norm is not None:
                        prenorm(NT, next_norm, mo)

        def out_proj(tile, tagname, mix3, mixr, next_norm=None):
            NT = tile["NT"]
            for u in range(4):
                Wv, wr = W.next((tagname, u))
                for mi in range(2):
                    mo = 2 * u + mi
                    bk, br = P()
                    for kt in range(KT):
                        mm(bk[:, 0:NT], Wv[:, kt, mi * 128:(mi + 1) * 128], mix3[:, kt, :], kt == 0, kt == KT - 1, [wr, mixr], [br])
                    v_tt(x_sb[:, mo, 0:NT], bk[:, 0:NT], x_sb[:, mo, 0:NT], ALU.add, [br, "x"], ["x"])
                    if next_norm is not None:
                        prenorm(NT, next_norm, mo)

        def head_norm(NT, osb3, osbr, gate3, gater, gidx, mix3, mixr, koff, scratch=None):
            if scratch is None:
                sqb, sqbr = A.alloc("hsq%d" % koff, 4 * NT, BF16)
                tm, tmr = A.alloc("htm%d" % koff, NT, F32)
                sqbl = [sqbr]
            else:
                sqb, sqbl, tm, tmr = scratch
            sqb3 = sqb.rearrange("p (h t) -> p h t", h=4)
            osbl = osbr if isinstance(osbr, list) else [osbr]
            act(lambda e: e.activation(out=sqb3, in_=osb3, func=AF.Square), osbl, sqbl)
            for h in range(4):
                bk, br = P()
                mm(bk[:, 0:NT], onesb[:], sqb3[:, h, :], True, True, sqbl + ["onesb"], [br])
                a_act(rt[:, 0, 0:NT], bk[:, 0:NT], AF.Ln, [br], ["rt0"], bias=EPS, scale=1.0 / 128.0)
                a_act(rt[:, 1, 0:NT], rt[:, 0, 0:NT], AF.Exp, ["rt0"], ["rt1"], scale=-0.5)
                v_tt(tm, osb3[:, h, :], rt[:, 1, 0:NT], ALU.mult, osbl + ["rt1"], [tmr])
                v_stt(mix3[:, koff + h, :], tm, hgain[:, gidx, h:h + 1], gate3[:, h, :], ALU.mult, ALU.mult,
                      [tmr, "hgain", gater], [mixr])

        def seg_of_chunk(tile, c):
            if tile["kind"] == "p":
                return 0, 0, (c == 0), (c == tile["NT"] // 64 - 1)
            sg = tile["segs"][c]
            if sg["seq"] == 0:
                return 0, c, (c == 0), sg["end"]
            return sg["seq"], c, True, True

        def state_io(tile, seq, start, end, s32, s32r, st_in, o_list, when):
            if when == "start":
                if seq == 0:
                    if tile["first"]:
                        v_memset(s32[:], 0.0, [s32r], eng="pool")
                else:
                    S.dma("pool", lambda e: e.dma_start(out=s32[:].rearrange("p (h e) -> p h e", h=4),
                                                        in_=st_in[seq - 1].rearrange("h d e -> d h e")),
                          None, reads=[], writes=[s32r])
            else:
                if seq == 0:
                    if tile["last"]:
                        S.dma("pool", lambda e: e.dma_start(out=o_list[0].rearrange("h d e -> d h e"),
                                                            in_=s32[:].rearrange("p (h e) -> p h e", h=4)),
                              None, reads=[s32r], writes=[])
                else:
                    S.dma("pool", lambda e: e.dma_start(out=o_list[1][seq - 1].rearrange("h d e -> d h e"),
                                                        in_=s32[:].rearrange("p (h e) -> p h e", h=4)),
                          None, reads=[s32r], writes=[])

        def linattn_core(tile, kind, qF, qFr, kF, kFr, kS, kSr, vtm, vtmr, osb3, osbr, s32, s32r, st_in, o_list,
                         ebend=None, ebendr=None, extra=None):
            NT = tile["NT"]
            NC = NT // 64
            sbf, sbfr = A.alloc("sbf_" + kind, (NC + 1) * 512, BF16)
            sbf3 = sbf.rearrange("p (c n) -> p c n", c=NC + 1)
            WD = 3
            attm = [A.alloc("attm%d_%s" % (i, kind), 256, BF16) for i in range(WD)]
            kdt = [A.alloc("kdt%d_%s" % (i, kind), 512, BF16) for i in range(WD)]
            mask = C64("retmask") if kind == "ret" else C64("incl4")
            done = {}

            def chunk_gen(c):
                seq, sgi, sstart, send = seg_of_chunk(tile, c)
                cs = slice(c * 64, (c + 1) * 64)
                bA, bAr = P((c % WD, WD + 1))
                for h in range(4):
                    mm(bA[0:64, h * 64:(h + 1) * 64], kF[:, h, cs], qF[:, h, cs], True, True, [kFr, qFr], [bAr])
                yield
                av, avr = attm[c % WD]
                v_tt(av[0:64, :], bA[0:64, 0:256], mask, ALU.mult, [bAr, "c64"], [avr])
                bT, bTr = P((c % WD, WD + 1))
                bTb = bT[0:64, 0:256].bitcast(BF16)
                for h in range(4):
                    tr(bTb[:, h * 128:(h + 1) * 128], kS[:, h, cs], identb[:], [kSr, "identb"], [bTr])
                yield
                kv, kvr = kdt[c % WD]
                if kind == "ret":
                    v_tt(kv[0:64, :], bTb, C64("kdec"), ALU.mult, [bTr, "c64"], [kvr])
                else:
                    act(lambda e, kv=kv, bTb=bTb: e.copy(out=kv[0:64, :], in_=bTb), [bTr], [kvr])
                yield
                while c > 0 and not done.get(c - 1):
                    yield
                if sstart:
                    state_io(tile, seq, sstart, send, s32, s32r, st_in, o_list, "start")
                    act(lambda e, c=c: e.copy(out=sbf3[:, c, :], in_=s32[:]), [s32r], [(sbfr, c)])
                bO, bOr = P((c % WD, WD + 1))
                for h in range(4):
                    mm(bO[:, h * 64:(h + 1) * 64], vtm[0:64, c, h * 128:(h + 1) * 128], av[0:64, h * 64:(h + 1) * 64], True, False,
                       [vtmr, avr], [bOr])
                    mm(bO[:, h * 64:(h + 1) * 64], sbf3[:, c, h * 128:(h + 1) * 128], qF[:, h, cs], False, True,
                       [(sbfr, c), qFr], [bOr])
                act(lambda e, bO=bO, cs=cs: e.copy(out=osb3[:, :, cs], in_=bO[:, 0:256].rearrange("p (h t) -> p h t", h=4)),
                    [bOr], [(osbr, c)])
                bS, bSr = P((c % WD, WD + 1))
                for h in range(4):
                    mm(bS[:, h * 128:(h + 1) * 128], kv[0:64, h * 128:(h + 1) * 128], vtm[0:64, c, h * 128:(h + 1) * 128], True, True,
                       [kvr, vtmr], [bSr])
                for h in range(4):
                    hs_ = slice(h * 128, (h + 1) * 128)
                    if kind == "ret":
                        sc = SDEC[h]
                        rr = [s32r, bSr]
                    else:
                        sc = ebend[:, h, c:c + 1]
                        rr = [s32r, bSr, ebendr]
                    v_stt(s32[:, hs_], s32[:, hs_], sc, bS[:, hs_], ALU.mult, ALU.add, rr, [s32r])
                if send:
                    state_io(tile, seq, sstart, send, s32, s32r, st_in, o_list, "end")
                else:
                    act(lambda e, c=c: e.copy(out=sbf3[:, c + 1, :], in_=s32[:]), [s32r], [(sbfr, c + 1)])
                done[c] = True

            gens = [(lambda c=c: chunk_gen(c)) for c in range(NC)]
            interleave(gens, WD, extra=(None if extra is None else (lambda: extra((WD, WD + 1)))))
            return [(osbr, c) for c in range(NC)]

        def gate_gen(NT, Wv, wr, gate3, gater, alloc):
            for h in range(4):
                bk, br = yield from alloc()
                for kt in range(KT):
                    mm(bk[:, 0:NT], Wv[:, kt, h * 128:(h + 1) * 128], hn[:, kt, 0:NT], kt == 0, kt == KT - 1, [wr, "hn"], [br])
                yield
                a_act(gate3[:, h, :], bk[:, 0:NT], AF.Silu, [br], [gater])
                yield br

        def tm_proj(tile, Wv, wr, vt3, vtr):
            NT = tile["NT"]
            for c in range(NT // 64):
                bk, br = P()
                for kt in range(KT):
                    mm(bk[0:64, :], hn[:, kt, c * 64:(c + 1) * 64], Wv[:, kt, 0:512], kt == 0, kt == KT - 1, [wr, "hn"], [br])
                act(lambda e, bk=bk, c=c: e.copy(out=vt3[0:64, c, :], in_=bk[0:64, :]), [br], [vtr])

        def fm_proj(NT, Wv, wr, h, sub=None):
            bk, br = P(sub)
            for kt in range(KT):
                mm(bk[:, 0:NT], Wv[:, kt, h * 128:(h + 1) * 128], hn[:, kt, 0:NT], kt == 0, kt == KT - 1, [wr, "hn"], [br])
            return bk, br

        def even_mixer(tile):
            NT = tile["NT"]
            NC = NT // 64
            A.reset()
            mix, mixr = A.alloc("mix", KT * NT, BF16)
            mix3 = mix.rearrange("p (k t) -> p k t", k=KT)
            rmsnorm(NT, 2)
            tab, tabr = A.alloc("tab", 2 * NT, F32)
            tab3 = tab.rearrange("p (a t) -> p a t", a=2)
            S.dma("pool", lambda e: e.dma_start(out=tab3, in_=rope_d[:, :, tile["tok0"]:tile["tok0"] + NT]), None,
                  reads=[], writes=[tabr])

            def al3(name, dt):
                v, r = A.alloc(name, 4 * NT, dt)
                return v.rearrange("p (h t) -> p h t", h=4), r

            qd3, qdr = al3("qd", BF16)
            kr3, krr = al3("krot", BF16)
            vt, vtr = A.alloc("vtm", NC * 512, BF16)
            vt3 = vt.rearrange("p (c n) -> p c n", c=NC)
            gate3, gater = al3("gate", F32)
            osb3, osbr = al3("osb", F32)
            xbf = [A.alloc("xbf%d" % i, NT, BF16) for i in range(2)]
            ta = [A.alloc("ta%d" % i, NT, F32) for i in range(2)]
            tb = [A.alloc("tb%d" % i, NT, F32) for i in range(2)]
            qdecv = C128("qdec").rearrange("p (h t) -> p h t", h=4)

            def rope_unit(Wv, wr, dst3, dstr, isq):
                for h in range(4):
                    bk, br = fm_proj(NT, Wv, wr, h)
                    xv, xr = xbf[h % 2]
                    act(lambda e, xv=xv, bk=bk: e.copy(out=xv, in_=bk[:, 0:NT]), [br], [xr])
                    b2, b2r = P()
                    mm(b2[:, 0:NT], rmatb[:], xv, True, True, [xr, "rmatb"], [b2r])
                    tav, tar = ta[h % 2]
                    tbv, tbr = tb[h % 2]
                    v_tt(tav, bk[:, 0:NT], tab3[:, 0, :], ALU.mult, [br, tabr], [tar])
                    v_tt(tbv, b2[:, 0:NT], tab3[:, 1, :], ALU.mult, [b2r, tabr], [tbr])
                    if isq:
                        v_tt(tav, tav, tbv, ALU.add, [tar, tbr], [tar])
                        v_tt(dst3[:, h, :].rearrange("p (c t) -> p c t", t=64),
                             tav.rearrange("p (c t) -> p c t", t=64),
                             qdecv[:, h:h + 1, :].to_broadcast([128, NC, 64]), ALU.mult, [tar, "c128"], [dstr])
                    else:
                        v_tt(dst3[:, h, :], tav, tbv, ALU.add, [tar, tbr], [dstr])

            Wv, wr = W.next(("even_in", 0))
            rope_unit(Wv, wr, qd3, qdr, True)
            Wv, wr = W.next(("even_in", 1))
            rope_unit(Wv, wr, kr3, krr, False)
            Wv, wr = W.next(("even_in", 2))
            tm_proj(tile, Wv, wr, vt3, vtr)
            Wg, wgr = W.next(("even_in", 3))

            def sub_alloc(sub):
                def alloc():
                    return P(sub)
                    yield
                return alloc

            osbr = linattn_core(tile, "ret", qd3, qdr, kr3, krr, kr3, krr, vt3, vtr, osb3, osbr, s_ret, "s_ret", st_ret, o_ret,
                                extra=lambda sub: gate_gen(NT, Wg, wgr, gate3, gater, sub_alloc(sub)))
            head_norm(NT, osb3, osbr, gate3, gater, 0, mix3, mixr, 0)

            A.reset()
            mix, mixr = A.alloc("mix", KT * NT, BF16)
            mix3 = mix.rearrange("p (k t) -> p k t", k=KT)
            qe3, qer = al3("qe", BF16)
            ke3, ker = al3("ke", BF16)
            kn3, knr = al3("kend", BF16)
            vt2, vt2r = A.alloc("vtm2", NC * 512, BF16)
            vt23 = vt2.rearrange("p (c n) -> p c n", c=NC)
            gate23, gate2r = al3("gate2", F32)
            osb23, osb2r = al3("osb2", F32)
            qs3, qsr = al3("qsil", F32)
            eb3, ebr = al3("eb", F32)
            fb3, fbr = al3("fb", F32)
            ebend, ebendr = A.alloc("ebend", 4 * NC, F32)
            ebend3 = ebend.rearrange("p (h c) -> p h c", h=4)
            t1 = [A.alloc("t1_%d" % i, NT, F32) for i in range(2)]
            Wv, wr = W.next(("even_in", 4))
            for h in range(4):
                bk, br = fm_proj(NT, Wv, wr, h)
                a_act(qs3[:, h, :], bk[:, 0:NT], AF.Silu, [br], [qsr])
            Wv, wr = W.next(("even_in", 5))
            for h in range(4):
                bk, br = fm_proj(NT, Wv, wr, h)
                tv, tr_ = t1[h % 2]
                a_act(tv, bk[:, 0:NT], AF.Sigmoid, [br], [tr_])
                v_ts(fb3[:, h, :], tv, lbt[:, 3, h:h + 1], lbt[:, 2, h:h + 1], ALU.mult, ALU.add, [tr_, "lbt"], [(fbr, h)])
                v_ts(eb3[:, h, :], fb3[:, h, :], -1.0, 1.0, ALU.mult, ALU.add, [(fbr, h)], [(ebr, h)])
                a_act(fb3[:, h, :], fb3[:, h, :], AF.Ln, [(fbr, h)], [(fbr, h)])
                dve(lambda e, h=h: e.tensor_tensor_scan(out=fb3[:, h, :], data0=C128("reset")[:, 0:NT], data1=fb3[:, h, :],
                                                        initial=0.0, op0=ALU.mult, op1=ALU.add),
                    [(fbr, h), "c128"], [(fbr, h)])
                tv2, tr2 = t1[(h + 1) % 2]
                a_act(tv2, fb3[:, h, :], AF.Exp, [(fbr, h)], [tr2], scale=-1.0)
                v_tt(ke3[:, h, :], eb3[:, h, :], tv2, ALU.mult, [(ebr, h), tr2], [ker])
                a_act(eb3[:, h, :], fb3[:, h, :], AF.Exp, [(fbr, h)], [(ebr, h)])
                v_tt(qe3[:, h, :], qs3[:, h, :], eb3[:, h, :], ALU.mult, [qsr, (ebr, h)], [qer])
                act(lambda e, h=h: e.copy(out=ebend3[:, h, :], in_=eb3[:, h, :].rearrange("p (c t) -> p c t", t=64)[:, :, 63]),
                    [(ebr, h)], [ebendr])
                v_tt(kn3[:, h, :].rearrange("p (c t) -> p c t", t=64), ke3[:, h, :].rearrange("p (c t) -> p c t", t=64),
                     ebend3[:, h, :].unsqueeze(2).to_broadcast([128, NC, 64]), ALU.mult, [ker, ebendr], [knr])
            Wv, wr = W.next(("even_in", 6))
            tm_proj(tile, Wv, wr, vt23, vt2r)
            Wg2, wg2r = W.next(("even_in", 7))
            osb2r = linattn_core(tile, "hg", qe3, qer, ke3, ker, kn3, knr, vt23, vt2r, osb23, osb2r, s_hg, "s_hg", st_hg, o_hg,
                                 ebend=ebend3, ebendr=ebendr,
                                 extra=lambda sub: gate_gen(NT, Wg2, wg2r, gate23, gate2r, sub_alloc(sub)))
            head_norm(NT, osb23, osb2r, gate23, gate2r, 1, mix3, mixr, 4)
            out_proj(tile, "even_out", mix3, mixr, next_norm=4)

        def odd_mixer(tile):
            NT = tile["NT"]
            NC = NT // 64
            nseg = len(tile["segs"])
            L = tile["segs"][0]["L"]
            SEGS = tile["segs"]
            def al3(name, dt, n=4):
                v, r = A.alloc(name, n * NT, dt)
                return v.rearrange("p (h t) -> p h t", h=n), r

            def common():
                A.reset()
                mix, mixr = A.alloc("mix", KT * NT, BF16)
                beta, betar = A.alloc("beta", NC * 4, F32)
                loga, logar = A.alloc("loga", NC * 4, F32)
                egdk, egdkr = A.alloc("egdk", NC * 8, F32)
                egrow3, egr = al3("egrow", F32)
                return (mix.rearrange("p (k t) -> p k t", k=KT), mixr, beta.rearrange("p (c h) -> p c h", c=NC), betar,
                        loga.rearrange("p (c h) -> p c h", c=NC), logar, egdk, egdk.rearrange("p (c h) -> p c h", c=NC), egdkr,
                        egrow3, egr)

            mix3, mixr, beta3, betar, loga3, logar, egdk, egdk3, egdkr, egrow3, egr = common()
            rmsnorm(NT, 3)

            Wv, wr = W.next(("odd_in", "bda"))
            bk, br = P()
            for c in range(NC):
                for kt in range(KT):
                    mm(bk[0:64, c * 8:(c + 1) * 8], hn[:, kt, c * 64:(c + 1) * 64], Wv[:, kt, 0:8], kt == 0, kt == KT - 1, [wr, "hn"], [br])
            bk3 = bk[0:64, 0:NC * 8].rearrange("p (c h) -> p c h", c=NC)
            a_act(beta3[0:64], bk3[:, :, 0:4], AF.Sigmoid, [br], [betar])
            v_tt(loga3[0:64], bk3[:, :, 4:8], dnb[:, 0, 0:NC, :], ALU.add, [br, "dnb"], [logar])
            a_act(loga3[0:64], loga3[0:64], AF.Exp, [logar], [logar])
            a_act(loga3[0:64], loga3[0:64], AF.Ln, [logar], [logar], bias=1.0)
            v_tt(loga3[0:64], loga3[0:64], dnb[:, 1, 0:NC, :], ALU.mult, [logar, "dnb"], [logar])
            b2, b2r = P()
            for c in range(NC):
                mm(b2[0:64, c * 8:c * 8 + 4], C64("U"), loga3[0:64, c, :], True, True, ["c64", logar], [b2r])
                mm(b2[0:64, c * 8 + 4:c * 8 + 8], C64("Urev"), loga3[0:64, c, :], True, True, ["c64", logar], [b2r])
            a_act(egdk[0:64, :], b2[0:64, 0:NC * 8], AF.Exp, [b2r], [egdkr])
            def make_X(c, Xv, Xr):
                X3 = Xv[0:64, 0:256].rearrange("p (h t) -> p h t", h=4)
                v_tt(X3, C64("U").unsqueeze(1).to_broadcast([64, 4, 64]),
                     loga3[0:64, c, :].unsqueeze(2).to_broadcast([64, 4, 64]), ALU.mult, ["c64", logar], [Xr])
                return X3

            Xt = [A.alloc("Xt%d" % i, 256, F32) for i in range(2)]
            for c in range(NC):
                Xv, Xr = Xt[c % 2]
                X3 = make_X(c, Xv, Xr)
                bE, bEr = P()
                for h in range(4):
                    mm(bE[:, h * 64:(h + 1) * 64], C64("ones")[:, 0:128], X3[:, h, :], True, True, ["c64", Xr], [bEr])
                act(lambda e, bE=bE, c=c: e.activation(out=egrow3[:, :, c * 64:(c + 1) * 64],
                                                       in_=bE[:, 0:256].rearrange("p (h t) -> p h t", h=4), func=AF.Exp),
                    [bEr], [egr])

            cvs = {}

            def conv_alloc(nbuf):
                cvs["nbuf"] = nbuf
                xh, xhr = A.alloc("xh", 4 * nseg * (3 + L), F32)
                cvs["xh4"] = xh.rearrange("p (j s t) -> p j s t", j=4, s=nseg)
                cvs["xhr"] = xhr
                cvs["cst"] = [A.alloc("cst%d" % i, 128, F32) for i in range(2)]
                cvs["cv"] = [A.alloc("cv%d" % i, NT, F32) for i in range(nbuf)]
                cvs["n"] = 0

            def conv_unit(g, Wv, wr, consume):
                xh4, xhr = cvs["xh4"], cvs["xhr"]
                nb = cvs["nbuf"]

                def tile_gen(j):
                    sub = (j % nb, nb)
                    gj = g * 4 + j
                    bk, br = fm_proj(NT, Wv, wr, j, sub)
                    for si, sg in enumerate(SEGS):
                        seq = sg["seq"]
                        if sg["prev_same"]:
                            continue
                        if seq == 0:
                            if sg["start"]:
                                v_memset(xh4[:, j, si, 0:3], 0.0, [(xhr, j)], eng="pool")
                            else:
                                v_copy(xh4[:, j, si, 0:3], hcar[:, gj, :], [("hcar", gj)], [(xhr, j)], eng="pool")
                        else:
                            if g == 0:
                                src = st_lc[seq - 1][:, j * 128:(j + 1) * 128]
                            else:
                                src = st_dc[seq - 1][:, (gj - 4) * 128:(gj - 3) * 128]
                            S.dma("pool", lambda e, j=j, si=si, src=src: e.dma_start(out=xh4[:, j, si, 0:3], in_=src.rearrange("r p -> p r")),
                                  None, reads=[], writes=[(xhr, j)])
                    act(lambda e, bk=bk, j=j: e.copy(out=xh4[:, j, :, 3:3 + L], in_=bk[:, 0:NT].rearrange("p (s t) -> p s t", s=nseg)),
                        [br], [(xhr, j)])
                    ks = [si for si, sg in enumerate(SEGS) if sg["prev_same"]]
                    if ks:
                        k0, k1 = ks[0], ks[-1] + 1
                        v_copy(xh4[:, j, k0:k1, 0:3], xh4[:, j, k0 - 1:k1 - 1, L:L + 3], [(xhr, j)], [(xhr, j)], eng="pool")
                    p_last = max([si for si, sg in enumerate(SEGS) if sg["seq"] == 0])
                    if not SEGS[p_last]["end"]:
                        v_copy(hcar[:, gj, :], xh4[:, j, p_last, L:L + 3], [(xhr, j)], [("hcar", gj)], eng="pool")
                    yield
                    cvv, cvr = cvs["cv"][j % nb]
                    cv3 = cvv.rearrange("p (s t) -> p s t", s=nseg)
                    if g == 0:
                        wcol = lambda k, j=j: lruv[:, k, j:j + 1]
                        v_ts(cv3, xh4[:, j, :, 0:L], wcol(0), lruv[:, 4, j:j + 1], ALU.mult, ALU.add, [(xhr, j), "lruv"], [cvr])
                    else:
                        wcol = lambda k, gj=gj: dncw[:, k, gj - 4:gj - 3]
                        v_ts(cv3, xh4[:, j, :, 0:L], wcol(0), None, ALU.mult, None, [(xhr, j), "dncw"], [cvr])
                    for k in range(1, 4):
                        v_stt(cv3, xh4[:, j, :, k:k + L], wcol(k), cv3, ALU.mult, ALU.add, [(xhr, j), "lruv", "dncw", cvr], [cvr],
                              eng="dve")
                    for si, sg in enumerate(SEGS):
                        seq = sg["seq"]
                        if sg["end"]:
                            bt, btr = P(sub)
                            tr(bt[0:3, 0:128], xh4[:, j, si, L:L + 3], ident, [(xhr, j), "c128"], [btr])
                            cv_, cvr_ = cvs["cst"][cvs["n"] % 2]
                            cvs["n"] += 1
                            act(lambda e, bt=bt, cv_=cv_: e.copy(out=cv_[0:3, :], in_=bt[0:3, 0:128]), [btr], [cvr_])
                            if g == 0:
                                dd = (o_lc[0][0] if seq == 0 else o_lc[1][seq - 1])[:, j * 128:(j + 1) * 128]
                            else:
                                dd = (o_dc[0][0] if seq == 0 else o_dc[1][seq - 1])[:, (gj - 4) * 128:(gj - 3) * 128]
                            S.dma("pool", lambda e, dd=dd, cv_=cv_: e.dma_start(out=dd, in_=cv_[0:3, :]), None, reads=[cvr_], writes=[])
                    yield
                    yield from consume(j, cvv, cvr, sub, nb)

                interleave([(lambda j=j: tile_gen(j)) for j in range(4)], nb)

            conv_alloc(4)
            hs3, hsr = al3("hs", F32)
            tmpA = [A.alloc("lA%d" % i, NT, F32) for i in range(4)]
            tmpB = [A.alloc("lB%d" % i, NT, F32) for i in range(4)]
            tmpC = [A.alloc("lC%d" % i, NT, F32) for i in range(4)]
            lcb = [A.alloc("lcb%d" % i, NT, BF16) for i in range(4)]

            def lru_consume(j, cvv, cvr, sub, nb):
                lb_, lbr_ = lcb[j % nb]
                act(lambda e: e.copy(out=lb_, in_=cvv), [cvr], [lbr_])
                br_, brr = P(sub)
                mm(br_[:, 0:NT], bd[:, 0, j, :], lb_, True, True, ["bd", lbr_], [brr])
                bi_, bir = P(sub)
                mm(bi_[:, 0:NT], bd[:, 1, j, :], lb_, True, True, ["bd", lbr_], [bir])
                yield
                rv, rr = tmpA[j % nb]
                iv, ir = tmpB[j % nb]
                av, ar = tmpC[j % nb]
                a_act(rv, br_[:, 0:NT], AF.Sigmoid, [brr, "lruv"], [rr], bias=lruv[:, 5, j:j + 1])
                a_act(iv, bi_[:, 0:NT], AF.Sigmoid, [bir, "lruv"], [ir], bias=lruv[:, 6, j:j + 1])
                yield
                a_act(av, rv, AF.Exp, [rr, "lrud"], [ar], scale=lrud[:, 0, j:j + 1])
                a_act(rv, rv, AF.Exp, [rr, "lrud"], [rr], scale=lrud[:, 1, j:j + 1])
                v_tt(iv, iv, cvv, ALU.mult, [ir, cvr], [ir])
                yield
                a_act(rv, rv, AF.Sqrt, [rr], [rr], bias=1.0, scale=-1.0)
                yield
                v_tt(iv, iv, rv, ALU.mult, [ir, rr], [ir])
                for si, sg in enumerate(SEGS):
                    seq = sg["seq"]
                    if seq == 0 and sg["start"]:
                        v_memset(hstate[:, 0, j:j + 1], 0.0, [("hstate", j)])
                    elif seq != 0:
                        S.dma("pool", lambda e, seq=seq, j=j: e.dma_start(
                            out=hstate[:, seq, j:j + 1], in_=st_lh[seq - 1:seq, j * 128:(j + 1) * 128].rearrange("o p -> p o")),
                            None, reads=[], writes=[("hstate", j)])
                    cols = slice(si * L, (si + 1) * L)
                    dve(lambda e, cols=cols, seq=seq: e.tensor_tensor_scan(out=hs3[:, j, cols], data0=av[:, cols], data1=iv[:, cols],
                                                                          initial=hstate[:, seq, j:j + 1], op0=ALU.mult, op1=ALU.add),
                        [ar, ir, ("hstate", j)], [(hsr, j)])
                    v_copy(hstate[:, seq, j:j + 1], hs3[:, j, (si + 1) * L - 1:(si + 1) * L], [(hsr, j)], [("hstate", j)])

            Wv, wr = W.next(("odd_in", 0))
            conv_unit(0, Wv, wr, lru_consume)
            Wv, wr = W.next(("odd_in", 1))
            for j in range(4):
                bk, br = fm_proj(NT, Wv, wr, j)
                x2, x2r = tmpA[j % 2]
                inn, innr = tmpB[j % 2]
                a_act(x2, bk[:, 0:NT], AF.Square, [br], [x2r])
                v_ts(x2, x2, 0.044715, 1.0, ALU.mult, ALU.add, [x2r], [x2r])
                v_tt(inn, x2, bk[:, 0:NT], ALU.mult, [x2r, br], [innr])
                a_act(inn, inn, AF.Sigmoid, [innr], [innr], scale=2.0 * math.sqrt(2.0 / math.pi))
                v_tt(inn, inn, bk[:, 0:NT], ALU.mult, [innr, br], [innr])
                v_tt(mix3[:, j, :], inn, hs3[:, j, :], ALU.mult, [innr, (hsr, j)], [mixr])
            for si, sg in enumerate(SEGS):
                seq = sg["seq"]
                if sg["end"]:
                    dst = o_lh[0][0:1, :] if seq == 0 else o_lh[1][seq - 1:seq, :]
                    S.dma("pool", lambda e, dst=dst, seq=seq: e.dma_start(out=dst.rearrange("o (j p) -> p (o j)", p=128), in_=hstate[:, seq, :]),
                          None, reads=[("hstate", j) for j in range(4)] + [(hsr, j) for j in range(4)], writes=[])

            mix3, mixr, beta3, betar, loga3, logar, egdk, egdk3, egdkr, egrow3, egr = common()
            conv_alloc(2)
            tmpA = [A.alloc("lA%d" % i, NT, F32) for i in range(2)]
            qF3, qFr = al3("qF", BF16)
            qg3, qgr = al3("qg", BF16)
            kF3, kFr = al3("kF", BF16)
            vF3, vFr = al3("vF", BF16)
            sqt = [A.alloc("sqt%d" % i, NT, BF16) for i in range(2)]

            def qk_consume(isq):
                def f(j, cvv, cvr, sub, nb):
                    a_act(cvv, cvv, AF.Silu, [cvr], [cvr])
                    sv, sr = sqt[j % nb]
                    a_act(sv, cvv, AF.Square, [cvr], [sr])
                    bk, br = P(sub)
                    mm(bk[:, 0:NT], onesb[:], sv, True, True, [sr, "onesb"], [br])
                    yield
                    tv, tvr = tmpA[j % nb]
                    if isq:
                        a_act(tv, bk[:, 0:NT], AF.Ln, [br], [tvr], bias=EPS * 128.0, scale=128.0)
                    else:
                        a_act(tv, bk[:, 0:NT], AF.Ln, [br], [tvr], bias=EPS, scale=1.0)
                    a_act(tv, tv, AF.Exp, [tvr], [tvr], scale=-0.5)
                    yield
                    if isq:
                        v_tt(cvv, cvv, tv, ALU.mult, [cvr, tvr], [cvr])
                        v_copy(qF3[:, j, :], cvv, [cvr], [qFr], eng="pool")
                        v_tt(qg3[:, j, :], cvv, egrow3[:, j, :], ALU.mult, [cvr, egr], [qgr])
                    else:
                        v_tt(kF3[:, j, :], cvv, tv, ALU.mult, [cvr, tvr], [kFr])
                return f

            def v_consume(j, cvv, cvr, sub, nb):
                a_act(vF3[:, j, :], cvv, AF.Silu, [cvr], [vFr])
                yield

            Wv, wr = W.next(("odd_in", 2))
            conv_unit(1, Wv, wr, qk_consume(True))
            Wv, wr = W.next(("odd_in", 3))
            conv_unit(2, Wv, wr, qk_consume(False))
            Wv, wr = W.next(("odd_in", 4))
            conv_unit(3, Wv, wr, v_consume)
            gate3, gater = al3("gate", F32)
            Wdg, wdgr = W.next(("odd_in", 5))
            osb3, osbr = al3("osb", F32)

            def f64(name, n=256):
                v, r = A.alloc(name, n, F32)
                return v, r

            WDN = 3
            held = set()

            def galloc(n):
                while True:
                    free = [(bank_ctr[0] + k) % 8 for k in range(8) if ((bank_ctr[0] + k) % 8) not in held]
                    if len(free) >= n:
                        out = []
                        for i in free[:n]:
                            held.add(i)
                            out.append((banks[i], ("ps", i)))
                        bank_ctr[0] = free[n - 1] + 1
                        return out
                    yield

            def gfree(*ress):
                for r in ress:
                    held.discard(r[1])
            Xc, Xcr = f64("Xc")
            negX, negXr = f64("negX")
            dgB, dgBr = f64("dgB")
            Gm, Gmr = f64("Gm")
            sets = []
            for i in range(WDN):
                d = {}
                d["DT"] = f64("DT%d" % i)
                d["Bm"] = f64("Bm%d" % i)
                d["Nn"] = [(nmr[:, i, k, :], ("nmr", i, k)) for k in range(2)]
                d["Mm"] = [(nmr[:, i, 2 + k, :], ("nmr", i, 2 + k)) for k in range(2)]
                d["Rr"] = (nmr[:, i, 4, :], ("nmr", i, 4))
                d["QKm"] = A.alloc("QKm%d" % i, 256, BF16)
                d["Yb"] = A.alloc("Yb%d" % i, 256, BF16)
                d["kgt"] = A.alloc("kgt%d" % i, 512, BF16)
                d["kdc"] = A.alloc("kdc%d" % i, 512, BF16)
                d["vtm"] = A.alloc("vtmd%d" % i, 512, BF16)
                d["usb"] = f64("usb%d" % i, 512)
                d["WkT"] = A.alloc("WkT%d" % i, 256, BF16)
                d["wv"] = A.alloc("wv%d" % i, 512, BF16)
                sets.append(d)
            sbf, sbfr = A.alloc("sbfd", 512, BF16)
            ones64 = C64("ones")[:, 0:64]
            id64 = ident[0:64, 0:64]
            dn_done = {}

            def h4(v):
                return v[0:64, 0:256].rearrange("p (h t) -> p h t", h=4)

            def r4(v):
                return v[0:64, 0:256].bitcast(F32R).rearrange("p (h t) -> p h t", h=4)

            def rr(v):
                return v[0:64, 0:256].bitcast(F32R)

            def dn_gen(c):
                d = sets[c % WDN]
                DT, DTr = d["DT"]
                Bm, Bmr = d["Bm"]
                Nn, Mm = d["Nn"], d["Mm"]
                Rr, Rrr = d["Rr"]
                QKm, QKmr = d["QKm"]
                Yb, Ybr = d["Yb"]
                kgt, kgtr = d["kgt"]
                kdc, kdcr = d["kdc"]
                vtm, vtmr = d["vtm"]
                usb, usbr = d["usb"]
                WkT, WkTr = d["WkT"]
                wv_, wvr = d["wv"]
                seq, sgi, sstart, send = seg_of_chunk(tile, c)
                cs = slice(c * 64, (c + 1) * 64)
                X3 = make_X(c, Xc, Xcr)
                act(lambda e: e.mul(out=negX[0:64, :], in_=Xc[0:64, :], mul=-1.0), [Xcr], [negXr])
                v_tt(h4(dgB), C64("i4").rearrange("p (h t) -> p h t", h=4),
                     beta3[0:64, c, :].unsqueeze(2).to_broadcast([64, 4, 64]), ALU.mult, ["c64", betar], [dgBr])
                (bG, bGr), (bK, bKr), (bT, bTr), (bV, bVr) = yield from galloc(4)
                for h in range(4):
                    mm(bG[0:64, h * 128:h * 128 + 64], ones64, X3[:, h, :], True, False, ["c64", Xcr], [bGr])
                    mm(bG[0:64, h * 128:h * 128 + 64], h4(negX)[:, h, :], ones64, False, True, ["c64", negXr], [bGr])
                    mm(bG[0:64, h * 128 + 64:h * 128 + 128], ones64, h4(dgB)[:, h, :], True, True, ["c64", dgBr], [bGr])
                bG3 = bG[0:64, :].rearrange("p (h t) -> p h t", h=4)
                neg3 = C64("neg4").rearrange("p (h t) -> p h t", h=4)
                str3 = C64("strict4").rearrange("p (h t) -> p h t", h=4)
                for h in range(4):
                    mm(bK[0:64, h * 128:h * 128 + 64], kF3[:, h, cs], kF3[:, h, cs], True, True, [kFr], [bKr])
                    mm(bK[0:64, h * 128 + 64:h * 128 + 128], kF3[:, h, cs], qF3[:, h, cs], True, True, [kFr, qFr], [bKr])
                bK3 = bK[0:64, :].rearrange("p (h t) -> p h t", h=4)
                bTb = bT[0:64, 0:256].bitcast(BF16)
                for h in range(4):
                    tr(bTb[:, h * 128:(h + 1) * 128], kF3[:, h, cs], identb[:], [kFr, "identb"], [bTr])
                bT3 = bTb.rearrange("p (h d) -> p h d", h=4)
                bVb = bV[0:64, 0:256].bitcast(BF16)
                for h in range(4):
                    tr(bVb[:, h * 128:(h + 1) * 128], vF3[:, h, cs], identb[:], [vFr, "identb"], [bVr])
                yield
                v_tt(h4(Gm), bG3[:, :, 0:64], neg3, ALU.add, [bGr, "c64"], [Gmr])
                a_act(DT[0:64, :], Gm[0:64, :], AF.Exp, [Gmr], [DTr])
                v_tt(h4(Bm), bG3[:, :, 64:128], str3, ALU.mult, [bGr, "c64"], [Bmr])
                v_tt(kgt[0:64, :].rearrange("p (h d) -> p h d", h=4), bT3,
                     egdk3[0:64, c, 0:4].unsqueeze(2).to_broadcast([64, 4, 128]), ALU.mult, [bTr, egdkr], [kgtr])
                v_tt(kdc[0:64, :].rearrange("p (h d) -> p h d", h=4), bT3,
                     egdk3[0:64, c, 4:8].unsqueeze(2).to_broadcast([64, 4, 128]), ALU.mult, [bTr, egdkr], [kdcr])
                act(lambda e, bVb=bVb: e.copy(out=vtm[0:64, :], in_=bVb), [bVr], [vtmr])
                gfree(bGr, bTr, bVr)
                yield
                v_tt(Bm[0:64, :], Bm[0:64, :], DT[0:64, :], ALU.mult, [Bmr, DTr], [Bmr])
                N0, N0r = Nn[0]
                v_tt(r4(N0), bK3[:, :, 0:64], h4(Bm), ALU.mult, [bKr, Bmr], [N0r])
                v_tt(h4(QKm), bK3[:, :, 64:128], h4(DT), ALU.mult, [bKr, DTr], [QKmr])
                gfree(bKr)
                yield
                M0, M0r = Mm[0]
                ((bt, btr),) = yield from galloc(1)
                for h in range(4):
                    tr(bt[0:64, h * 64:(h + 1) * 64], h4(N0)[:, h, :], id64, [N0r, "c128"], [btr])
                act(lambda e, bt=bt, M0=M0: e.copy(out=rr(M0), in_=bt[0:64, 0:256]), [btr], [M0r])
                gfree(btr)
                v_tt(rr(Rr), C64("i4"), N0[0:64, :], ALU.subtract, [N0r, "c64"], [Rrr])
                yield
                cur = 0
                for stg in range(5):
                    Nc_, Ncr = Nn[cur]
                    Mc_, Mcr = Mm[cur]
                    Nx_, Nxr = Nn[1 - cur]
                    Mx_, Mxr = Mm[1 - cur]
                    last = (stg == 4)
                    if not last:
                        (bN, bNr), (bM, bMr) = yield from galloc(2)
                        for h in range(4):
                            mm(bN[0:64, h * 64:(h + 1) * 64], r4(Mc_)[:, h, :], r4(Nc_)[:, h, :], True, True, [Mcr, Ncr], [bNr])
                    else:
                        ((bM, bMr),) = yield from galloc(1)
                    for h in range(4):
                        mm(bM[0:64, h * 64:(h + 1) * 64], r4(Nc_)[:, h, :], r4(Mc_)[:, h, :], True, True, [Mcr, Ncr], [bMr])
                    yield
                    if not last:
                        v_tt(r4(Nx_), bN[0:64, 0:256].rearrange("p (h t) -> p h t", h=4),
                             C64("ones")[:, 0:64].unsqueeze(1).to_broadcast([64, 4, 64]), ALU.mult, [bNr, "c64"], [Nxr])
                    act(lambda e, bM=bM, Mx_=Mx_: e.copy(out=rr(Mx_), in_=bM[0:64, 0:256]), [bMr], [Mxr])
                    if not last:
                        gfree(bNr)
                    gfree(bMr)
                    ((bR, bRr),) = yield from galloc(1)
                    for h in range(4):
                        mm(bR[0:64, h * 64:(h + 1) * 64], r4(Mx_)[:, h, :], r4(Rr)[:, h, :], True, True, [Mxr, Rrr], [bRr])
                    yield
                    v_tt(rr(Rr), Rr[0:64, :], bR[0:64, 0:256], ALU.add, [Rrr, bRr], [Rrr])
                    gfree(bRr)
                    cur = 1 - cur
                for h in range(4):
                    act(lambda e, h=h, c=c: e.activation(out=h4(Yb)[:, h, :], in_=h4(Rr)[:, h, :], func=AF.Copy, scale=beta3[0:64, c, h:h + 1]),
                        [Rrr, betar], [Ybr])
                yield
                (bU, bUr), (bW, bWr) = yield from galloc(2)
                for h in range(4):
                    mm(bU[0:64, h * 128:(h + 1) * 128], h4(Yb)[:, h, :], vtm[0:64, h * 128:(h + 1) * 128], True, True, [Ybr, vtmr], [bUr])
                act(lambda e, bU=bU: e.copy(out=usb[0:64, :], in_=bU[0:64, :]), [bUr], [usbr])
                for h in range(4):
                    mm(bW[:, h * 64:(h + 1) * 64], kgt[0:64, h * 128:(h + 1) * 128], h4(Yb)[:, h, :], True, True, [kgtr, Ybr], [bWr])
                act(lambda e, bW=bW: e.copy(out=WkT, in_=bW[:, 0:256]), [bWr], [WkTr])
                gfree(bUr, bWr)
                yield
                while c > 0 and not dn_done.get(c - 1):
                    yield
                if sstart:
                    state_io(tile, seq, sstart, send, s_dn, "s_dn", st_dn, o_dn, "start")
                    act(lambda e: e.copy(out=sbf, in_=s_dn[:]), ["s_dn"], [sbfr])
                ((bWS, bWSr),) = yield from galloc(1)
                for h in range(4):
                    mm(bWS[0:64, h * 128:(h + 1) * 128], WkT[:, h * 64:(h + 1) * 64], sbf[:, h * 128:(h + 1) * 128], True, True,
                       [WkTr, sbfr], [bWSr])
                v_tt(wv_[0:64, :], usb[0:64, :], bWS[0:64, :], ALU.subtract, [usbr, bWSr], [wvr])
                gfree(bWSr)
                ((bO, bOr),) = yield from galloc(1)
                for h in range(4):
                    mm(bO[:, h * 64:(h + 1) * 64], wv_[0:64, h * 128:(h + 1) * 128], h4(QKm)[:, h, :], True, False, [wvr, QKmr], [bOr])
                    mm(bO[:, h * 64:(h + 1) * 64], sbf[:, h * 128:(h + 1) * 128], qg3[:, h, cs], False, True, [sbfr, qgr], [bOr])
                act(lambda e, bO=bO, cs=cs: e.copy(out=osb3[:, :, cs], in_=bO[:, 0:256].rearrange("p (h t) -> p h t", h=4)),
                    [bOr], [(osbr, c)])
                gfree(bOr)
                ((bS, bSr),) = yield from galloc(1)
                for h in range(4):
                    mm(bS[:, h * 128:(h + 1) * 128], kdc[0:64, h * 128:(h + 1) * 128], wv_[0:64, h * 128:(h + 1) * 128], True, True,
                       [kdcr, wvr], [bSr])
                for h in range(4):
                    hs_ = slice(h * 128, (h + 1) * 128)
                    v_stt(s_dn[:, hs_], s_dn[:, hs_], egrow3[:, h, c * 64 + 63:c * 64 + 64], bS[:, hs_], ALU.mult, ALU.add,
                          ["s_dn", bSr, egr], ["s_dn"])
                gfree(bSr)
                if send:
                    state_io(tile, seq, sstart, send, s_dn, "s_dn", st_dn, o_dn, "end")
                else:
                    act(lambda e: e.copy(out=sbf, in_=s_dn[:]), ["s_dn"], [sbfr])
                dn_done[c] = True

            def dn_gate():
                def alloc():
                    ((bk, br),) = yield from galloc(1)
                    return bk, br
                g = gate_gen(NT, Wdg, wdgr, gate3, gater, alloc)
                for r in g:
                    if r is not None:
                        gfree(r)
                    yield

            gens = [(lambda c=c: dn_gen(c)) for c in range(NC)]
            interleave(gens, WDN, extra=dn_gate)
            osbr = [(osbr, c) for c in range(NC)]
            cv0, cv0r = cvs["cv"][0]
            head_norm(NT, osb3, osbr, gate3, gater, 2, mix3, mixr, 4,
                      scratch=(A.raw["xh"][:, 0:4 * NT], [(cvs["xhr"], j) for j in range(4)], cv0, cv0r))
            out_proj(tile, "odd_out", mix3, mixr, next_norm=5)

        def final_out(tile):
            NT = tile["NT"]
            A.reset()
            rmsnorm_f32_out(tile)

        def rmsnorm_f32_out(tile):
            NT = tile["NT"]
            sq, sqr = A.alloc("nsq", KT * NT, BF16)
            sq3 = sq.rearrange("p (k t) -> p k t", k=KT)
            yv, yr = A.alloc("yfm", KT * NT, F32)
            y3 = yv.rearrange("p (k t) -> p k t", k=KT)
            act(lambda e: e.activation(out=sq3, in_=x_sb[:, :, 0:NT], func=AF.Square), ["x"], [sqr])
            bk, br = P()
            for kt in range(KT):
                mm(bk[:, 0:NT], onesb[:], sq3[:, kt, :], kt == 0, kt == KT - 1, [sqr, "onesb"], [br])
            a_act(rt[:, 0, 0:NT], bk[:, 0:NT], AF.Ln, [br], ["rt0"], bias=EPS, scale=1.0 / D)
            a_act(rt[:, 1, 0:NT], rt[:, 0, 0:NT], AF.Exp, ["rt0"], ["rt1"], scale=-0.5)
            for kt in range(KT):
                v_stt(y3[:, kt, :], x_sb[:, kt, 0:NT], normw[:, 6, kt:kt + 1], rt[:, 1, 0:NT], ALU.mult, ALU.mult,
                      ["x", "normw", "rt1"], [(yr, kt)])
            for b in range(NT // 128):
                sl = b % 2
                for half in range(2):
                    bk, br = P()
                    for q in range(4):
                        kt = half * 4 + q
                        tr(bk[:, q * 128:(q + 1) * 128], y3[:, kt, b * 128:(b + 1) * 128], ident, [(yr, kt), "c128"], [br])
                    act(lambda e, bk=bk, half=half, sl=sl: e.copy(out=xin[:, sl, half * 512:(half + 1) * 512], in_=bk[:, :]),
                        [br], [("xin", sl)])
                if b * 128 < tile["ptok"]:
                    dstb = yp[tile["tok0"] + b * 128:tile["tok0"] + (b + 1) * 128, :]
                else:
                    dstb = ys[b * 128 - tile["ptok"]:(b + 1) * 128 - tile["ptok"], :]
                S.dma("pool", lambda e, sl=sl, dstb=dstb: e.dma_start(out=dstb, in_=xin[:, sl, :]),
                      "xin%d" % sl, reads=[("xin", sl)], writes=[])

        for tile in tiles:
            t0_ = (tile is tiles[0])
            load_x(tile)
            for mo_ in range(KT):
                prenorm(tile["NT"], 0, mo_)
            if t0_:
                cast_group(2)
            if stage >= 1:
                ffn(tile, 0, 0)
            if t0_:
                cast_group(3)
            if stage >= 2:
                even_mixer(tile)
            if t0_:
                cast_group(4)
            if stage >= 3:
                ffn(tile, 1, 0, next_norm=1)
            if t0_:
                cast_group(5)
            if stage >= 3:
                ffn(tile, 0, 1)
            if stage >= 4:
                odd_mixer(tile)
            if stage >= 5:
                ffn(tile, 1, 1)
            final_out(tile)
        assert W.consumed == len(W.units)
        S.wait_deps("pool", [v for k, v in S.dma_last.items() if not (k.startswith("w") or k.startswith("cast"))])

        S.emit({"pe": block.tensor, "act": block.scalar, "dve": block.vector, "pool": block.gpsimd, "sp": block.sync},
               eng_sems, dma_sems)
    return nc


_PROG_CACHE = {}


def kernel(x_prompt, x_sample, state_ret, state_hgrn, state_lru_h, state_lru_conv, state_dn, state_dn_conv,
           ffn1_norm, ffn1_w_in, ffn1_w_out, mix_norm, ffn2_norm, ffn2_w_in, ffn2_w_out, final_norm,
           even_w_in, even_w_out, ret_out_norm, hg_out_norm, hg_lb_logits, odd_w_in, odd_w_out,
           lru_conv_w, lru_conv_b, lru_w_a, lru_b_a, lru_w_x, lru_b_x, lru_lambda, dn_conv_w, dn_a_log,
           dn_dt_bias, dn_out_norm, _past_len=2048, _stage=99, _ncores=NCORES):
    f = lambda a: np.ascontiguousarray(np.asarray(a, dtype=np.float32))
    x_prompt = f(x_prompt)
    x_sample = f(x_sample)
    B, TP, _ = x_prompt.shape
    assert B == 4 and x_sample.shape[0] == 16 and x_sample.shape[1] == 64
    hc = host_consts(TP, _past_len)
    if (TP, _stage) not in _PROG_CACHE:
        _PROG_CACHE[(TP, _stage)] = build_program(TP, _stage)
    nc = _PROG_CACHE[(TP, _stage)]
    shared = {
        "ffn1_w_in": f(ffn1_w_in), "ffn2_w_in": f(ffn2_w_in), "ffn1_w_out": f(ffn1_w_out), "ffn2_w_out": f(ffn2_w_out),
        "even_w_in": f(even_w_in)[0], "even_w_out": f(even_w_out)[0], "odd_w_in": f(odd_w_in)[0], "odd_w_out": f(odd_w_out)[0],
        "norms": np.ascontiguousarray(np.concatenate([f(ffn1_norm), f(mix_norm), f(ffn2_norm), f(final_norm)[None]], 0)),
        "hnorm": np.ascontiguousarray(np.concatenate([f(ret_out_norm), f(hg_out_norm), f(dn_out_norm)], 0)),
        "hg_lb_logits": f(hg_lb_logits),
        "lru_vecs": np.ascontiguousarray(np.concatenate([f(lru_conv_w)[0], f(lru_conv_b), f(lru_b_a), f(lru_b_x), f(lru_lambda)], 0)),
        "lru_w_a": f(lru_w_a)[0], "lru_w_x": f(lru_w_x)[0],
        "dn_conv_w": f(dn_conv_w)[0],
        "dn_scal": np.ascontiguousarray(np.concatenate([f(dn_a_log), f(dn_dt_bias)], 0)),
        "c128": hc["c128"], "c64": hc["c64"], "rope": hc["rope"],
    }
    sr, sh, sd = f(state_ret)[0], f(state_hgrn)[0], f(state_dn)[0]
    slh, slc, sdc = f(state_lru_h)[0], f(state_lru_conv)[0], f(state_dn_conv)[0]
    in_maps = []
    zero_prompt = np.zeros_like(x_prompt[0])
    for c in range(NCORES):
        m = dict(shared)
        m["xp"] = x_prompt[PROMPT_OF_CORE[c]] if PROMPT_OF_CORE[c] is not None else zero_prompt
        m["xs"] = np.ascontiguousarray(x_sample[2 * c:2 * c + 2].reshape(128, D))
        m["st_ret"] = np.ascontiguousarray(sr[2 * c:2 * c + 2])
        m["st_hg"] = np.ascontiguousarray(sh[2 * c:2 * c + 2])
        m["st_dn"] = np.ascontiguousarray(sd[2 * c:2 * c + 2])
        m["st_lh"] = np.ascontiguousarray(slh[2 * c:2 * c + 2])
        m["st_lc"] = np.ascontiguousarray(slc[2 * c:2 * c + 2])
        m["st_dc"] = np.ascontiguousarray(sdc[2 * c:2 * c + 2])
        in_maps.append(m)
    res = run_bass_kernel_spmd(nc, in_maps[:_ncores], core_ids=list(range(_ncores)))
    R = list(res.results)
    while len(R) < NCORES:
        R.append(R[0])
    y_prompt = np.stack([R[c]["yp"] for c in CORE_OF_PROMPT], 0)
    y_sample = np.concatenate([R[c]["ys"].reshape(2, 64, D) for c in range(NCORES)], 0)

    def gp(name, shape):
        return np.stack([R[c][name].reshape(shape) for c in CORE_OF_PROMPT], 0)[None]

    def gs(name, shape):
        return np.concatenate([R[c][name].reshape((2,) + shape) for c in range(NCORES)], 0)[None]

    return (y_prompt, y_sample,
            gp("ret_p", (4, 128, 128)), gs("ret_s", (4, 128, 128)),
            gp("hg_p", (4, 128, 128)), gs("hg_s", (4, 128, 128)),
            gp("lh_p", (512,)), gs("lh_s", (512,)),
            gp("lc_p", (3, 512)), gs("lc_s", (3, 512)),
            gp("dn_p", (4, 128, 128)), gs("dn_s", (4, 128, 128)),
            gp("dc_p", (3, 1536)), gs("dc_s", (3, 1536)))
```

```python
import contextlib
import math
import numpy as np
import concourse.bass as bass
import concourse.mybir as mybir
from concourse.bass_utils import run_bass_kernel_spmd

F32 = mybir.dt.float32
BF16 = mybir.dt.bfloat16
F32R = mybir.dt.float32r
AF = mybir.ActivationFunctionType
ALU = mybir.AluOpType

D = 1024
KT = 8
FF = 2816
FT = 22
EPS = 1e-6
LRU_C = 8.0
NCORES = 8
PROMPT_OF_CORE = [0, 1, None, None, 2, 3, None, None]
CORE_OF_PROMPT = [0, 1, 4, 5]


class Op:
    __slots__ = ("eng", "fn", "waits", "signal", "idx", "dma_sem", "sig_count")

    def __init__(self, eng, fn):
        self.eng = eng
        self.fn = fn
        self.waits = []
        self.signal = False
        self.dma_sem = None
        self.sig_count = 0
        self.idx = 0


class Sched:
    def __init__(self):
        self.ops = {e: [] for e in ("pe", "act", "dve", "pool", "sp")}
        self.last_write = {}
        self.readers = {}
        self.waited = {e: {} for e in self.ops}
        self.dma_counts = {}
        self.dma_last = {}
        self.misc_ctr = {}
        self.inherit = {}

    def _need(self, op, dep, is_dma=False):
        if dep is None:
            return
        kind, key, val = dep
        if kind == "eng" and key == op.eng and key == "pe" and not is_dma:
            return
        w = self.waited[op.eng]
        k = (kind, key)
        if w.get(k, -1) >= val:
            return
        w[k] = val
        op.waits.append(dep)

    def _deps(self, op, reads, writes, is_dma=False):
        if self.inherit:
            for r in list(reads) + list(writes):
                for base in ((r, r[0]) if (isinstance(r, tuple) and len(r) == 2 and isinstance(r[0], tuple)) else (r,)):
                    for d in self.inherit.get(base, ()):
                        self._need(op, d, is_dma)
        for r in reads:
            self._need(op, self.last_write.get(r), is_dma)
        for r in writes:
            self._need(op, self.last_write.get(r), is_dma)
            for d in self.readers.get(r, ()):
                self._need(op, d, is_dma)

    def _commit(self, dep, reads, writes):
        for r in reads:
            self.readers.setdefault(r, []).append(dep)
        for r in writes:
            self.last_write[r] = dep
            self.readers[r] = []

    def op(self, eng, fn, reads=(), writes=()):
        psr = [r for r in reads if isinstance(r, tuple) and r[0] == "ps"]
        if psr:
            writes = list(writes) + [r for r in psr if r not in writes]
        o = Op(eng, fn)
        o.idx = len(self.ops[eng])
        self._deps(o, reads, writes)
        self.ops[eng].append(o)
        self._commit(("eng", eng, o.idx), reads, writes)
        return o

    NMISC = 24

    def dma(self, eng, fn, sem=None, reads=(), writes=()):
        o = Op(eng, fn)
        o.idx = len(self.ops[eng])
        if sem is None:
            mc = self.misc_ctr.get(eng, 0)
            sem = "m%s%d" % (eng, mc % self.NMISC)
            self.misc_ctr[eng] = mc + 1
        if not sem.startswith("cast") and not sem.startswith("w"):
            self._need(o, self.dma_last.get(sem), True)
        self._deps(o, reads, writes, True)
        c = self.dma_counts.get(sem, 0) + 1
        self.dma_counts[sem] = c
        o.dma_sem = sem
        self.ops[eng].append(o)
        dep = ("dma", sem, 16 * c)
        self.dma_last[sem] = dep
        self._commit(dep, reads, writes)
        return o

    def barrier(self):
        lasts = []
        for e in ("pe", "act", "dve", "pool"):
            if self.ops[e]:
                for o in reversed(self.ops[e]):
                    if o.fn is not None and o.dma_sem is None:
                        lasts.append(("eng", e, o.idx))
                        break
        dmas = [v for k, v in self.dma_last.items() if not (k.startswith('w') or k.startswith('cast'))]
        for e in ("pe", "act", "dve", "pool"):
            o = Op(e, None)
            o.idx = len(self.ops[e])
            for d in lasts + dmas:
                self._need(o, d)
            self.ops[e].append(o)

    def wait_deps(self, eng, deps):
        o = Op(eng, None)
        o.idx = len(self.ops[eng])
        for d in deps:
            self._need(o, d, True)
        self.ops[eng].append(o)

    def finalize(self):
        for e, lst in self.ops.items():
            for o in lst:
                for kind, key, val in o.waits:
                    if kind == "eng":
                        self.ops[key][val].signal = True
        for e, lst in self.ops.items():
            c = 0
            for o in lst:
                if o.signal:
                    c += 1
                o.sig_count = c

    def emit(self, regs, eng_sems, dma_sems):
        self.finalize()
        for e, reg in regs.items():
            lst = self.ops[e]
            if not lst:
                continue

            def body(engine, lst=lst, e=e):
                for o in lst:
                    for kind, key, val in o.waits:
                        if kind == "eng":
                            engine.wait_ge(eng_sems[key], self.ops[key][val].sig_count)
                        else:
                            engine.wait_ge(dma_sems[key], val)
                    if o.fn is None:
                        continue
                    ins = o.fn(engine)
                    if o.dma_sem is not None:
                        ins.then_inc(dma_sems[o.dma_sem], 16)
                    elif o.signal:
                        ins.then_inc(eng_sems[e], 1)

            reg(body)


def host_consts(TP, past_len):
    c = {}
    g = np.array([np.log1p(-2.0 ** (-5.0 - h)) for h in range(4)], np.float64)
    p = np.arange(64, dtype=np.float64)
    c128 = {}
    c128["ident"] = np.eye(128)
    rm = np.zeros((128, 128))
    for d in range(64):
        rm[d + 64, d] = -1.0
        rm[d, d + 64] = 1.0
    c128["rmat"] = rm
    qd = np.stack([(128.0 ** -0.5) * np.exp(g[h] * (p + 1.0)) for h in range(4)], 0)
    c128["qdec"] = np.broadcast_to(qd.reshape(1, 256), (128, 256))
    rs = np.ones(512)
    rs[0::64] = 0.0
    c128["reset"] = np.broadcast_to(rs.reshape(1, 512), (128, 512))
    c128["ones"] = np.ones((128, 128))
    names128 = ["ident", "rmat", "qdec", "reset", "ones"]
    c["c128"] = np.concatenate([np.asarray(c128[k], np.float64) for k in names128], 1).astype(np.float32)
    off = 0
    c["off128"] = {}
    for k in names128:
        c["off128"][k] = (off, c128[k].shape[1])
        off += c128[k].shape[1]
    c64 = {}
    s = p.reshape(64, 1)
    t = p.reshape(1, 64)
    c64["retmask"] = np.concatenate([np.exp(g[h] * (np.abs(t - s) - (t + 1.0))) for h in range(4)], 1)
    c64["kdec"] = np.concatenate([np.broadcast_to(np.exp(g[h] * (63.0 - s)), (64, 128)) for h in range(4)], 1)
    incl = (s <= t).astype(np.float64)
    strict = (s < t).astype(np.float64)
    c64["incl4"] = np.tile(incl, (1, 4))
    c64["neg4"] = np.tile((1.0 - incl) * -30000.0, (1, 4))
    c64["strict4"] = np.tile(strict, (1, 4))
    c64["i4"] = np.tile(np.eye(64), (1, 4))
    c64["U"] = incl
    c64["Urev"] = (s > t).astype(np.float64)
    c64["ones"] = np.ones((64, 128))
    names64 = ["retmask", "kdec", "incl4", "neg4", "strict4", "i4", "U", "Urev", "ones"]
    c["c64"] = np.concatenate([c64[k] for k in names64], 1).astype(np.float32)
    off = 0
    c["off64"] = {}
    for k in names64:
        c["off64"][k] = (off, c64[k].shape[1])
        off += c64[k].shape[1]
    c["sdec"] = [float(np.exp(g[h] * 64.0)) for h in range(4)]
    pos = np.concatenate([np.arange(TP), past_len + np.arange(64), past_len + np.arange(64)]).astype(np.float64)
    half = 64
    freq = 10000.0 ** (-np.arange(half, dtype=np.float64) / half)
    ang = (pos.astype(np.float32)[None, :] * freq.astype(np.float32)[:, None]).astype(np.float32).astype(np.float64)
    cos = np.cos(ang)
    sin = np.sin(ang)
    tab = np.zeros((128, 2, TP + 128), np.float32)
    tab[0:64, 0] = cos
    tab[64:128, 0] = cos
    tab[0:64, 1] = sin
    tab[64:128, 1] = sin
    c["rope"] = tab
    return c


def build_program(TP, stage=99):
    assert TP % 512 == 0
    hc = host_consts(TP, 0)
    off128, off64, SDEC = hc["off128"], hc["off64"], hc["sdec"]
    C128W = hc["c128"].shape[1]
    C64W = hc["c64"].shape[1]

    nc = bass.Bass("TRN2", target_bir_lowering=False)

    def din(name, shape, dt=F32):
        return nc.dram_tensor(name, list(shape), dt, kind="ExternalInput").ap()

    def dout(name, shape):
        return nc.dram_tensor(name, list(shape), F32, kind="ExternalOutput").ap()

    def dscr(name, shape, dt):
        return nc.dram_tensor(name, list(shape), dt, kind="Internal").ap()

    xp = din("xp", [TP, D])
    xs = din("xs", [128, D])
    st_ret = din("st_ret", [2, 4, 128, 128])
    st_hg = din("st_hg", [2, 4, 128, 128])
    st_dn = din("st_dn", [2, 4, 128, 128])
    st_lh = din("st_lh", [2, 512])
    st_lc = din("st_lc", [2, 3, 512])
    st_dc = din("st_dc", [2, 3, 1536])
    w_ffn_in = [din("ffn1_w_in", [2, D, 2 * FF]), din("ffn2_w_in", [2, D, 2 * FF])]
    w_ffn_out = [din("ffn1_w_out", [2, FF, D]), din("ffn2_w_out", [2, FF, D])]
    w_even_in = din("even_w_in", [D, 4096])
    w_even_out = din("even_w_out", [D, D])
    w_odd_in = din("odd_w_in", [D, 3080])
    w_odd_out = din("odd_w_out", [D, D])
    norms_d = din("norms", [7, D])
    hnorm_d = din("hnorm", [3, 512])
    lb_logits = din("hg_lb_logits", [2, 512])
    lru_vecs = din("lru_vecs", [8, 512])
    lru_wa = din("lru_w_a", [8, 64, 64])
    lru_wx = din("lru_w_x", [8, 64, 64])
    dn_conv_w = din("dn_conv_w", [4, 1536])
    dn_scal = din("dn_scal", [2, 4])
    c128_d = din("c128", [128, C128W])
    c64_d = din("c64", [64, C64W])
    rope_d = din("rope", [128, 2, TP + 128])

    yp = dout("yp", [TP, D])
    ys = dout("ys", [128, D])
    o_ret = [dout("ret_p", [4, 128, 128]), dout("ret_s", [2, 4, 128, 128])]
    o_hg = [dout("hg_p", [4, 128, 128]), dout("hg_s", [2, 4, 128, 128])]
    o_dn = [dout("dn_p", [4, 128, 128]), dout("dn_s", [2, 4, 128, 128])]
    o_lh = [dout("lh_p", [1, 512]), dout("lh_s", [2, 512])]
    o_lc = [dout("lc_p", [1, 3, 512]), dout("lc_s", [2, 3, 512])]
    o_dc = [dout("dc_p", [1, 3, 1536]), dout("dc_s", [2, 3, 1536])]

    wb_ffn_in = [dscr("b_ffn1_w_in", [2, D, 2 * FF], BF16), dscr("b_ffn2_w_in", [2, D, 2 * FF], BF16)]
    wb_ffn_out = [dscr("b_ffn1_w_out", [2, FF, D], BF16), dscr("b_ffn2_w_out", [2, FF, D], BF16)]
    wb_even_in = dscr("b_even_w_in", [D, 4096], BF16)
    wb_even_out = dscr("b_even_w_out", [D, D], BF16)
    wb_odd_in = dscr("b_odd_w_in", [D, 3080], BF16)
    wb_odd_out = dscr("b_odd_w_out", [D, D], BF16)

    S = Sched()
    es = contextlib.ExitStack()
    with es:
        def sb(name, shape, dt):
            return es.enter_context(nc.sbuf_tensor("sb_" + name, list(shape), dt))

        x_sb = sb("x_sb", [128, KT, 512], F32)
        hn = sb("hn", [128, KT, 512], BF16)
        NSLOT = 3
        SLOTSZ = FT * 256
        wring = sb("wring", [128, NSLOT, SLOTSZ], BF16)
        xin = sb("xin", [128, 2, D], F32)
        c128 = sb("c128", [128, C128W], F32)
        c64 = sb("c64", [64, C64W], F32)
        identb = sb("identb", [128, 128], BF16)
        rmatb = sb("rmatb", [128, 128], BF16)
        onesb = sb("onesb", [128, 128], BF16)
        normw = sb("normw", [128, 7, KT], F32)
        hgain = sb("hgain", [128, 3, 4], F32)
        lbt = sb("lbt", [128, 4, 4], F32)
        lruv = sb("lruv", [128, 8, 4], F32)
        lrud = sb("lrud", [128, 4, 4], F32)
        dncw = sb("dncw", [128, 4, 12], F32)
        bd = sb("bd", [128, 2, 4, 128], BF16)
        dnb = sb("dnb", [64, 2, 8, 4], F32)
        dnraw = sb("dnraw", [64, 2, 4], F32)
        s_ret = sb("s_ret", [128, 512], F32)
        s_hg = sb("s_hg", [128, 512], F32)
        s_dn = sb("s_dn", [128, 512], F32)
        hstate = sb("hstate", [128, 3, 4], F32)
        hcar = sb("hcar", [128, 16, 3], F32)
        rt = sb("rt", [128, 2, 512], F32)
        ARENA = 51800
        nmr = sb("nmr", [64, 3, 5, 256], F32)
        arena = sb("arena", [128, ARENA], BF16)

        banks = [es.enter_context(nc.psum_tensor("ps%d" % i, [128, 512], F32)) for i in range(8)]
        eng_sems = {e: es.enter_context(nc.semaphore("sem_" + e)) for e in ("pe", "act", "dve", "pool")}
        dma_names = ["w%d" % i for i in range(NSLOT)] + ["xin0", "xin1"] + ["cast%d" % i for i in range(6)] + ["m%s%d" % (q, i) for q in ("sp", "pool") for i in range(Sched.NMISC)]
        dma_sems = {k: es.enter_context(nc.semaphore("dsem_" + k)) for k in dma_names}
        block = es.enter_context(nc.Block())
        es.enter_context(nc.allow_non_contiguous_dma(reason="tiny per-channel vectors"))

        def C128(k):
            o, n = off128[k]
            return c128[:, o:o + n]

        def C64(k):
            o, n = off64[k]
            return c64[:, o:o + n]

        ident = C128("ident")

        bank_ctr = [0]

        sub_ctr = {}

        def P(sub=None):
            if sub is None:
                i = bank_ctr[0] % 8
                bank_ctr[0] += 1
                return banks[i], ("ps", i)
            k, n = sub
            mine = [b for b in range(8) if b % n == k]
            j = sub_ctr.get(sub, 0)
            sub_ctr[sub] = j + 1
            i = mine[j % len(mine)]
            return banks[i], ("ps", i)

        class Arena:
            def __init__(self):
                self.off = 0
                self.gen = 0
                self.raw = {}
                self.live = []

            def reset(self):
                self.off = 0
                self.gen += 1

            def _inherit(self, lo, hi, newres):
                deps = []
                keep = []
                for (a, b, r) in self.live:
                    if r[1] == self.gen or b <= lo or a >= hi:
                        keep.append((a, b, r))
                        continue
                    for k, d in list(S.last_write.items()):
                        if k == r or (isinstance(k, tuple) and len(k) == 2 and k[0] == r):
                            if d is not None:
                                deps.append(d)
                            deps.extend(S.readers.get(k, ()))
                    if a < lo:
                        keep.append((a, lo, r))
                    if b > hi:
                        keep.append((hi, b, r))
                self.live = keep
                if deps:
                    S.inherit.setdefault(newres, []).extend(deps)

            def alloc(self, name, n, dt):
                if dt == F32:
                    ne = 2 * n
                else:
                    ne = n
                ne = (ne + 15) // 16 * 16
                assert self.off + ne <= ARENA, (name, self.off, ne)
                v = arena[:, self.off:self.off + ne]
                res = ("ar", self.gen, name)
                self._inherit(self.off, self.off + ne, res)
                self.live.append((self.off, self.off + ne, res))
                self.raw[name] = v
                self.off += ne
                if dt == F32:
                    v = v.bitcast(F32)[:, 0:n]
                else:
                    v = v[:, 0:n]
                return v, res

        A = Arena()

        def pe(fn, r=(), w=()):
            return S.op("pe", fn, r, w)

        def act(fn, r=(), w=()):
            return S.op("act", fn, r, w)

        def dve(fn, r=(), w=()):
            return S.op("dve", fn, r, w)

        def pool(fn, r=(), w=()):
            return S.op("pool", fn, r, w)

        def mm(out, lhsT, rhs, start, stop, r, w):
            return pe(lambda e: e.matmul(out, lhsT=lhsT, rhs=rhs, start=start, stop=stop), r, w)

        def tr(out, in_, idn, r, w):
            return pe(lambda e: e.transpose(out=out, in_=in_, identity=idn), r, w)

        def a_act(out, in_, func, r, w, bias=None, scale=None):
            kw = {}
            if bias is not None:
                kw["bias"] = bias
            if scale is not None:
                kw["scale"] = scale
            return act(lambda e: e.activation(out=out, in_=in_, func=func, **kw), r, w)

        def v_tt(out, in0, in1, op, r, w, eng="dve"):
            return S.op(eng, lambda e: e.tensor_tensor(out=out, in0=in0, in1=in1, op=op), r, w)

        def v_ts(out, in0, s1, s2, op0, op1, r, w, eng="dve"):
            if op1 is None:
                return S.op(eng, lambda e: e.tensor_scalar(out=out, in0=in0, scalar1=s1, scalar2=None, op0=op0), r, w)
            return S.op(eng, lambda e: e.tensor_scalar(out=out, in0=in0, scalar1=s1, scalar2=s2, op0=op0, op1=op1), r, w)

        def v_stt(out, in0, sc, in1, op0, op1, r, w, eng="dve"):
            return S.op(eng, lambda e: e.scalar_tensor_tensor(out=out, in0=in0, scalar=sc, in1=in1, op0=op0, op1=op1), r, w)

        def v_copy(out, in_, r, w, eng="dve"):
            return S.op(eng, lambda e: e.tensor_copy(out=out, in_=in_), r, w)

        def v_memset(ap, val, w, eng="dve"):
            return S.op(eng, lambda e: e.memset(ap, val), (), w)

        class WStream:
            def __init__(self):
                self.units = []
                self.issued = 0
                self.consumed = 0

            def plan(self, tag, src, shape):
                self.units.append((tag, src, shape))

            @staticmethod
            def group_of(tag):
                if tag[0].startswith("ffn"):
                    which, layer = tag[1], tag[2]
                    return {(0, 0): 0, (1, 0): 2, (0, 1): 3, (1, 1): 5}[(which, layer)]
                return 1 if tag[0].startswith("even") else 4

            def _issue(self):
                i = self.issued
                tag, src, shape = self.units[i]
                S.wait_deps("sp", [cast_dep[self.group_of(tag)]])
                slot = i % NSLOT
                n = 1
                for d in shape:
                    n *= d
                dst = wring[:, slot, 0:n]
                if len(shape) == 2:
                    dst = dst.rearrange("p (a b) -> p a b", a=shape[0])
                srcs = src if isinstance(src, list) else [src]
                if len(srcs) == 1:
                    pairs = [(dst, srcs[0])]
                else:
                    g = len(srcs)
                    d4 = dst.rearrange("p a (g c) -> p a g c", g=g)
                    pairs = [(d4[:, :, k, :], srcs[k]) for k in range(g)]
                for dd, ss in pairs:
                    S.dma("sp", lambda e, dd=dd, ss=ss: e.dma_start(out=dd, in_=ss), "w%d" % slot,
                          reads=[], writes=[("wslot", slot)])
                self.issued += 1

            def next(self, tag):
                while self.issued < min(len(self.units), self.consumed + NSLOT):
                    self._issue()
                i = self.consumed
                t, src, shape = self.units[i]
                assert t == tag, (t, tag)
                slot = i % NSLOT
                n = 1
                for d in shape:
                    n *= d
                v = wring[:, slot, 0:n]
                if len(shape) == 2:
                    v = v.rearrange("p (a b) -> p a b", a=shape[0])
                self.consumed += 1
                return v, ("wslot", slot)

        W = WStream()

        def in_view(wap, c0, ncol):
            return wap.rearrange("(kt p) c -> p kt c", p=128)[:, :, c0:c0 + ncol]

        if TP >= 2048:
            sizes = [512] * (TP // 512 - 2) + [384, 384, 256]
        else:
            sizes = [TP // 2, TP // 2]
        assert sum(sizes) == TP and all(z % 128 == 0 for z in sizes) and sizes[-1] <= 384
        tiles = []
        off = 0
        for i, sz in enumerate(sizes):
            lastp = (i == len(sizes) - 1)
            if not lastp:
                segs = [dict(seq=0, L=sz, start=(i == 0), end=False, prev_same=False)]
                NT_ = sz
            else:
                npc = sz // 64
                segs = [dict(seq=0, L=64, start=(i == 0 and k == 0), end=(k == npc - 1), prev_same=(k > 0)) for k in range(npc)]
                segs += [dict(seq=1, L=64, start=True, end=True, prev_same=False), dict(seq=2, L=64, start=True, end=True, prev_same=False)]
                NT_ = sz + 128
            tiles.append(dict(kind=("m" if lastp else "p"), tok0=off, ptok=sz, NT=NT_, segs=segs, first=(i == 0), last=lastp))
            off += sz

        def plan_ffn(which, layer):
            wi = wb_ffn_in[which][layer]
            wo = wb_ffn_out[which][layer]
            for u in range(11):
                w4 = wi.rearrange("(kt p) (g c) -> p kt g c", p=128, g=2)
                src = [w4[:, :, 0, u * 256:(u + 1) * 256], w4[:, :, 1, u * 256:(u + 1) * 256]]
                W.plan(("ffn_in", which, layer, u), src, (KT, 512))
            for u in range(4):
                src = wo.rearrange("(kt p) c -> p kt c", p=128)[:, :, u * 256:(u + 1) * 256]
                W.plan(("ffn_out", which, layer, u), src, (FT, 256))

        def plan_tile():
            if stage >= 1:
                plan_ffn(0, 0)
            if stage >= 2:
                for u in range(8):
                    W.plan(("even_in", u), in_view(wb_even_in, u * 512, 512), (KT, 512))
                for u in range(4):
                    W.plan(("even_out", u), in_view(wb_even_out, u * 256, 256), (KT, 256))
            if stage >= 3:
                plan_ffn(1, 0)
                plan_ffn(0, 1)
            if stage >= 4:
                W.plan(("odd_in", "bda"), in_view(wb_odd_in, 3072, 8), (KT, 8))
                for u in range(6):
                    W.plan(("odd_in", u), in_view(wb_odd_in, u * 512, 512), (KT, 512))
                for u in range(4):
                    W.plan(("odd_out", u), in_view(wb_odd_out, u * 256, 256), (KT, 256))
            if stage >= 5:
                plan_ffn(1, 1)

        for _ in tiles:
            plan_tile()

        cast_dep = {}

        def cast_w(src2d, dst2d, rows, grp):
            r0 = 0
            while r0 < rows:
                rr = min(256, rows - r0)
                S.dma("pool", lambda e, a=dst2d[r0:r0 + rr, :], b=src2d[r0:r0 + rr, :]: e.dma_start(out=a, in_=b),
                      "cast%d" % grp, reads=[], writes=[("wcast", grp)])
                r0 += rr
            cast_dep[grp] = S.dma_last["cast%d" % grp]

        def cast_group(grp):
            if grp == 0:
                cast_w(w_ffn_in[0][0], wb_ffn_in[0][0], D, 0)
                cast_w(w_ffn_out[0][0], wb_ffn_out[0][0], FF, 0)
            elif grp == 1:
                cast_w(w_even_in, wb_even_in, D, 1)
                cast_w(w_even_out, wb_even_out, D, 1)
            elif grp == 2:
                cast_w(w_ffn_in[1][0], wb_ffn_in[1][0], D, 2)
                cast_w(w_ffn_out[1][0], wb_ffn_out[1][0], FF, 2)
            elif grp == 3:
                cast_w(w_ffn_in[0][1], wb_ffn_in[0][1], D, 3)
                cast_w(w_ffn_out[0][1], wb_ffn_out[0][1], FF, 3)
            elif grp == 4:
                cast_w(w_odd_in, wb_odd_in, D, 4)
                cast_w(w_odd_out, wb_odd_out, D, 4)
            elif grp == 5:
                cast_w(w_ffn_in[1][1], wb_ffn_in[1][1], D, 5)
                cast_w(w_ffn_out[1][1], wb_ffn_out[1][1], FF, 5)

        S.dma("sp", lambda e: e.dma_start(out=c128[:], in_=c128_d), None, writes=["c128"])
        S.dma("sp", lambda e: e.dma_start(out=c64[:], in_=c64_d), None, writes=["c64"])
        S.dma("sp", lambda e: e.dma_start(out=normw[:], in_=norms_d.rearrange("n (kt p) -> p n kt", p=128)), None, writes=["normw"])
        S.dma("sp", lambda e: e.dma_start(out=hgain[:], in_=hnorm_d.rearrange("n (h p) -> p n h", p=128)), None, writes=["hgain"])
        S.dma("sp", lambda e: e.dma_start(out=lbt[:, 0:2, :], in_=lb_logits.rearrange("n (h p) -> p n h", p=128)), None, writes=["lbt"])
        S.dma("sp", lambda e: e.dma_start(out=lruv[:], in_=lru_vecs.rearrange("n (j p) -> p n j", p=128)), None, writes=["lruv"])
        S.dma("sp", lambda e: e.dma_start(out=dncw[:], in_=dn_conv_w.rearrange("n (j p) -> p n j", p=128)), None, writes=["dncw"])
        S.dma("sp", lambda e: e.dma_start(out=dnraw[:], in_=dn_scal.rearrange("(o n) h -> o n h", o=1).to_broadcast([64, 2, 4])), None, writes=["dnraw"])
        bdst_v, _ = A.alloc("bdst", 2 * 4 * 128, F32)
        bdst = bdst_v.rearrange("p (g j c) -> p g j c", g=2, j=4)
        pool(lambda e: e.memset(bdst_v, 0.0), (), ["bdst"])
        for gi, wsrc in enumerate((lru_wa, lru_wx)):
            for n in range(8):
                j, hh = n // 2, n % 2
                S.dma("sp", lambda e, gi=gi, n=n, j=j, hh=hh, wsrc=wsrc: e.dma_start(
                    out=bdst[hh * 64:(hh + 1) * 64, gi, j, hh * 64:(hh + 1) * 64], in_=wsrc[n]), None,
                    reads=[], writes=["bdst"])
        cast_group(0)
        cast_group(1)

        v_copy(identb[:], ident, ["c128"], ["identb"])
        v_copy(rmatb[:], C128("rmat"), ["c128"], ["rmatb"])
        v_copy(onesb[:], C128("ones"), ["c128"], ["onesb"])
        v_copy(bd[:], bdst, ["bdst"], ["bd"])
        v_tt(lbt[:, 2, :], lbt[:, 0, :], lbt[:, 1, :], ALU.subtract, ["lbt"], ["lbt"])
        a_act(lbt[:, 2, :], lbt[:, 2, :], AF.Sigmoid, ["lbt"], ["lbt"])
        v_ts(lbt[:, 3, :], lbt[:, 2, :], -1.0, 1.0, ALU.mult, ALU.add, ["lbt"], ["lbt"])
        a_act(lrud[:, 2, :], lruv[:, 7, :], AF.Exp, ["lruv"], ["lrud"], scale=-1.0)
        a_act(lrud[:, 3, :], lrud[:, 2, :], AF.Ln, ["lrud"], ["lrud"], bias=1.0)
        v_ts(lrud[:, 0, :], lrud[:, 3, :], -LRU_C, None, ALU.mult, None, ["lrud"], ["lrud"])
        v_ts(lrud[:, 1, :], lrud[:, 3, :], -2.0 * LRU_C, None, ALU.mult, None, ["lrud"], ["lrud"])
        a_act(dnraw[:, 0, :], dnraw[:, 0, :], AF.Exp, ["dnraw"], ["dnraw"])
        v_ts(dnraw[:, 0, :], dnraw[:, 0, :], -1.0, None, ALU.mult, None, ["dnraw"], ["dnraw"])
        for cc in range(8):
            v_copy(dnb[:, 0, cc, :], dnraw[:, 1, :], ["dnraw"], ["dnb"])
            v_copy(dnb[:, 1, cc, :], dnraw[:, 0, :], ["dnraw"], ["dnb"])

        def interleave(gen_fns, width, extra=None):
            active = []
            nxt = 0
            ex = extra() if extra is not None else None
            while active or nxt < len(gen_fns) or ex is not None:
                while len(active) < width and nxt < len(gen_fns):
                    active.append(gen_fns[nxt]())
                    nxt += 1
                for g in list(active):
                    try:
                        next(g)
                    except StopIteration:
                        active.remove(g)
                if ex is not None:
                    try:
                        next(ex)
                    except StopIteration:
                        ex = None

        def load_x(tile):
            NT = tile["NT"]
            for b in range(NT // 128):
                sl = b % 2
                if b * 128 < tile["ptok"]:
                    srcb = xp[tile["tok0"] + b * 128:tile["tok0"] + (b + 1) * 128, :]
                else:
                    srcb = xs[b * 128 - tile["ptok"]:(b + 1) * 128 - tile["ptok"], :]
                S.dma("pool", lambda e, sl=sl, srcb=srcb: e.dma_start(out=xin[:, sl, :], in_=srcb),
                      "xin%d" % sl, reads=[], writes=[("xin", sl)])
                for half in range(2):
                    bk, br = P()
                    for q in range(4):
                        kt = half * 4 + q
                        tr(bk[:, q * 128:(q + 1) * 128], xin[:, sl, kt * 128:(kt + 1) * 128], ident, [("xin", sl), "c128"], [br])
                    act(lambda e, bk=bk, half=half, b=b: e.copy(
                        out=x_sb[:, half * 4:half * 4 + 4, b * 128:(b + 1) * 128],
                        in_=bk[:, :].rearrange("p (q t) -> p q t", q=4)), [br], ["x"])

        def rmsnorm(NT, nidx):
            sq, sqr = A.alloc("nsq", KT * NT, BF16)
            sq3 = sq.rearrange("p (k t) -> p k t", k=KT)
            act(lambda e: e.activation(out=sq3, in_=x_sb[:, :, 0:NT], func=AF.Square), ["x"], [sqr])
            bk, br = P()
            for kt in range(KT):
                mm(bk[:, 0:NT], onesb[:], sq3[:, kt, :], kt == 0, kt == KT - 1, [sqr, "onesb"], [br])
            a_act(rt[:, 0, 0:NT], bk[:, 0:NT], AF.Ln, [br], ["rt0"], bias=EPS, scale=1.0 / D)
            a_act(rt[:, 1, 0:NT], rt[:, 0, 0:NT], AF.Exp, ["rt0"], ["rt1"], scale=-0.5)
            for kt in range(KT):
                v_stt(hn[:, kt, 0:NT], x_sb[:, kt, 0:NT], normw[:, nidx, kt:kt + 1], rt[:, 1, 0:NT], ALU.mult, ALU.mult,
                      ["x", "normw", "rt1"], ["hn"])

        def rstd_only(NT):
            sq, sqr = A.alloc("nsq", KT * NT, BF16)
            sq3 = sq.rearrange("p (k t) -> p k t", k=KT)
            act(lambda e: e.activation(out=sq3, in_=x_sb[:, :, 0:NT], func=AF.Square), ["x"], [sqr])
            bk, br = P()
            for kt in range(KT):
                mm(bk[:, 0:NT], onesb[:], sq3[:, kt, :], kt == 0, kt == KT - 1, [sqr, "onesb"], [br])
            a_act(rt[:, 0, 0:NT], bk[:, 0:NT], AF.Ln, [br], ["rt0"], bias=EPS, scale=1.0 / D)
            a_act(rt[:, 1, 0:NT], rt[:, 0, 0:NT], AF.Exp, ["rt0"], ["rt1"], scale=-0.5)

        def prenorm(NT, nidx, mo):
            v_ts(hn[:, mo, 0:NT], x_sb[:, mo, 0:NT], normw[:, nidx, mo:mo + 1], None, ALU.mult, None, ["x", "normw"], ["hn"])

        def ffn(tile, which, layer, next_norm=None):
            NT = tile["NT"]
            A.reset()
            hid, hidr = A.alloc("hid", FT * NT, BF16)
            hid3 = hid.rearrange("p (m t) -> p m t", m=FT)
            sg = [A.alloc("sg%d" % i, NT, F32) for i in range(2)]
            su = [A.alloc("su%d" % i, NT, F32) for i in range(2)]
            for u in range(11):
                Wv, wr = W.next(("ffn_in", which, layer, u))
                for mi in range(2):
                    m = 2 * u + mi
                    bg, bgr = P()
                    for kt in range(KT):
                        mm(bg[:, 0:NT], Wv[:, kt, mi * 128:(mi + 1) * 128], hn[:, kt, 0:NT], kt == 0, kt == KT - 1, [wr, "hn"], [bgr])
                    bu, bur = P()
                    for kt in range(KT):
                        mm(bu[:, 0:NT], Wv[:, kt, 256 + mi * 128:256 + (mi + 1) * 128], hn[:, kt, 0:NT], kt == 0, kt == KT - 1,
                           [wr, "hn"], [bur])
                    if m == 0:
                        rstd_only(NT)
                    sgv, sgr = sg[m % 2]
                    suv, sur = su[m % 2]
                    v_tt(sgv, bg[:, 0:NT], rt[:, 1, 0:NT], ALU.mult, [bgr, "rt1"], [sgr])
                    a_act(sgv, sgv, AF.Silu, [sgr], [sgr])
                    v_tt(suv, bu[:, 0:NT], rt[:, 1, 0:NT], ALU.mult, [bur, "rt1"], [sur])
                    v_tt(hid3[:, m, :], sgv, suv, ALU.mult, [sgr, sur], [(hidr, m)])
            for u in range(4):
                Wv, wr = W.next(("ffn_out", which, layer, u))
                for mi in range(2):
                    mo = 2 * u + mi
                    bk, br = P()
                    for kt in range(FT):
                        mm(bk[:, 0:NT], Wv[:, kt, mi * 128:(mi + 1) * 128], hid3[:, kt, :], kt == 0, kt == FT - 1,
                           [wr, (hidr, kt)], [br])
                    v_stt(x_sb[:, mo, 0:NT], bk[:, 0:NT], 0.5, x_sb[:, mo, 0:NT], ALU.mult, ALU.add, [br, "x"], ["x"])
                    if next_norm is not None:
                        prenorm(NT, next_norm, mo)

        def out_proj(tile, tagname, mix3, mixr, next_norm=None):
            NT = tile["NT"]
            for u in range(4):
                Wv, wr = W.next((tagname, u))
                for mi in range(2):
                    mo = 2 * u + mi
                    bk, br = P()
                    for kt in range(KT):
                        mm(bk[:, 0:NT], Wv[:, kt, mi * 128:(mi + 1) * 128], mix3[:, kt, :], kt == 0, kt == KT - 1, [wr, mixr], [br])
                    v_tt(x_sb[:, mo, 0:NT], bk[:, 0:NT], x_sb[:, mo, 0:NT], ALU.add, [br, "x"], ["x"])
                    if next_norm is not None:
                        prenorm(NT, next_norm, mo)

        def head_norm(NT, osb3, osbr, gate3, gater, gidx, mix3, mixr, koff, scratch=None):
            if scratch is None:
                sqb, sqbr = A.alloc("hsq%d" % koff, 4 * NT, BF16)
                tm, tmr = A.alloc("htm%d" % koff, NT, F32)
                sqbl = [sqbr]
            else:
                sqb, sqbl, tm, tmr = scratch
            sqb3 = sqb.rearrange("p (h t) -> p h t", h=4)
            osbl = osbr if isinstance(osbr, list) else [osbr]
            act(lambda e: e.activation(out=sqb3, in_=osb3, func=AF.Square), osbl, sqbl)
            for h in range(4):
                bk, br = P()
                mm(bk[:, 0:NT], onesb[:], sqb3[:, h, :], True, True, sqbl + ["onesb"], [br])
                a_act(rt[:, 0, 0:NT], bk[:, 0:NT], AF.Ln, [br], ["rt0"], bias=EPS, scale=1.0 / 128.0)
                a_act(rt[:, 1, 0:NT], rt[:, 0, 0:NT], AF.Exp, ["rt0"], ["rt1"], scale=-0.5)
                v_tt(tm, osb3[:, h, :], rt[:, 1, 0:NT], ALU.mult, osbl + ["rt1"], [tmr])
                v_stt(mix3[:, koff + h, :], tm, hgain[:, gidx, h:h + 1], gate3[:, h, :], ALU.mult, ALU.mult,
                      [tmr, "hgain", gater], [mixr])

        def seg_of_chunk(tile, c):
            if tile["kind"] == "p":
                return 0, 0, (c == 0), (c == tile["NT"] // 64 - 1)
            sg = tile["segs"][c]
            if sg["seq"] == 0:
                return 0, c, (c == 0), sg["end"]
            return sg["seq"], c, True, True

        def state_io(tile, seq, start, end, s32, s32r, st_in, o_list, when):
            if when == "start":
                if seq == 0:
                    if tile["first"]:
                        v_memset(s32[:], 0.0, [s32r], eng="pool")
                else:
                    S.dma("pool", lambda e: e.dma_start(out=s32[:].rearrange("p (h e) -> p h e", h=4),
                                                        in_=st_in[seq - 1].rearrange("h d e -> d h e")),
                          None, reads=[], writes=[s32r])
            else:
                if seq == 0:
                    if tile["last"]:
                        S.dma("pool", lambda e: e.dma_start(out=o_list[0].rearrange("h d e -> d h e"),
                                                            in_=s32[:].rearrange("p (h e) -> p h e", h=4)),
                              None, reads=[s32r], writes=[])
                else:
                    S.dma("pool", lambda e: e.dma_start(out=o_list[1][seq - 1].rearrange("h d e -> d h e"),
                                                        in_=s32[:].rearrange("p (h e) -> p h e", h=4)),
                          None, reads=[s32r], writes=[])

        def linattn_core(tile, kind, qF, qFr, kF, kFr, kS, kSr, vtm, vtmr, osb3, osbr, s32, s32r, st_in, o_list,
                         ebend=None, ebendr=None, extra=None):
            NT = tile["NT"]
            NC = NT // 64
            sbf, sbfr = A.alloc("sbf_" + kind, (NC + 1) * 512, BF16)
            sbf3 = sbf.rearrange("p (c n) -> p c n", c=NC + 1)
            WD = 3
            attm = [A.alloc("attm%d_%s" % (i, kind), 256, BF16) for i in range(WD)]
            kdt = [A.alloc("kdt%d_%s" % (i, kind), 512, BF16) for i in range(WD)]
            mask = C64("retmask") if kind == "ret" else C64("incl4")
            done = {}

            def chunk_gen(c):
                seq, sgi, sstart, send = seg_of_chunk(tile, c)
                cs = slice(c * 64, (c + 1) * 64)
                bA, bAr = P((c % WD, WD + 1))
                for h in range(4):
                    mm(bA[0:64, h * 64:(h + 1) * 64], kF[:, h, cs], qF[:, h, cs], True, True, [kFr, qFr], [bAr])
                yield
                av, avr = attm[c % WD]
                v_tt(av[0:64, :], bA[0:64, 0:256], mask, ALU.mult, [bAr, "c64"], [avr])
                bT, bTr = P((c % WD, WD + 1))
                bTb = bT[0:64, 0:256].bitcast(BF16)
                for h in range(4):
                    tr(bTb[:, h * 128:(h + 1) * 128], kS[:, h, cs], identb[:], [kSr, "identb"], [bTr])
                yield
                kv, kvr = kdt[c % WD]
                if kind == "ret":
                    v_tt(kv[0:64, :], bTb, C64("kdec"), ALU.mult, [bTr, "c64"], [kvr])
                else:
                    act(lambda e, kv=kv, bTb=bTb: e.copy(out=kv[0:64, :], in_=bTb), [bTr], [kvr])
                yield
                while c > 0 and not done.get(c - 1):
                    yield
                if sstart:
                    state_io(tile, seq, sstart, send, s32, s32r, st_in, o_list, "start")
                    act(lambda e, c=c: e.copy(out=sbf3[:, c, :], in_=s32[:]), [s32r], [(sbfr, c)])
                bO, bOr = P((c % WD, WD + 1))
                for h in range(4):
                    mm(bO[:, h * 64:(h + 1) * 64], vtm[0:64, c, h * 128:(h + 1) * 128], av[0:64, h * 64:(h + 1) * 64], True, False,
                       [vtmr, avr], [bOr])
                    mm(bO[:, h * 64:(h + 1) * 64], sbf3[:, c, h * 128:(h + 1) * 128], qF[:, h, cs], False, True,
                       [(sbfr, c), qFr], [bOr])
                act(lambda e, bO=bO, cs=cs: e.copy(out=osb3[:, :, cs], in_=bO[:, 0:256].rearrange("p (h t) -> p h t", h=4)),
                    [bOr], [(osbr, c)])
                bS, bSr = P((c % WD, WD + 1))
                for h in range(4):
                    mm(bS[:, h * 128:(h + 1) * 128], kv[0:64, h * 128:(h + 1) * 128], vtm[0:64, c, h * 128:(h + 1) * 128], True, True,
                       [kvr, vtmr], [bSr])
                for h in range(4):
                    hs_ = slice(h * 128, (h + 1) * 128)
                    if kind == "ret":
                        sc = SDEC[h]
                        rr = [s32r, bSr]
                    else:
                        sc = ebend[:, h, c:c + 1]
                        rr = [s32r, bSr, ebendr]
                    v_stt(s32[:, hs_], s32[:, hs_], sc, bS[:, hs_], ALU.mult, ALU.add, rr, [s32r])
                if send:
                    state_io(tile, seq, sstart, send, s32, s32r, st_in, o_list, "end")
                else:
                    act(lambda e, c=c: e.copy(out=sbf3[:, c + 1, :], in_=s32[:]), [s32r], [(sbfr, c + 1)])
                done[c] = True

            gens = [(lambda c=c: chunk_gen(c)) for c in range(NC)]
            interleave(gens, WD, extra=(None if extra is None else (lambda: extra((WD, WD + 1)))))
            return [(osbr, c) for c in range(NC)]

        def gate_gen(NT, Wv, wr, gate3, gater, alloc):
            for h in range(4):
                bk, br = yield from alloc()
                for kt in range(KT):
                    mm(bk[:, 0:NT], Wv[:, kt, h * 128:(h + 1) * 128], hn[:, kt, 0:NT], kt == 0, kt == KT - 1, [wr, "hn"], [br])
                yield
                a_act(gate3[:, h, :], bk[:, 0:NT], AF.Silu, [br], [gater])
                yield br

        def tm_proj(tile, Wv, wr, vt3, vtr):
            NT = tile["NT"]
            for c in range(NT // 64):
                bk, br = P()
                for kt in range(KT):
                    mm(bk[0:64, :], hn[:, kt, c * 64:(c + 1) * 64], Wv[:, kt, 0:512], kt == 0, kt == KT - 1, [wr, "hn"], [br])
                act(lambda e, bk=bk, c=c: e.copy(out=vt3[0:64, c, :], in_=bk[0:64, :]), [br], [vtr])

        def fm_proj(NT, Wv, wr, h, sub=None):
            bk, br = P(sub)
            for kt in range(KT):
                mm(bk[:, 0:NT], Wv[:, kt, h * 128:(h + 1) * 128], hn[:, kt, 0:NT], kt == 0, kt == KT - 1, [wr, "hn"], [br])
            return bk, br

        def even_mixer(tile):
            NT = tile["NT"]
            NC = NT // 64
            A.reset()
            mix, mixr = A.alloc("mix", KT * NT, BF16)
            mix3 = mix.rearrange("p (k t) -> p k t", k=KT)
            rmsnorm(NT, 2)
            tab, tabr = A.alloc("tab", 2 * NT, F32)
            tab3 = tab.rearrange("p (a t) -> p a t", a=2)
            S.dma("pool", lambda e: e.dma_start(out=tab3, in_=rope_d[:, :, tile["tok0"]:tile["tok0"] + NT]), None,
                  reads=[], writes=[tabr])

            def al3(name, dt):
                v, r = A.alloc(name, 4 * NT, dt)
                return v.rearrange("p (h t) -> p h t", h=4), r

            qd3, qdr = al3("qd", BF16)
            kr3, krr = al3("krot", BF16)
            vt, vtr = A.alloc("vtm", NC * 512, BF16)
            vt3 = vt.rearrange("p (c n) -> p c n", c=NC)
            gate3, gater = al3("gate", F32)
            osb3, osbr = al3("osb", F32)
            xbf = [A.alloc("xbf%d" % i, NT, BF16) for i in range(2)]
            ta = [A.alloc("ta%d" % i, NT, F32) for i in range(2)]
            tb = [A.alloc("tb%d" % i, NT, F32) for i in range(2)]
            qdecv = C128("qdec").rearrange("p (h t) -> p h t", h=4)

            def rope_unit(Wv, wr, dst3, dstr, isq):
                for h in range(4):
                    bk, br = fm_proj(NT, Wv, wr, h)
                    xv, xr = xbf[h % 2]
                    act(lambda e, xv=xv, bk=bk: e.copy(out=xv, in_=bk[:, 0:NT]), [br], [xr])
                    b2, b2r = P()
                    mm(b2[:, 0:NT], rmatb[:], xv, True, True, [xr, "rmatb"], [b2r])
                    tav, tar = ta[h % 2]
                    tbv, tbr = tb[h % 2]
                    v_tt(tav, bk[:, 0:NT], tab3[:, 0, :], ALU.mult, [br, tabr], [tar])
                    v_tt(tbv, b2[:, 0:NT], tab3[:, 1, :], ALU.mult, [b2r, tabr], [tbr])
                    if isq:
                        v_tt(tav, tav, tbv, ALU.add, [tar, tbr], [tar])
                        v_tt(dst3[:, h, :].rearrange("p (c t) -> p c t", t=64),
                             tav.rearrange("p (c t) -> p c t", t=64),
                             qdecv[:, h:h + 1, :].to_broadcast([128, NC, 64]), ALU.mult, [tar, "c128"], [dstr])
                    else:
                        v_tt(dst3[:, h, :], tav, tbv, ALU.add, [tar, tbr], [dstr])

            Wv, wr = W.next(("even_in", 0))
            rope_unit(Wv, wr, qd3, qdr, True)
            Wv, wr = W.next(("even_in", 1))
            rope_unit(Wv, wr, kr3, krr, False)
            Wv, wr = W.next(("even_in", 2))
            tm_proj(tile, Wv, wr, vt3, vtr)
            Wg, wgr = W.next(("even_in", 3))

            def sub_alloc(sub):
                def alloc():
                    return P(sub)
                    yield
                return alloc

            osbr = linattn_core(tile, "ret", qd3, qdr, kr3, krr, kr3, krr, vt3, vtr, osb3, osbr, s_ret, "s_ret", st_ret, o_ret,
                                extra=lambda sub: gate_gen(NT, Wg, wgr, gate3, gater, sub_alloc(sub)))
            head_norm(NT, osb3, osbr, gate3, gater, 0, mix3, mixr, 0)

            A.reset()
            mix, mixr = A.alloc("mix", KT * NT, BF16)
            mix3 = mix.rearrange("p (k t) -> p k t", k=KT)
            qe3, qer = al3("qe", BF16)
            ke3, ker = al3("ke", BF16)
            kn3, knr = al3("kend", BF16)
            vt2, vt2r = A.alloc("vtm2", NC * 512, BF16)
            vt23 = vt2.rearrange("p (c n) -> p c n", c=NC)
            gate23, gate2r = al3("gate2", F32)
            osb23, osb2r = al3("osb2", F32)
            qs3, qsr = al3("qsil", F32)
            eb3, ebr = al3("eb", F32)
            fb3, fbr = al3("fb", F32)
            ebend, ebendr = A.alloc("ebend", 4 * NC, F32)
            ebend3 = ebend.rearrange("p (h c) -> p h c", h=4)
            t1 = [A.alloc("t1_%d" % i, NT, F32) for i in range(2)]
            Wv, wr = W.next(("even_in", 4))
            for h in range(4):
                bk, br = fm_proj(NT, Wv, wr, h)
                a_act(qs3[:, h, :], bk[:, 0:NT], AF.Silu, [br], [qsr])
            Wv, wr = W.next(("even_in", 5))
            for h in range(4):
                bk, br = fm_proj(NT, Wv, wr, h)
                tv, tr_ = t1[h % 2]
                a_act(tv, bk[:, 0:NT], AF.Sigmoid, [br], [tr_])
                v_ts(fb3[:, h, :], tv, lbt[:, 3, h:h + 1], lbt[:, 2, h:h + 1], ALU.mult, ALU.add, [tr_, "lbt"], [(fbr, h)])
                v_ts(eb3[:, h, :], fb3[:, h, :], -1.0, 1.0, ALU.mult, ALU.add, [(fbr, h)], [(ebr, h)])
                a_act(fb3[:, h, :], fb3[:, h, :], AF.Ln, [(fbr, h)], [(fbr, h)])
                dve(lambda e, h=h: e.tensor_tensor_scan(out=fb3[:, h, :], data0=C128("reset")[:, 0:NT], data1=fb3[:, h, :],
                                                        initial=0.0, op0=ALU.mult, op1=ALU.add),
                    [(fbr, h), "c128"], [(fbr, h)])
                tv2, tr2 = t1[(h + 1) % 2]
                a_act(tv2, fb3[:, h, :], AF.Exp, [(fbr, h)], [tr2], scale=-1.0)
                v_tt(ke3[:, h, :], eb3[:, h, :], tv2, ALU.mult, [(ebr, h), tr2], [ker])
                a_act(eb3[:, h, :], fb3[:, h, :], AF.Exp, [(fbr, h)], [(ebr, h)])
                v_tt(qe3[:, h, :], qs3[:, h, :], eb3[:, h, :], ALU.mult, [qsr, (ebr, h)], [qer])
                act(lambda e, h=h: e.copy(out=ebend3[:, h, :], in_=eb3[:, h, :].rearrange("p (c t) -> p c t", t=64)[:, :, 63]),
                    [(ebr, h)], [ebendr])
                v_tt(kn3[:, h, :].rearrange("p (c t) -> p c t", t=64), ke3[:, h, :].rearrange("p (c t) -> p c t", t=64),
                     ebend3[:, h, :].unsqueeze(2).to_broadcast([128, NC, 64]), ALU.mult, [ker, ebendr], [knr])
            Wv, wr = W.next(("even_in", 6))
            tm_proj(tile, Wv, wr, vt23, vt2r)
            Wg2, wg2r = W.next(("even_in", 7))
            osb2r = linattn_core(tile, "hg", qe3, qer, ke3, ker, kn3, knr, vt23, vt2r, osb23, osb2r, s_hg, "s_hg", st_hg, o_hg,
                                 ebend=ebend3, ebendr=ebendr,
                                 extra=lambda sub: gate_gen(NT, Wg2, wg2r, gate23, gate2r, sub_alloc(sub)))
            head_norm(NT, osb23, osb2r, gate23, gate2r, 1, mix3, mixr, 4)
            out_proj(tile, "even_out", mix3, mixr, next_norm=4)

        def odd_mixer(tile):
            NT = tile["NT"]
            NC = NT // 64
            nseg = len(tile["segs"])
            L = tile["segs"][0]["L"]
            SEGS = tile["segs"]
            def al3(name, dt, n=4):
                v, r = A.alloc(name, n * NT, dt)
                return v.rearrange("p (h t) -> p h t", h=n), r

            def common():
                A.reset()
                mix, mixr = A.alloc("mix", KT * NT, BF16)
                beta, betar = A.alloc("beta", NC * 4, F32)
                loga, logar = A.alloc("loga", NC * 4, F32)
                egdk, egdkr = A.alloc("egdk", NC * 8, F32)
                egrow3, egr = al3("egrow", F32)
                return (mix.rearrange("p (k t) -> p k t", k=KT), mixr, beta.rearrange("p (c h) -> p c h", c=NC), betar,
                        loga.rearrange("p (c h) -> p c h", c=NC), logar, egdk, egdk.rearrange("p (c h) -> p c h", c=NC), egdkr,
                        egrow3, egr)

            mix3, mixr, beta3, betar, loga3, logar, egdk, egdk3, egdkr, egrow3, egr = common()
            rmsnorm(NT, 3)

            Wv, wr = W.next(("odd_in", "bda"))
            bk, br = P()
            for c in range(NC):
                for kt in range(KT):
                    mm(bk[0:64, c * 8:(c + 1) * 8], hn[:, kt, c * 64:(c + 1) * 64], Wv[:, kt, 0:8], kt == 0, kt == KT - 1, [wr, "hn"], [br])
            bk3 = bk[0:64, 0:NC * 8].rearrange("p (c h) -> p c h", c=NC)
            a_act(beta3[0:64], bk3[:, :, 0:4], AF.Sigmoid, [br], [betar])
            v_tt(loga3[0:64], bk3[:, :, 4:8], dnb[:, 0, 0:NC, :], ALU.add, [br, "dnb"], [logar])
            a_act(loga3[0:64], loga3[0:64], AF.Exp, [logar], [logar])
            a_act(loga3[0:64], loga3[0:64], AF.Ln, [logar], [logar], bias=1.0)
            v_tt(loga3[0:64], loga3[0:64], dnb[:, 1, 0:NC, :], ALU.mult, [logar, "dnb"], [logar])
            b2, b2r = P()
            for c in range(NC):
                mm(b2[0:64, c * 8:c * 8 + 4], C64("U"), loga3[0:64, c, :], True, True, ["c64", logar], [b2r])
                mm(b2[0:64, c * 8 + 4:c * 8 + 8], C64("Urev"), loga3[0:64, c, :], True, True, ["c64", logar], [b2r])
            a_act(egdk[0:64, :], b2[0:64, 0:NC * 8], AF.Exp, [b2r], [egdkr])
            def make_X(c, Xv, Xr):
                X3 = Xv[0:64, 0:256].rearrange("p (h t) -> p h t", h=4)
                v_tt(X3, C64("U").unsqueeze(1).to_broadcast([64, 4, 64]),
                     loga3[0:64, c, :].unsqueeze(2).to_broadcast([64, 4, 64]), ALU.mult, ["c64", logar], [Xr])
                return X3

            Xt = [A.alloc("Xt%d" % i, 256, F32) for i in range(2)]
            for c in range(NC):
                Xv, Xr = Xt[c % 2]
                X3 = make_X(c, Xv, Xr)
                bE, bEr = P()
                for h in range(4):
                    mm(bE[:, h * 64:(h + 1) * 64], C64("ones")[:, 0:128], X3[:, h, :], True, True, ["c64", Xr], [bEr])
                act(lambda e, bE=bE, c=c: e.activation(out=egrow3[:, :, c * 64:(c + 1) * 64],
                                                       in_=bE[:, 0:256].rearrange("p (h t) -> p h t", h=4), func=AF.Exp),
                    [bEr], [egr])

            cvs = {}

            def conv_alloc(nbuf):
                cvs["nbuf"] = nbuf
                xh, xhr = A.alloc("xh", 4 * nseg * (3 + L), F32)
                cvs["xh4"] = xh.rearrange("p (j s t) -> p j s t", j=4, s=nseg)
                cvs["xhr"] = xhr
                cvs["cst"] = [A.alloc("cst%d" % i, 128, F32) for i in range(2)]
                cvs["cv"] = [A.alloc("cv%d" % i, NT, F32) for i in range(nbuf)]
                cvs["n"] = 0

            def conv_unit(g, Wv, wr, consume):
                xh4, xhr = cvs["xh4"], cvs["xhr"]
                nb = cvs["nbuf"]

                def tile_gen(j):
                    sub = (j % nb, nb)
                    gj = g * 4 + j
                    bk, br = fm_proj(NT, Wv, wr, j, sub)
                    for si, sg in enumerate(SEGS):
                        seq = sg["seq"]
                        if sg["prev_same"]:
                            continue
                        if seq == 0:
                            if sg["start"]:
                                v_memset(xh4[:, j, si, 0:3], 0.0, [(xhr, j)], eng="pool")
                            else:
                                v_copy(xh4[:, j, si, 0:3], hcar[:, gj, :], [("hcar", gj)], [(xhr, j)], eng="pool")
                        else:
                            if g == 0:
                                src = st_lc[seq - 1][:, j * 128:(j + 1) * 128]
                            else:
                                src = st_dc[seq - 1][:, (gj - 4) * 128:(gj - 3) * 128]
                            S.dma("pool", lambda e, j=j, si=si, src=src: e.dma_start(out=xh4[:, j, si, 0:3], in_=src.rearrange("r p -> p r")),
                                  None, reads=[], writes=[(xhr, j)])
                    act(lambda e, bk=bk, j=j: e.copy(out=xh4[:, j, :, 3:3 + L], in_=bk[:, 0:NT].rearrange("p (s t) -> p s t", s=nseg)),
                        [br], [(xhr, j)])
                    ks = [si for si, sg in enumerate(SEGS) if sg["prev_same"]]
                    if ks:
                        k0, k1 = ks[0], ks[-1] + 1
                        v_copy(xh4[:, j, k0:k1, 0:3], xh4[:, j, k0 - 1:k1 - 1, L:L + 3], [(xhr, j)], [(xhr, j)], eng="pool")
                    p_last = max([si for si, sg in enumerate(SEGS) if sg["seq"] == 0])
                    if not SEGS[p_last]["end"]:
                        v_copy(hcar[:, gj, :], xh4[:, j, p_last, L:L + 3], [(xhr, j)], [("hcar", gj)], eng="pool")
                    yield
                    cvv, cvr = cvs["cv"][j % nb]
                    cv3 = cvv.rearrange("p (s t) -> p s t", s=nseg)
                    if g == 0:
                        wcol = lambda k, j=j: lruv[:, k, j:j + 1]
                        v_ts(cv3, xh4[:, j, :, 0:L], wcol(0), lruv[:, 4, j:j + 1], ALU.mult, ALU.add, [(xhr, j), "lruv"], [cvr])
                    else:
                        wcol = lambda k, gj=gj: dncw[:, k, gj - 4:gj - 3]
                        v_ts(cv3, xh4[:, j, :, 0:L], wcol(0), None, ALU.mult, None, [(xhr, j), "dncw"], [cvr])
                    for k in range(1, 4):
                        v_stt(cv3, xh4[:, j, :, k:k + L], wcol(k), cv3, ALU.mult, ALU.add, [(xhr, j), "lruv", "dncw", cvr], [cvr],
                              eng="dve")
                    for si, sg in enumerate(SEGS):
                        seq = sg["seq"]
                        if sg["end"]:
                            bt, btr = P(sub)
                            tr(bt[0:3, 0:128], xh4[:, j, si, L:L + 3], ident, [(xhr, j), "c128"], [btr])
                            cv_, cvr_ = cvs["cst"][cvs["n"] % 2]
                            cvs["n"] += 1
                            act(lambda e, bt=bt, cv_=cv_: e.copy(out=cv_[0:3, :], in_=bt[0:3, 0:128]), [btr], [cvr_])
                            if g == 0:
                                dd = (o_lc[0][0] if seq == 0 else o_lc[1][seq - 1])[:, j * 128:(j + 1) * 128]
                            else:
                                dd = (o_dc[0][0] if seq == 0 else o_dc[1][seq - 1])[:, (gj - 4) * 128:(gj - 3) * 128]
                            S.dma("pool", lambda e, dd=dd, cv_=cv_: e.dma_start(out=dd, in_=cv_[0:3, :]), None, reads=[cvr_], writes=[])
                    yield
                    yield from consume(j, cvv, cvr, sub, nb)

                interleave([(lambda j=j: tile_gen(j)) for j in range(4)], nb)

            conv_alloc(4)
            hs3, hsr = al3("hs", F32)
            tmpA = [A.alloc("lA%d" % i, NT, F32) for i in range(4)]
            tmpB = [A.alloc("lB%d" % i, NT, F32) for i in range(4)]
            tmpC = [A.alloc("lC%d" % i, NT, F32) for i in range(4)]
            lcb = [A.alloc("lcb%d" % i, NT, BF16) for i in range(4)]

            def lru_consume(j, cvv, cvr, sub, nb):
                lb_, lbr_ = lcb[j % nb]
                act(lambda e: e.copy(out=lb_, in_=cvv), [cvr], [lbr_])
                br_, brr = P(sub)
                mm(br_[:, 0:NT], bd[:, 0, j, :], lb_, True, True, ["bd", lbr_], [brr])
                bi_, bir = P(sub)
                mm(bi_[:, 0:NT], bd[:, 1, j, :], lb_, True, True, ["bd", lbr_], [bir])
                yield
                rv, rr = tmpA[j % nb]
                iv, ir = tmpB[j % nb]
                av, ar = tmpC[j % nb]
                a_act(rv, br_[:, 0:NT], AF.Sigmoid, [brr, "lruv"], [rr], bias=lruv[:, 5, j:j + 1])
                a_act(iv, bi_[:, 0:NT], AF.Sigmoid, [bir, "lruv"], [ir], bias=lruv[:, 6, j:j + 1])
                yield
                a_act(av, rv, AF.Exp, [rr, "lrud"], [ar], scale=lrud[:, 0, j:j + 1])
                a_act(rv, rv, AF.Exp, [rr, "lrud"], [rr], scale=lrud[:, 1, j:j + 1])
                v_tt(iv, iv, cvv, ALU.mult, [ir, cvr], [ir])
                yield
                a_act(rv, rv, AF.Sqrt, [rr], [rr], bias=1.0, scale=-1.0)
                yield
                v_tt(iv, iv, rv, ALU.mult, [ir, rr], [ir])
                for si, sg in enumerate(SEGS):
                    seq = sg["seq"]
                    if seq == 0 and sg["start"]:
                        v_memset(hstate[:, 0, j:j + 1], 0.0, [("hstate", j)])
                    elif seq != 0:
                        S.dma("pool", lambda e, seq=seq, j=j: e.dma_start(
                            out=hstate[:, seq, j:j + 1], in_=st_lh[seq - 1:seq, j * 128:(j + 1) * 128].rearrange("o p -> p o")),
                            None, reads=[], writes=[("hstate", j)])
                    cols = slice(si * L, (si + 1) * L)
                    dve(lambda e, cols=cols, seq=seq: e.tensor_tensor_scan(out=hs3[:, j, cols], data0=av[:, cols], data1=iv[:, cols],
                                                                          initial=hstate[:, seq, j:j + 1], op0=ALU.mult, op1=ALU.add),
                        [ar, ir, ("hstate", j)], [(hsr, j)])
                    v_copy(hstate[:, seq, j:j + 1], hs3[:, j, (si + 1) * L - 1:(si + 1) * L], [(hsr, j)], [("hstate", j)])

            Wv, wr = W.next(("odd_in", 0))
            conv_unit(0, Wv, wr, lru_consume)
            Wv, wr = W.next(("odd_in", 1))
            for j in range(4):
                bk, br = fm_proj(NT, Wv, wr, j)
                inn, innr = tmpB[j % 2]
                a_act(inn, bk[:, 0:NT], AF.Gelu_apprx_tanh, [br], [innr])
                v_tt(mix3[:, j, :], inn, hs3[:, j, :], ALU.mult, [innr, (hsr, j)], [mixr])
            for si, sg in enumerate(SEGS):
                seq = sg["seq"]
                if sg["end"]:
                    dst = o_lh[0][0:1, :] if seq == 0 else o_lh[1][seq - 1:seq, :]
                    S.dma("pool", lambda e, dst=dst, seq=seq: e.dma_start(out=dst.rearrange("o (j p) -> p (o j)", p=128), in_=hstate[:, seq, :]),
                          None, reads=[("hstate", j) for j in range(4)] + [(hsr, j) for j in range(4)], writes=[])

            mix3, mixr, beta3, betar, loga3, logar, egdk, egdk3, egdkr, egrow3, egr = common()
            conv_alloc(2)
            tmpA = [A.alloc("lA%d" % i, NT, F32) for i in range(2)]
            qF3, qFr = al3("qF", BF16)
            qg3, qgr = al3("qg", BF16)
            kF3, kFr = al3("kF", BF16)
            vF3, vFr = al3("vF", BF16)
            sqt = [A.alloc("sqt%d" % i, NT, BF16) for i in range(2)]

            def qk_consume(isq):
                def f(j, cvv, cvr, sub, nb):
                    a_act(cvv, cvv, AF.Silu, [cvr], [cvr])
                    sv, sr = sqt[j % nb]
                    a_act(sv, cvv, AF.Square, [cvr], [sr])
                    bk, br = P(sub)
                    mm(bk[:, 0:NT], onesb[:], sv, True, True, [sr, "onesb"], [br])
                    yield
                    tv, tvr = tmpA[j % nb]
                    if isq:
                        a_act(tv, bk[:, 0:NT], AF.Ln, [br], [tvr], bias=EPS * 128.0, scale=128.0)
                    else:
                        a_act(tv, bk[:, 0:NT], AF.Ln, [br], [tvr], bias=EPS, scale=1.0)
                    a_act(tv, tv, AF.Exp, [tvr], [tvr], scale=-0.5)
                    yield
                    if isq:
                        v_tt(cvv, cvv, tv, ALU.mult, [cvr, tvr], [cvr])
                        v_copy(qF3[:, j, :], cvv, [cvr], [qFr], eng="pool")
                        v_tt(qg3[:, j, :], cvv, egrow3[:, j, :], ALU.mult, [cvr, egr], [qgr])
                    else:
                        v_tt(kF3[:, j, :], cvv, tv, ALU.mult, [cvr, tvr], [kFr])
                return f

            def v_consume(j, cvv, cvr, sub, nb):
                a_act(vF3[:, j, :], cvv, AF.Silu, [cvr], [vFr])
                yield

            Wv, wr = W.next(("odd_in", 2))
            conv_unit(1, Wv, wr, qk_consume(True))
            Wv, wr = W.next(("odd_in", 3))
            conv_unit(2, Wv, wr, qk_consume(False))
            Wv, wr = W.next(("odd_in", 4))
            conv_unit(3, Wv, wr, v_consume)
            gate3, gater = al3("gate", F32)
            Wdg, wdgr = W.next(("odd_in", 5))
            osb3, osbr = al3("osb", F32)

            def f64(name, n=256):
                v, r = A.alloc(name, n, F32)
                return v, r

            WDN = 3
            held = set()

            def galloc(n):
                while True:
                    free = [(bank_ctr[0] + k) % 8 for k in range(8) if ((bank_ctr[0] + k) % 8) not in held]
                    if len(free) >= n:
                        out = []
                        for i in free[:n]:
                            held.add(i)
                            out.append((banks[i], ("ps", i)))
                        bank_ctr[0] = free[n - 1] + 1
                        return out
                    yield

            def gfree(*ress):
                for r in ress:
                    held.discard(r[1])
            Xc, Xcr = f64("Xc")
            negX, negXr = f64("negX")
            dgB, dgBr = f64("dgB")
            Gm, Gmr = f64("Gm")
            sets = []
            for i in range(WDN):
                d = {}
                d["DT"] = f64("DT%d" % i)
                d["Bm"] = f64("Bm%d" % i)
                d["Nn"] = [(nmr[:, i, k, :], ("nmr", i, k)) for k in range(2)]
                d["Mm"] = [(nmr[:, i, 2 + k, :], ("nmr", i, 2 + k)) for k in range(2)]
                d["Rr"] = (nmr[:, i, 4, :], ("nmr", i, 4))
                d["QKm"] = A.alloc("QKm%d" % i, 256, BF16)
                d["Yb"] = A.alloc("Yb%d" % i, 256, BF16)
                d["kgt"] = A.alloc("kgt%d" % i, 512, BF16)
                d["kdc"] = A.alloc("kdc%d" % i, 512, BF16)
                d["vtm"] = A.alloc("vtmd%d" % i, 512, BF16)
                d["usb"] = f64("usb%d" % i, 512)
                d["WkT"] = A.alloc("WkT%d" % i, 256, BF16)
                d["wv"] = A.alloc("wv%d" % i, 512, BF16)
                sets.append(d)
            sbf, sbfr = A.alloc("sbfd", 512, BF16)
            ones64 = C64("ones")[:, 0:64]
            id64 = ident[0:64, 0:64]
            dn_done = {}

            def h4(v):
                return v[0:64, 0:256].rearrange("p (h t) -> p h t", h=4)

            def r4(v):
                return v[0:64, 0:256].bitcast(F32R).rearrange("p (h t) -> p h t", h=4)

            def rr(v):
                return v[0:64, 0:256].bitcast(F32R)

            def dn_gen(c):
                d = sets[c % WDN]
                DT, DTr = d["DT"]
                Bm, Bmr = d["Bm"]
                Nn, Mm = d["Nn"], d["Mm"]
                Rr, Rrr = d["Rr"]
                QKm, QKmr = d["QKm"]
                Yb, Ybr = d["Yb"]
                kgt, kgtr = d["kgt"]
                kdc, kdcr = d["kdc"]
                vtm, vtmr = d["vtm"]
                usb, usbr = d["usb"]
                WkT, WkTr = d["WkT"]
                wv_, wvr = d["wv"]
                seq, sgi, sstart, send = seg_of_chunk(tile, c)
                cs = slice(c * 64, (c + 1) * 64)
                X3 = make_X(c, Xc, Xcr)
                act(lambda e: e.mul(out=negX[0:64, :], in_=Xc[0:64, :], mul=-1.0), [Xcr], [negXr])
                v_tt(h4(dgB), C64("i4").rearrange("p (h t) -> p h t", h=4),
                     beta3[0:64, c, :].unsqueeze(2).to_broadcast([64, 4, 64]), ALU.mult, ["c64", betar], [dgBr])
                (bG, bGr), (bK, bKr), (bT, bTr), (bV, bVr) = yield from galloc(4)
                for h in range(4):
                    mm(bG[0:64, h * 128:h * 128 + 64], ones64, X3[:, h, :], True, False, ["c64", Xcr], [bGr])
                    mm(bG[0:64, h * 128:h * 128 + 64], h4(negX)[:, h, :], ones64, False, True, ["c64", negXr], [bGr])
                    mm(bG[0:64, h * 128 + 64:h * 128 + 128], ones64, h4(dgB)[:, h, :], True, True, ["c64", dgBr], [bGr])
                bG3 = bG[0:64, :].rearrange("p (h t) -> p h t", h=4)
                neg3 = C64("neg4").rearrange("p (h t) -> p h t", h=4)
                str3 = C64("strict4").rearrange("p (h t) -> p h t", h=4)
                for h in range(4):
                    mm(bK[0:64, h * 128:h * 128 + 64], kF3[:, h, cs], kF3[:, h, cs], True, True, [kFr], [bKr])
                    mm(bK[0:64, h * 128 + 64:h * 128 + 128], kF3[:, h, cs], qF3[:, h, cs], True, True, [kFr, qFr], [bKr])
                bK3 = bK[0:64, :].rearrange("p (h t) -> p h t", h=4)
                bTb = bT[0:64, 0:256].bitcast(BF16)
                for h in range(4):
                    tr(bTb[:, h * 128:(h + 1) * 128], kF3[:, h, cs], identb[:], [kFr, "identb"], [bTr])
                bT3 = bTb.rearrange("p (h d) -> p h d", h=4)
                bVb = bV[0:64, 0:256].bitcast(BF16)
                for h in range(4):
                    tr(bVb[:, h * 128:(h + 1) * 128], vF3[:, h, cs], identb[:], [vFr, "identb"], [bVr])
                yield
                v_tt(h4(Gm), bG3[:, :, 0:64], neg3, ALU.add, [bGr, "c64"], [Gmr])
                a_act(DT[0:64, :], Gm[0:64, :], AF.Exp, [Gmr], [DTr])
                v_tt(h4(Bm), bG3[:, :, 64:128], str3, ALU.mult, [bGr, "c64"], [Bmr])
                v_tt(kgt[0:64, :].rearrange("p (h d) -> p h d", h=4), bT3,
                     egdk3[0:64, c, 0:4].unsqueeze(2).to_broadcast([64, 4, 128]), ALU.mult, [bTr, egdkr], [kgtr])
                v_tt(kdc[0:64, :].rearrange("p (h d) -> p h d", h=4), bT3,
                     egdk3[0:64, c, 4:8].unsqueeze(2).to_broadcast([64, 4, 128]), ALU.mult, [bTr, egdkr], [kdcr])
                act(lambda e, bVb=bVb: e.copy(out=vtm[0:64, :], in_=bVb), [bVr], [vtmr])
                gfree(bGr, bTr, bVr)
                yield
                v_tt(Bm[0:64, :], Bm[0:64, :], DT[0:64, :], ALU.mult, [Bmr, DTr], [Bmr])
                N0, N0r = Nn[0]
                v_tt(r4(N0), bK3[:, :, 0:64], h4(Bm), ALU.mult, [bKr, Bmr], [N0r])
                v_tt(h4(QKm), bK3[:, :, 64:128], h4(DT), ALU.mult, [bKr, DTr], [QKmr])
                gfree(bKr)
                yield
                M0, M0r = Mm[0]
                ((bt, btr),) = yield from galloc(1)
                for h in range(4):
                    tr(bt[0:64, h * 64:(h + 1) * 64], h4(N0)[:, h, :], id64, [N0r, "c128"], [btr])
                act(lambda e, bt=bt, M0=M0: e.copy(out=rr(M0), in_=bt[0:64, 0:256]), [btr], [M0r])
                gfree(btr)
                v_tt(rr(Rr), C64("i4"), N0[0:64, :], ALU.subtract, [N0r, "c64"], [Rrr])
                yield
                cur = 0
                for stg in range(5):
                    Nc_, Ncr = Nn[cur]
                    Mc_, Mcr = Mm[cur]
                    Nx_, Nxr = Nn[1 - cur]
                    Mx_, Mxr = Mm[1 - cur]
                    last = (stg == 4)
                    if not last:
                        (bN, bNr), (bM, bMr) = yield from galloc(2)
                        for h in range(4):
                            mm(bN[0:64, h * 64:(h + 1) * 64], r4(Mc_)[:, h, :], r4(Nc_)[:, h, :], True, True, [Mcr, Ncr], [bNr])
                    else:
                        ((bM, bMr),) = yield from galloc(1)
                    for h in range(4):
                        mm(bM[0:64, h * 64:(h + 1) * 64], r4(Nc_)[:, h, :], r4(Mc_)[:, h, :], True, True, [Mcr, Ncr], [bMr])
                    yield
                    if not last:
                        v_tt(r4(Nx_), bN[0:64, 0:256].rearrange("p (h t) -> p h t", h=4),
                             C64("ones")[:, 0:64].unsqueeze(1).to_broadcast([64, 4, 64]), ALU.mult, [bNr, "c64"], [Nxr])
                    act(lambda e, bM=bM, Mx_=Mx_: e.copy(out=rr(Mx_), in_=bM[0:64, 0:256]), [bMr], [Mxr])
                    if not last:
                        gfree(bNr)
                    gfree(bMr)
                    ((bR, bRr),) = yield from galloc(1)
                    for h in range(4):
                        mm(bR[0:64, h * 64:(h + 1) * 64], r4(Mx_)[:, h, :], r4(Rr)[:, h, :], True, True, [Mxr, Rrr], [bRr])
                    yield
                    v_tt(rr(Rr), Rr[0:64, :], bR[0:64, 0:256], ALU.add, [Rrr, bRr], [Rrr])
                    gfree(bRr)
                    cur = 1 - cur
                for h in range(4):
                    act(lambda e, h=h, c=c: e.activation(out=h4(Yb)[:, h, :], in_=h4(Rr)[:, h, :], func=AF.Copy, scale=beta3[0:64, c, h:h + 1]),
                        [Rrr, betar], [Ybr])
                yield
                (bU, bUr), (bW, bWr) = yield from galloc(2)
                for h in range(4):
                    mm(bU[0:64, h * 128:(h + 1) * 128], h4(Yb)[:, h, :], vtm[0:64, h * 128:(h + 1) * 128], True, True, [Ybr, vtmr], [bUr])
                act(lambda e, bU=bU: e.copy(out=usb[0:64, :], in_=bU[0:64, :]), [bUr], [usbr])
                for h in range(4):
                    mm(bW[:, h * 64:(h + 1) * 64], kgt[0:64, h * 128:(h + 1) * 128], h4(Yb)[:, h, :], True, True, [kgtr, Ybr], [bWr])
                act(lambda e, bW=bW: e.copy(out=WkT, in_=bW[:, 0:256]), [bWr], [WkTr])
                gfree(bUr, bWr)
                yield
                while c > 0 and not dn_done.get(c - 1):
                    yield
                if sstart:
                    state_io(tile, seq, sstart, send, s_dn, "s_dn", st_dn, o_dn, "start")
                    act(lambda e: e.copy(out=sbf, in_=s_dn[:]), ["s_dn"], [sbfr])
                ((bWS, bWSr),) = yield from galloc(1)
                for h in range(4):
                    mm(bWS[0:64, h * 128:(h + 1) * 128], WkT[:, h * 64:(h + 1) * 64], sbf[:, h * 128:(h + 1) * 128], True, True,
                       [WkTr, sbfr], [bWSr])
                v_tt(wv_[0:64, :], usb[0:64, :], bWS[0:64, :], ALU.subtract, [usbr, bWSr], [wvr])
                gfree(bWSr)
                ((bO, bOr),) = yield from galloc(1)
                for h in range(4):
                    mm(bO[:, h * 64:(h + 1) * 64], wv_[0:64, h * 128:(h + 1) * 128], h4(QKm)[:, h, :], True, False, [wvr, QKmr], [bOr])
                    mm(bO[:, h * 64:(h + 1) * 64], sbf[:, h * 128:(h + 1) * 128], qg3[:, h, cs], False, True, [sbfr, qgr], [bOr])
                act(lambda e, bO=bO, cs=cs: e.copy(out=osb3[:, :, cs], in_=bO[:, 0:256].rearrange("p (h t) -> p h t", h=4)),
                    [bOr], [(osbr, c)])
                gfree(bOr)
                ((bS, bSr),) = yield from galloc(1)
                for h in range(4):
                    mm(bS[:, h * 128:(h + 1) * 128], kdc[0:64, h * 128:(h + 1) * 128], wv_[0:64, h * 128:(h + 1) * 128], True, True,
                       [kdcr, wvr], [bSr])
                for h in range(4):
                    hs_ = slice(h * 128, (h + 1) * 128)
                    v_stt(s_dn[:, hs_], s_dn[:, hs_], egrow3[:, h, c * 64 + 63:c * 64 + 64], bS[:, hs_], ALU.mult, ALU.add,
                          ["s_dn", bSr, egr], ["s_dn"])
                gfree(bSr)
                if send:
                    state_io(tile, seq, sstart, send, s_dn, "s_dn", st_dn, o_dn, "end")
                else:
                    act(lambda e: e.copy(out=sbf, in_=s_dn[:]), ["s_dn"], [sbfr])
                dn_done[c] = True

            def dn_gate():
                def alloc():
                    ((bk, br),) = yield from galloc(1)
                    return bk, br
                g = gate_gen(NT, Wdg, wdgr, gate3, gater, alloc)
                for r in g:
                    if r is not None:
                        gfree(r)
                    yield

            gens = [(lambda c=c: dn_gen(c)) for c in range(NC)]
            interleave(gens, WDN, extra=dn_gate)
            osbr = [(osbr, c) for c in range(NC)]
            cv0, cv0r = cvs["cv"][0]
            head_norm(NT, osb3, osbr, gate3, gater, 2, mix3, mixr, 4,
                      scratch=(A.raw["xh"][:, 0:4 * NT], [(cvs["xhr"], j) for j in range(4)], cv0, cv0r))
            out_proj(tile, "odd_out", mix3, mixr, next_norm=5)

        def final_out(tile):
            NT = tile["NT"]
            A.reset()
            rmsnorm_f32_out(tile)

        def rmsnorm_f32_out(tile):
            NT = tile["NT"]
            sq, sqr = A.alloc("nsq", KT * NT, BF16)
            sq3 = sq.rearrange("p (k t) -> p k t", k=KT)
            yv, yr = A.alloc("yfm", KT * NT, F32)
            y3 = yv.rearrange("p (k t) -> p k t", k=KT)
            act(lambda e: e.activation(out=sq3, in_=x_sb[:, :, 0:NT], func=AF.Square), ["x"], [sqr])
            bk, br = P()
            for kt in range(KT):
                mm(bk[:, 0:NT], onesb[:], sq3[:, kt, :], kt == 0, kt == KT - 1, [sqr, "onesb"], [br])
            a_act(rt[:, 0, 0:NT], bk[:, 0:NT], AF.Ln, [br], ["rt0"], bias=EPS, scale=1.0 / D)
            a_act(rt[:, 1, 0:NT], rt[:, 0, 0:NT], AF.Exp, ["rt0"], ["rt1"], scale=-0.5)
            for kt in range(KT):
                v_stt(y3[:, kt, :], x_sb[:, kt, 0:NT], normw[:, 6, kt:kt + 1], rt[:, 1, 0:NT], ALU.mult, ALU.mult,
                      ["x", "normw", "rt1"], [(yr, kt)])
            for b in range(NT // 128):
                sl = b % 2
                for half in range(2):
                    bk, br = P()
                    for q in range(4):
                        kt = half * 4 + q
                        tr(bk[:, q * 128:(q + 1) * 128], y3[:, kt, b * 128:(b + 1) * 128], ident, [(yr, kt), "c128"], [br])
                    act(lambda e, bk=bk, half=half, sl=sl: e.copy(out=xin[:, sl, half * 512:(half + 1) * 512], in_=bk[:, :]),
                        [br], [("xin", sl)])
                if b * 128 < tile["ptok"]:
                    dstb = yp[tile["tok0"] + b * 128:tile["tok0"] + (b + 1) * 128, :]
                else:
                    dstb = ys[b * 128 - tile["ptok"]:(b + 1) * 128 - tile["ptok"], :]
                S.dma("pool", lambda e, sl=sl, dstb=dstb: e.dma_start(out=dstb, in_=xin[:, sl, :]),
                      "xin%d" % sl, reads=[("xin", sl)], writes=[])

        for tile in tiles:
            t0_ = (tile is tiles[0])
            load_x(tile)
            for mo_ in range(KT):
                prenorm(tile["NT"], 0, mo_)
            if t0_:
                cast_group(2)
            if stage >= 1:
                ffn(tile, 0, 0)
            if t0_:
                cast_group(3)
            if stage >= 2:
                even_mixer(tile)
            if t0_:
                cast_group(4)
            if stage >= 3:
                ffn(tile, 1, 0, next_norm=1)
            if t0_:
                cast_group(5)
            if stage >= 3:
                ffn(tile, 0, 1)
            if stage >= 4:
                odd_mixer(tile)
            if stage >= 5:
                ffn(tile, 1, 1)
            final_out(tile)
        assert W.consumed == len(W.units)
        S.wait_deps("pool", [v for k, v in S.dma_last.items() if not (k.startswith("w") or k.startswith("cast"))])

        S.emit({"pe": block.tensor, "act": block.scalar, "dve": block.vector, "pool": block.gpsimd, "sp": block.sync},
               eng_sems, dma_sems)
    return nc


_PROG_CACHE = {}


def kernel(x_prompt, x_sample, state_ret, state_hgrn, state_lru_h, state_lru_conv, state_dn, state_dn_conv,
           ffn1_norm, ffn1_w_in, ffn1_w_out, mix_norm, ffn2_norm, ffn2_w_in, ffn2_w_out, final_norm,
           even_w_in, even_w_out, ret_out_norm, hg_out_norm, hg_lb_logits, odd_w_in, odd_w_out,
           lru_conv_w, lru_conv_b, lru_w_a, lru_b_a, lru_w_x, lru_b_x, lru_lambda, dn_conv_w, dn_a_log,
           dn_dt_bias, dn_out_norm, _past_len=2048, _stage=99, _ncores=NCORES):
    f = lambda a: np.ascontiguousarray(np.asarray(a, dtype=np.float32))
    x_prompt = f(x_prompt)
    x_sample = f(x_sample)
    B, TP, _ = x_prompt.shape
    assert B == 4 and x_sample.shape[0] == 16 and x_sample.shape[1] == 64
    hc = host_consts(TP, _past_len)
    if (TP, _stage) not in _PROG_CACHE:
        _PROG_CACHE[(TP, _stage)] = build_program(TP, _stage)
    nc = _PROG_CACHE[(TP, _stage)]
    shared = {
        "ffn1_w_in": f(ffn1_w_in), "ffn2_w_in": f(ffn2_w_in), "ffn1_w_out": f(ffn1_w_out), "ffn2_w_out": f(ffn2_w_out),
        "even_w_in": f(even_w_in)[0], "even_w_out": f(even_w_out)[0], "odd_w_in": f(odd_w_in)[0], "odd_w_out": f(odd_w_out)[0],
        "norms": np.ascontiguousarray(np.concatenate([f(ffn1_norm), f(mix_norm), f(ffn2_norm), f(final_norm)[None]], 0)),
        "hnorm": np.ascontiguousarray(np.concatenate([f(ret_out_norm), f(hg_out_norm), f(dn_out_norm)], 0)),
        "hg_lb_logits": f(hg_lb_logits),
        "lru_vecs": np.ascontiguousarray(np.concatenate([f(lru_conv_w)[0], f(lru_conv_b), f(lru_b_a), f(lru_b_x), f(lru_lambda)], 0)),
        "lru_w_a": f(lru_w_a)[0], "lru_w_x": f(lru_w_x)[0],
        "dn_conv_w": f(dn_conv_w)[0],
        "dn_scal": np.ascontiguousarray(np.concatenate([f(dn_a_log), f(dn_dt_bias)], 0)),
        "c128": hc["c128"], "c64": hc["c64"], "rope": hc["rope"],
    }
    sr, sh, sd = f(state_ret)[0], f(state_hgrn)[0], f(state_dn)[0]
    slh, slc, sdc = f(state_lru_h)[0], f(state_lru_conv)[0], f(state_dn_conv)[0]
    in_maps = []
    zero_prompt = np.zeros_like(x_prompt[0])
    for c in range(NCORES):
        m = dict(shared)
        m["xp"] = x_prompt[PROMPT_OF_CORE[c]] if PROMPT_OF_CORE[c] is not None else zero_prompt
        m["xs"] = np.ascontiguousarray(x_sample[2 * c:2 * c + 2].reshape(128, D))
        m["st_ret"] = np.ascontiguousarray(sr[2 * c:2 * c + 2])
        m["st_hg"] = np.ascontiguousarray(sh[2 * c:2 * c + 2])
        m["st_dn"] = np.ascontiguousarray(sd[2 * c:2 * c + 2])
        m["st_lh"] = np.ascontiguousarray(slh[2 * c:2 * c + 2])
        m["st_lc"] = np.ascontiguousarray(slc[2 * c:2 * c + 2])
        m["st_dc"] = np.ascontiguousarray(sdc[2 * c:2 * c + 2])
        in_maps.append(m)
    res = run_bass_kernel_spmd(nc, in_maps[:_ncores], core_ids=list(range(_ncores)))
    R = list(res.results)
    while len(R) < NCORES:
        R.append(R[0])
    y_prompt = np.stack([R[c]["yp"] for c in CORE_OF_PROMPT], 0)
    y_sample = np.concatenate([R[c]["ys"].reshape(2, 64, D) for c in range(NCORES)], 0)

    def gp(name, shape):
        return np.stack([R[c][name].reshape(shape) for c in CORE_OF_PROMPT], 0)[None]

    def gs(name, shape):
        return np.concatenate([R[c][name].reshape((2,) + shape) for c in range(NCORES)], 0)[None]

    return (y_prompt, y_sample,
            gp("ret_p", (4, 128, 128)), gs("ret_s", (4, 128, 128)),
            gp("hg_p", (4, 128, 128)), gs("hg_s", (4, 128, 128)),
            gp("lh_p", (512,)), gs("lh_s", (512,)),
            gp("lc_p", (3, 512)), gs("lc_s", (3, 512)),
            gp("dn_p", (4, 128, 128)), gs("dn_s", (4, 128, 128)),
            gp("dc_p", (3, 1536)), gs("dc_s", (3, 1536)))
```
